# Optimizing a Trainium2 kernel written in Bass

```python
import math
import jax, jax.numpy as jnp
from jax import lax
import numpy as np

D_MODEL = 1024
BATCH = 4
SEQ = 4096
DEPTH = 2
DEC_BATCH = 8
DEC_SEQ = 32
PAST_LEN = 4096

CHUNK = 64
N_MIXERS = 2
N_SSM_LAYERS = (DEPTH + 1) // 2
N_ATTN_LAYERS = DEPTH // 2
SSM_GROUP = 16
SSM_GROUPS = D_MODEL // SSM_GROUP
SSM_STATE = 64
DT_MIN = 1e-3
DT_MAX = 1e-1
N_HEADS = 16
N_KV_HEADS = 4
HEAD_DIM = 64
ATTN_WIDTH = N_HEADS * HEAD_DIM
KV_WIDTH = N_KV_HEADS * HEAD_DIM
IDX_HEADS = 8
IDX_DIM = 64
TOPK_MAX = 256
Q_BLOCK = 128
ROT_DIM = HEAD_DIM // 4
ROPE_THETA = 500000.0
OFF_K = ATTN_WIDTH
OFF_V = OFF_K + KV_WIDTH
OFF_QI = OFF_V + KV_WIDTH
OFF_KI = OFF_QI + IDX_HEADS * IDX_DIM
OFF_WI = OFF_KI + IDX_DIM
IN_COLS = OFF_WI + IDX_HEADS
D_FF = 2816
CONV_W = 3
EPS = 1e-6
NEG = -1e30

kernel_name = "hybrid_s5_dsa_convffn_stream_step"


def rms_norm(x, g):
    xf = x.astype(jnp.float32)
    y = xf * lax.rsqrt(jnp.mean(xf * xf, axis=-1, keepdims=True) + EPS)
    return (y * g.astype(jnp.float32)).astype(x.dtype)


def rotary_partial(x, pos):
    half = ROT_DIM // 2
    inv = ROPE_THETA ** (-jnp.arange(half, dtype=jnp.float32) / half)
    ang = pos.astype(jnp.float32)[:, None] * inv[None, :]
    cos = jnp.cos(ang)[None, :, None, :]
    sin = jnp.sin(ang)[None, :, None, :]
    xr = x[..., :ROT_DIM].astype(jnp.float32)
    x1, x2 = xr[..., :half], xr[..., half:]
    rot = jnp.concatenate([x1 * cos - x2 * sin, x2 * cos + x1 * sin], axis=-1).astype(x.dtype)
    return jnp.concatenate([rot, x[..., ROT_DIM:]], axis=-1)


def s5_discretize(lam_re, lam_im, log_dt, b_re, b_im):
    lre = jnp.minimum(lam_re.astype(jnp.float32), -1e-4)
    lim = lam_im.astype(jnp.float32)
    dt = jnp.exp(log_dt.astype(jnp.float32))[:, None]
    mag = jnp.exp(lre * dt)
    a_re = mag * jnp.cos(lim * dt)
    a_im = mag * jnp.sin(lim * dt)
    den = lre * lre + lim * lim
    n_re = a_re - 1.0
    f_re = (n_re * lre + a_im * lim) / den
    f_im = (a_im * lre - n_re * lim) / den
    br = b_re.astype(jnp.float32)
    bi = b_im.astype(jnp.float32)
    bb_re = f_re[..., None] * br - f_im[..., None] * bi
    bb_im = f_re[..., None] * bi + f_im[..., None] * br
    return a_re, a_im, bb_re, bb_im


def _cplx_combine(e1, e2):
    a1r, a1i, b1r, b1i = e1
    a2r, a2i, b2r, b2i = e2
    return (a2r * a1r - a2i * a1i,
            a2r * a1i + a2i * a1r,
            a2r * b1r - a2i * b1i + b2r,
            a2r * b1i + a2i * b1r + b2i)


def s5_mixer(u, h0_re, h0_im, lam_re, lam_im, log_dt, b_re, b_im, c_re, c_im, d_skip, w_glu, b_glu):
    b, t, _ = u.shape
    uf = u.astype(jnp.float32).reshape(b, t, SSM_GROUPS, SSM_GROUP)
    a_re, a_im, bb_re, bb_im = s5_discretize(lam_re, lam_im, log_dt, b_re, b_im)
    bu_re = jnp.einsum('btgc,gpc->btgp', uf, bb_re)
    bu_im = jnp.einsum('btgc,gpc->btgp', uf, bb_im)
    h0r = h0_re.astype(jnp.float32)
    h0i = h0_im.astype(jnp.float32)
    bu_re = bu_re.at[:, 0].add(a_re * h0r - a_im * h0i)
    bu_im = bu_im.at[:, 0].add(a_re * h0i + a_im * h0r)
    ar = jnp.broadcast_to(a_re, bu_re.shape)
    ai = jnp.broadcast_to(a_im, bu_im.shape)
    _, _, h_re, h_im = lax.associative_scan(_cplx_combine, (ar, ai, bu_re, bu_im), axis=1)
    y = (jnp.einsum('btgp,gcp->btgc', h_re, c_re.astype(jnp.float32))
         - jnp.einsum('btgp,gcp->btgc', h_im, c_im.astype(jnp.float32)))
    y = y + d_skip.astype(jnp.float32).reshape(SSM_GROUPS, SSM_GROUP) * uf
    z = jax.nn.gelu(y.reshape(b, t, D_MODEL)).astype(u.dtype)
    g = z @ w_glu + b_glu
    out = g[..., :D_MODEL] * jax.nn.sigmoid(g[..., D_MODEL:])
    return out.astype(u.dtype), h_re[:, -1], h_im[:, -1]


def dsa_project(xn, pos, w_in, q_norm, k_norm):
    b, t, _ = xn.shape
    proj = xn @ w_in
    q = proj[..., :OFF_K].reshape(b, t, N_HEADS, HEAD_DIM)
    k = proj[..., OFF_K:OFF_V].reshape(b, t, N_KV_HEADS, HEAD_DIM)
    v = proj[..., OFF_V:OFF_QI].reshape(b, t, N_KV_HEADS, HEAD_DIM)
    qi = proj[..., OFF_QI:OFF_KI].reshape(b, t, IDX_HEADS, IDX_DIM)
    ki = proj[..., OFF_KI:OFF_WI]
    wi = proj[..., OFF_WI:]
    q = rotary_partial(rms_norm(q, q_norm), pos)
    k = rotary_partial(rms_norm(k, k_norm), pos)
    qi = rotary_partial(qi, pos)
    ki = rotary_partial(ki[:, :, None, :], pos)[:, :, 0]
    return q, k, v, qi, ki, wi


def dsa_attend_block(q, qi, wi, qpos, k, v, ki, kpos, topk):
    b, tb = q.shape[0], q.shape[1]
    logits = jnp.einsum('bthd,bsd->bths', qi.astype(jnp.float32), ki.astype(jnp.float32)) * (IDX_DIM ** -0.5)
    score = jnp.einsum('bths,bth->bts', jax.nn.relu(logits), wi.astype(jnp.float32) * (IDX_HEADS ** -0.5))
    allowed = (kpos[None, :] // CHUNK) <= (qpos[:, None] // CHUNK)
    score = jnp.where(allowed[None], score, NEG)
    vals, idx = lax.top_k(score, topk)
    valid = vals > (0.5 * NEG)
    gather = jax.vmap(lambda rows, ii: rows[ii])
    ks = gather(k, idx).astype(jnp.float32)
    vs = gather(v, idx).astype(jnp.float32)
    qg = q.astype(jnp.float32).reshape(b, tb, N_KV_HEADS, N_HEADS // N_KV_HEADS, HEAD_DIM)
    s = jnp.einsum('btkgd,btjkd->btkgj', qg, ks) * (HEAD_DIM ** -0.5)
    s = jnp.where(valid[:, :, None, None, :], s, NEG)
    p = jax.nn.softmax(s, axis=-1)
    o = jnp.einsum('btkgj,btjkd->btkgd', p, vs)
    return o.reshape(b, tb, ATTN_WIDTH).astype(q.dtype)


def dsa_prompt_attend(q, qi, wi, pos, k, v, ki, topk):
    b, t = q.shape[0], q.shape[1]
    nb = t // Q_BLOCK
    to_blocks = lambda a: a.reshape((b, nb, Q_BLOCK) + a.shape[2:]).swapaxes(0, 1)

    def one_block(args):
        qb, qib, wib, pb = args
        return dsa_attend_block(qb, qib, wib, pb, k, v, ki, pos, topk)

    out = lax.map(one_block, (to_blocks(q), to_blocks(qi), to_blocks(wi), pos.reshape(nb, Q_BLOCK)))
    return out.swapaxes(0, 1).reshape(b, t, ATTN_WIDTH)


def conv_ffn(xn, conv_state, w_up, conv_w, conv_b, w_down):
    t = xn.shape[1]
    h = xn @ w_up
    a, bv = h[..., :D_FF], h[..., D_FF:]
    padded = jnp.concatenate([conv_state.astype(a.dtype), a], axis=1)
    c = conv_b + sum(conv_w[j] * padded[:, j:j + t] for j in range(CONV_W))
    out = (jax.nn.gelu(c) * bv) @ w_down
    return out, padded[:, -(CONV_W - 1):]


def setup_inputs(seed: int = 0) -> dict:
    key = jax.random.key(seed)
    ks = jax.random.split(key, 32)
    nrm = lambda k, shape, s: jax.random.normal(k, shape, jnp.float32) * s
    lam_im0 = math.pi * jnp.arange(SSM_STATE, dtype=jnp.float32)
    return {
        "x_prompt": nrm(ks[0], (BATCH, SEQ, D_MODEL), 1.0),
        "x_sample": nrm(ks[1], (DEC_BATCH, DEC_SEQ, D_MODEL), 1.0),
        "state_ssm_re": nrm(ks[2], (N_SSM_LAYERS, DEC_BATCH, SSM_GROUPS, SSM_STATE), 0.1),
        "state_ssm_im": nrm(ks[3], (N_SSM_LAYERS, DEC_BATCH, SSM_GROUPS, SSM_STATE), 0.1),
        "cache_k": nrm(ks[4], (N_ATTN_LAYERS, DEC_BATCH, PAST_LEN, N_KV_HEADS, HEAD_DIM), 1.0),
        "cache_v": nrm(ks[5], (N_ATTN_LAYERS, DEC_BATCH, PAST_LEN, N_KV_HEADS, HEAD_DIM), 1.0),
        "cache_kidx": nrm(ks[6], (N_ATTN_LAYERS, DEC_BATCH, PAST_LEN, IDX_DIM), 1.0),
        "cache_conv": nrm(ks[7], (DEPTH, DEC_BATCH, CONV_W - 1, D_FF), 1.0),
        "norm_mix": 1.0 + nrm(ks[8], (DEPTH, D_MODEL), 0.02),
        "norm_ffn": 1.0 + nrm(ks[9], (DEPTH, D_MODEL), 0.02),
        "ssm_lambda_re": -0.5 + nrm(ks[10], (N_SSM_LAYERS, SSM_GROUPS, SSM_STATE), 0.01),
        "ssm_lambda_im": lam_im0 + nrm(ks[11], (N_SSM_LAYERS, SSM_GROUPS, SSM_STATE), 0.01),
        "ssm_log_dt": jax.random.uniform(ks[12], (N_SSM_LAYERS, SSM_GROUPS), jnp.float32,
                                         math.log(DT_MIN), math.log(DT_MAX)),
        "ssm_b_re": nrm(ks[13], (N_SSM_LAYERS, SSM_GROUPS, SSM_STATE, SSM_GROUP), (2.0 * SSM_GROUP) ** -0.5),
        "ssm_b_im": nrm(ks[14], (N_SSM_LAYERS, SSM_GROUPS, SSM_STATE, SSM_GROUP), (2.0 * SSM_GROUP) ** -0.5),
        "ssm_c_re": nrm(ks[15], (N_SSM_LAYERS, SSM_GROUPS, SSM_GROUP, SSM_STATE), SSM_STATE ** -0.5),
        "ssm_c_im": nrm(ks[16], (N_SSM_LAYERS, SSM_GROUPS, SSM_GROUP, SSM_STATE), SSM_STATE ** -0.5),
        "ssm_d": nrm(ks[17], (N_SSM_LAYERS, D_MODEL), 1.0),
        "ssm_w_glu": nrm(ks[18], (N_SSM_LAYERS, D_MODEL, 2 * D_MODEL), D_MODEL ** -0.5),
        "ssm_b_glu": nrm(ks[19], (N_SSM_LAYERS, 2 * D_MODEL), 0.01),
        "attn_w_in": nrm(ks[20], (N_ATTN_LAYERS, D_MODEL, IN_COLS), D_MODEL ** -0.5),
        "attn_q_norm": 1.0 + nrm(ks[21], (N_ATTN_LAYERS, HEAD_DIM), 0.02),
        "attn_k_norm": 1.0 + nrm(ks[22], (N_ATTN_LAYERS, HEAD_DIM), 0.02),
        "attn_w_o": nrm(ks[23], (N_ATTN_LAYERS, ATTN_WIDTH, D_MODEL), ATTN_WIDTH ** -0.5),
        "ffn_w_up": nrm(ks[24], (DEPTH, D_MODEL, 2 * D_FF), D_MODEL ** -0.5),
        "ffn_conv_w": nrm(ks[25], (DEPTH, CONV_W, D_FF), CONV_W ** -0.5),
        "ffn_conv_b": nrm(ks[26], (DEPTH, D_FF), 0.01),
        "ffn_w_down": nrm(ks[27], (DEPTH, D_FF, D_MODEL), D_FF ** -0.5),
    }


def reference(x_prompt, x_sample, state_ssm_re, state_ssm_im, cache_k, cache_v, cache_kidx, cache_conv,
              norm_mix, norm_ffn, ssm_lambda_re, ssm_lambda_im, ssm_log_dt, ssm_b_re, ssm_b_im,
              ssm_c_re, ssm_c_im, ssm_d, ssm_w_glu, ssm_b_glu, attn_w_in, attn_q_norm, attn_k_norm,
              attn_w_o, ffn_w_up, ffn_conv_w, ffn_conv_b, ffn_w_down):
    b_p, t_p = x_prompt.shape[0], x_prompt.shape[1]
    b_s, t_s = x_sample.shape[0], x_sample.shape[1]
    past = cache_k.shape[2]
    pos_p = jnp.arange(t_p, dtype=jnp.int32)
    pos_s = past + jnp.arange(t_s, dtype=jnp.int32)
    kpos_s = jnp.arange(past + t_s, dtype=jnp.int32)
    topk_p = min(TOPK_MAX, t_p // 4)
    topk_s = min(TOPK_MAX, (past + t_s) // 4)

    xp, xs = x_prompt, x_sample
    ssm_re_p, ssm_im_p, ssm_re_s, ssm_im_s = [], [], [], []
    k_p, v_p, ki_p, k_s, v_s, ki_s = [], [], [], [], [], []
    conv_p, conv_s = [], []
    for i in range(DEPTH):
        j = i // N_MIXERS
        hp = rms_norm(xp, norm_mix[i])
        hs = rms_norm(xs, norm_mix[i])
        if i % N_MIXERS == 0:
            sp = (ssm_lambda_re[j], ssm_lambda_im[j], ssm_log_dt[j], ssm_b_re[j], ssm_b_im[j],
                  ssm_c_re[j], ssm_c_im[j], ssm_d[j], ssm_w_glu[j], ssm_b_glu[j])
            zero = jnp.zeros((b_p, SSM_GROUPS, SSM_STATE), jnp.float32)
            yp, hpr, hpi = s5_mixer(hp, zero, zero, *sp)
            ys, hsr, hsi = s5_mixer(hs, state_ssm_re[j], state_ssm_im[j], *sp)
            ssm_re_p.append(hpr.astype(x_prompt.dtype))
            ssm_im_p.append(hpi.astype(x_prompt.dtype))
            ssm_re_s.append(hsr.astype(state_ssm_re.dtype))
            ssm_im_s.append(hsi.astype(state_ssm_im.dtype))
        else:
            qp, kp, vp, qip, kip, wip = dsa_project(hp, pos_p, attn_w_in[j], attn_q_norm[j], attn_k_norm[j])
            yp = dsa_prompt_attend(qp, qip, wip, pos_p, kp, vp, kip, topk_p) @ attn_w_o[j]
            qs, ks_, vs_, qis, kis, wis = dsa_project(hs, pos_s, attn_w_in[j], attn_q_norm[j], attn_k_norm[j])
            k_all = jnp.concatenate([cache_k[j].astype(ks_.dtype), ks_], axis=1)
            v_all = jnp.concatenate([cache_v[j].astype(vs_.dtype), vs_], axis=1)
            ki_all = jnp.concatenate([cache_kidx[j].astype(kis.dtype), kis], axis=1)
            ys = dsa_attend_block(qs, qis, wis, pos_s, k_all, v_all, ki_all, kpos_s, topk_s) @ attn_w_o[j]
            k_p.append(kp); v_p.append(vp); ki_p.append(kip)
            k_s.append(ks_); v_s.append(vs_); ki_s.append(kis)
        xp = xp + yp.astype(xp.dtype)
        xs = xs + ys.astype(xs.dtype)
        zero_conv = jnp.zeros((b_p, CONV_W - 1, D_FF), xp.dtype)
        fp, cp = conv_ffn(rms_norm(xp, norm_ffn[i]), zero_conv, ffn_w_up[i], ffn_conv_w[i], ffn_conv_b[i], ffn_w_down[i])
        fs, cs = conv_ffn(rms_norm(xs, norm_ffn[i]), cache_conv[i], ffn_w_up[i], ffn_conv_w[i], ffn_conv_b[i], ffn_w_down[i])
        xp = xp + fp.astype(xp.dtype)
        xs = xs + fs.astype(xs.dtype)
        conv_p.append(cp)
        conv_s.append(cs)

    new_ssm_re_p = jnp.stack(ssm_re_p)
    new_ssm_im_p = jnp.stack(ssm_im_p)
    new_ssm_re_s = jnp.stack(ssm_re_s)
    new_ssm_im_s = jnp.stack(ssm_im_s)
    new_k_p = jnp.stack(k_p)
    new_v_p = jnp.stack(v_p)
    new_ki_p = jnp.stack(ki_p)
    new_k_s = jnp.stack(k_s)
    new_v_s = jnp.stack(v_s)
    new_ki_s = jnp.stack(ki_s)
    new_conv_p = jnp.stack(conv_p)
    new_conv_s = jnp.stack(conv_s)
    return (xp, xs, new_ssm_re_p, new_ssm_im_p, new_ssm_re_s, new_ssm_im_s,
            new_k_p, new_v_p, new_ki_p, new_k_s, new_v_s, new_ki_s, new_conv_p, new_conv_s)
```

```python
import contextlib
import math
import numpy as np
import concourse.bass as bass
import concourse.mybir as mybir
from concourse.bass_utils import run_bass_kernel_spmd

F32 = mybir.dt.float32
BF16 = mybir.dt.bfloat16
I32 = mybir.dt.int32
AF = mybir.ActivationFunctionType
ALU = mybir.AluOpType
AX = mybir.AxisListType

NDS = 24
DEBUG = False
D = 1024
DFF = 2816
NFC = 22
NBIS = 16
EPS = 1e-6
TWO_PI = 2.0 * math.pi
SIN_SCALE = TWO_PI * (1.0 - 1e-6)


class Buf:
    __slots__ = ("name", "w", "r")

    def __init__(self, name):
        self.name = name
        self.w = None
        self.r = []


class Op:
    __slots__ = ("eng", "fn", "deps", "sig", "cnt", "idx", "dma", "dsem", "dval")


class Prog:
    ENG = ["pe", "act", "dve", "pool", "sp"]

    def __init__(self, nc):
        self.nc = nc
        self.q = {e: [] for e in self.ENG}
        self.ndma_sp = 0
        self.ndma_pool = 0
        self.dma_last = [None] * NDS
        self.dma_cnt = [0] * NDS
        self.bufs = {}

    def _B(self, x):
        if isinstance(x, Buf):
            return x
        b = self.bufs.get(x)
        if b is None:
            b = Buf(x)
            self.bufs[x] = b
        return b

    def op(self, eng, fn, reads=(), writes=(), dma=False, extra=()):
        o = Op()
        o.eng = eng
        o.fn = fn
        o.sig = False
        o.cnt = None
        o.dma = dma
        o.idx = len(self.q[eng])
        deps = {}
        reads = [self._B(b) for b in reads]
        writes = [self._B(b) for b in writes]
        for b in reads:
            if b.w is not None:
                deps[id(b.w)] = (b.w, True)
        for b in writes:
            if b.w is not None and id(b.w) not in deps:
                deps[id(b.w)] = (b.w, False)
            for r in b.r:
                if id(r) not in deps:
                    deps[id(r)] = (r, False)
        for d in extra:
            if d is not None and id(d) not in deps:
                deps[id(d)] = (d, True)
        if dma:
            half = NDS // 2
            if eng == "sp":
                j = self.ndma_sp % half
                self.ndma_sp += 1
            else:
                j = half + (self.ndma_pool % half)
                self.ndma_pool += 1
            prev = self.dma_last[j]
            if prev is not None:
                deps[id(prev)] = (prev, False)
            self.dma_cnt[j] += 16
            o.dsem = j
            o.dval = self.dma_cnt[j]
            self.dma_last[j] = o
        o.deps = []
        for d, raw in deps.values():
            if d is o:
                continue
            if d.dma:
                o.deps.append(d)
            elif d.eng != eng:
                d.sig = True
                o.deps.append(d)
            else:
                if raw and (o.idx - d.idx) <= 2 and eng != "pe":
                    d.sig = True
                    o.deps.append(d)
        for b in reads:
            b.r.append(o)
        for b in writes:
            b.w = o
            b.r = []
        self.q[eng].append(o)
        return o

    def barrier(self, tiny):
        firsts = []
        alld = [d for d in self.dma_last if d is not None]
        for e in self.ENG:
            firsts.append(self.op(e, tiny[e], reads=(), writes=["_barA_" + e], dma=(e == "sp"), extra=alld))
        alld = [d for d in self.dma_last if d is not None]
        for e in self.ENG:
            self.op(e, tiny[e], reads=["_barA_" + x for x in self.ENG], writes=["_barB_" + e], dma=(e == "sp"),
                    extra=alld)
        for b in self.bufs.values():
            if not b.name.startswith("_bar"):
                b.w = None
                b.r = []

    def emit(self):
        nc = self.nc
        for e in self.ENG:
            c = 0
            for o in self.q[e]:
                if o.sig and not o.dma:
                    c += 1
                    o.cnt = c
        with contextlib.ExitStack() as st:
            esem = {e: st.enter_context(nc.semaphore("s_" + e)) for e in self.ENG}
            dsem = [st.enter_context(nc.semaphore("d_%d" % j)) for j in range(NDS)]
            block = st.enter_context(nc.Block())
            engobj = {"pe": block.tensor, "act": block.scalar, "dve": block.vector,
                      "pool": block.gpsimd, "sp": block.sync}

            def make(e):
                def body(eng):
                    seen = {}
                    for o in self.q[e]:
                        for d in o.deps:
                            if d.dma:
                                key = ("d", d.dsem)
                                val = d.dval
                                sem = dsem[d.dsem]
                            else:
                                key = ("e", d.eng)
                                val = d.cnt
                                sem = esem[d.eng]
                            if seen.get(key, 0) >= val:
                                continue
                            seen[key] = val
                            eng.wait_ge(sem, val)
                        ins = o.fn(eng)
                        if o.dma:
                            ins.then_inc(dsem[o.dsem], 16)
                        elif o.sig:
                            ins.then_inc(esem[e], 1)
                    for j in range(NDS):
                        last = self.dma_last[j]
                        if last is not None and last.eng == e:
                            if seen.get(("d", j), 0) < last.dval:
                                eng.wait_ge(dsem[j], last.dval)
                return body

            for e in self.ENG:
                engobj[e](make(e))


def bc_last(ap2d, p, a, b):
    return ap2d.rearrange("p (a o) -> p a o", o=1).to_broadcast([p, a, b])


def build():
    nc = bass.Bass("TRN2", target_bir_lowering=False)

    def din(name, shape, dtype=F32):
        return nc.dram_tensor(name, list(shape), dtype, kind="ExternalInput").ap()

    def dout(name, shape, dtype=F32):
        return nc.dram_tensor(name, list(shape), dtype, kind="ExternalOutput").ap()

    def dscr(name, shape, dtype=F32):
        return nc.dram_tensor(name, list(shape), dtype, kind="Internal").ap()

    xp = din("xp", [4096, D])
    xs = din("xs", [32, D])
    flag = din("flag", [128, 4])
    st_re = din("st_re", [64, 64])
    st_im = din("st_im", [64, 64])
    ck = din("ck", [4096, 256])
    cv = din("cv", [4096, 256])
    cki = din("cki", [4096, 64])
    cconv = din("cconv", [2, 2, DFF])
    norm_mix = din("norm_mix", [2, D])
    norm_ffn = din("norm_ffn", [2, D])
    lam_re = din("lam_re", [64, 64])
    lam_im = din("lam_im", [64, 64])
    log_dt = din("log_dt", [1, 64])
    b_re = din("b_re", [64, 64, 16])
    b_im = din("b_im", [64, 64, 16])
    c_re = din("c_re", [64, 16, 64])
    c_im = din("c_im", [64, 16, 64])
    ssm_d = din("ssm_d", [1, D])
    w_glu = din("w_glu", [D, 2 * D])
    b_glu = din("b_glu", [1, 2 * D])
    w_in = din("w_in", [D, 2120])
    q_norm = din("q_norm", [1, 64])
    k_norm = din("k_norm", [1, 64])
    w_o = din("w_o", [D, D])
    w_up = din("w_up", [2, D, 2 * DFF])
    conv_w = din("conv_w", [2, 3, DFF])
    conv_b = din("conv_b", [2, DFF])
    w_down = din("w_down", [2, DFF, D])

    y_own = dout("y_own", [2048, D])
    y_s = dout("y_s", [32, D])
    o_ssm_p = dout("o_ssm_p", [2, 64, 64])
    o_ssm_s = dout("o_ssm_s", [2, 64, 64])
    o_k = dout("o_k", [2048, 256])
    o_v = dout("o_v", [2048, 256])
    o_ki = dout("o_ki", [2048, 64])
    o_ks = dout("o_ks", [32, 256])
    o_vs = dout("o_vs", [32, 256])
    o_kis = dout("o_kis", [32, 64])
    o_conv_p = dout("o_conv_p", [2, 2, DFF])
    o_conv_s = dout("o_conv_s", [2, 2, DFF])

    Zp = dscr("Zp", [4096, D], BF16)
    Zs = dscr("Zs", [32, D], BF16)
    X2p = dscr("X2p", [4096, D])
    X2s = dscr("X2s", [32, D])
    X3p = dscr("X3p", [2176, D])
    X3s = dscr("X3s", [32, D])
    Wglu_b = dscr("Wglu_b", [D, 2 * D], BF16)
    Win_b = dscr("Win_b", [D, 2120], BF16)
    Wo_b = dscr("Wo_b", [D, D], BF16)
    Wd_b = [dscr("Wd_b%d" % l, [DFF, D], BF16) for l in range(2)]
    Wup_blk = [dscr("Wup_blk%d" % l, [NFC, 128, 8, 256], BF16) for l in range(2)]

    P = Prog(nc)
    DBG = {}

    def dbgd(name, src_ap, shape, dtype, reads):
        if not DEBUG:
            return
        o = nc.dram_tensor("dbg_" + name, list(shape), dtype, kind="ExternalOutput").ap()
        P.op("pool", lambda e: e.dma_start(out=o, in_=src_ap), reads, ["dbgout_" + name], dma=True)

    def dbg(name, tile_ap, shape, dtype, reads):
        if not DEBUG:
            return
        o = nc.dram_tensor("dbg_" + name, list(shape), dtype, kind="ExternalOutput").ap()
        P.op("pool", lambda e: e.dma_start(out=o, in_=tile_ap), reads, ["dbgout_" + name], dma=True)
    V = lambda fn, r=(), w=(): P.op("dve", fn, r, w)
    A = lambda fn, r=(), w=(): P.op("act", fn, r, w)
    G = lambda fn, r=(), w=(): P.op("pool", fn, r, w)
    T = lambda fn, r=(), w=(): P.op("pe", fn, r, w)
    LD = lambda fn, r=(), w=(): P.op("sp", fn, r, w, dma=True)
    ST = lambda fn, r=(), w=(): P.op("pool", fn, r, w, dma=True)

    with contextlib.ExitStack() as top:
        sbt = lambda st, name, shape, dt: st.enter_context(nc.sbuf_tensor(name, list(shape), dt))
        psF = [top.enter_context(nc.psum_tensor("psF%d" % i, [128, 512], F32)) for i in range(6)]
        psB = [top.enter_context(nc.psum_tensor("psB%d" % i, [128, 1024], BF16)) for i in range(2)]
        identf = sbt(top, "identf", [128, 128], F32)
        identb = sbt(top, "identb", [128, 128], BF16)
        iot = sbt(top, "iot", [128, 128], F32)
        flg = sbt(top, "flg", [128, 4], F32)
        tinyt = sbt(top, "tinyt", [128, 8], F32)
        tiny = {
            "pe": lambda e: e.matmul(psF[5][0:1, 0:1], lhsT=identb[0:1, 0:1], rhs=identb[0:1, 0:1], start=True, stop=True),
            "act": lambda e: e.copy(out=tinyt[0:1, 0:1], in_=tinyt[0:1, 4:5]),
            "dve": lambda e: e.memset(tinyt[0:1, 1:2], 0.0),
            "pool": lambda e: e.memset(tinyt[0:1, 2:3], 0.0),
            "sp": lambda e: e.dma_start(out=tinyt[0:1, 3:4], in_=tinyt[0:1, 5:6]),
        }
        G(lambda e: e.memset(tinyt[:], 0.0), (), ["tinyt"])
        G(lambda e: e.iota(iot[:], pattern=[[1, 128]], base=0, channel_multiplier=-1,
                           allow_small_or_imprecise_dtypes=True), (), ["iot"])
        V(lambda e: e.tensor_single_scalar(out=identf[:], in_=iot[:], scalar=0.0, op=ALU.is_equal), ["iot"], ["identf"])
        V(lambda e: e.tensor_copy(out=identb[:], in_=identf[:]), ["identf"], ["identb"])
        LD(lambda e: e.dma_start(out=flg[:], in_=flag), (), ["flg"])


        if True:
            phc = top
            CW = 704
            def conv_dma(src, dst, dname):
                ST(lambda e: e.dma_start(out=dst, in_=src), (), [dname])

            def conv_plain(src2d, dst2d, nrows, ncols, dname):
                for r in range(nrows // 128):
                    conv_dma(src2d[r * 128:(r + 1) * 128, :], dst2d[r * 128:(r + 1) * 128, :], dname)
                    yield

            def conv_all():
                yield from conv_plain(w_glu, Wglu_b, D, 2 * D, "Wglu_b")
                for l in range(2):
                    for dk in range(8):
                        for half in range(2):
                            conv_dma(w_up[l, dk * 128:(dk + 1) * 128, half * DFF:(half + 1) * DFF].rearrange("p (b c) -> p b c", c=128),
                                     Wup_blk[l].rearrange("b p dk c -> p b dk c")[:, :, dk, half * 128:(half + 1) * 128], "Wup_blk%d" % l)
                            yield
                    yield from conv_plain(w_down[l], Wd_b[l], DFF, D, "Wd_b%d" % l)
                yield from conv_plain(w_in, Win_b, D, 2120, "Win_b")
                yield from conv_plain(w_o, Wo_b, D, D, "Wo_b")
            conv_gen = [conv_all()]

            def conv_advance(n):
                for _ in range(n):
                    if conv_gen[0] is None:
                        return
                    try:
                        next(conv_gen[0])
                    except StopIteration:
                        conv_gen[0] = None
                        return
        with contextlib.ExitStack() as ph1:
            Tw = sbt(ph1, "Tw", [128, 64, 128], BF16)
            Pre = sbt(ph1, "Pre", [128, 64, 64], BF16)
            Pim = sbt(ph1, "Pim", [128, 64, 64], BF16)
            Qre = sbt(ph1, "Qre", [64, 64, 128], BF16)
            Qni = sbt(ph1, "Qni", [64, 64, 128], BF16)
            AR2 = sbt(ph1, "AR2", [64, 2, 64], F32)
            AIn = sbt(ph1, "AIn", [64, 2, 64], F32)
            gmx = sbt(ph1, "gmx", [64, D], F32)
            LD(lambda e: e.dma_start(out=gmx[:], in_=norm_mix[0:1, :].partition_broadcast(64)), (), ["gmx"])

            conv_advance(24)
            with contextlib.ExitStack() as ph0:
                lre = sbt(ph0, "lre", [64, 64], F32)
                lim = sbt(ph0, "lim", [64, 64], F32)
                dtb = sbt(ph0, "dtb", [64, 64], F32)
                ldre = sbt(ph0, "ldre", [64, 64], F32)
                ldim = sbt(ph0, "ldim", [64, 64], F32)
                E = sbt(ph0, "E", [64, 16, 64], F32)
                Y = sbt(ph0, "Y", [64, 16, 64], F32)
                Ri = sbt(ph0, "Ri", [64, 16, 64], I32)
                Rf = sbt(ph0, "Rf", [64, 16, 64], F32)
                CO = sbt(ph0, "CO", [64, 16, 64], F32)
                SI = sbt(ph0, "SI", [64, 16, 64], F32)
                are = sbt(ph0, "are", [64, 16, 64], F32)
                aim = sbt(ph0, "aim", [64, 16, 64], F32)
                t64 = [sbt(ph0, "t64_%d" % i, [64, 64], F32) for i in range(6)]
                fre = sbt(ph0, "fre", [64, 64], F32)
                fim = sbt(ph0, "fim", [64, 64], F32)
                Bre = sbt(ph0, "Bre", [64, 64, 16], F32)
                Bim = sbt(ph0, "Bim", [64, 64, 16], F32)
                bbre = sbt(ph0, "bbre", [64, 64, 16], F32)
                bbim = sbt(ph0, "bbim", [64, 64, 16], F32)
                tb1 = sbt(ph0, "tb1", [64, 64, 16], F32)
                tb2 = sbt(ph0, "tb2", [64, 64, 16], F32)
                Cn = [sbt(ph0, "Cn%d" % i, [128, 8, 64], F32) for i in range(2)]
                Ct = [sbt(ph0, "Ct%d" % i, [64, 64, 16], F32) for i in range(2)]
                maskf = sbt(ph0, "maskf", [128, 128], F32)
                dE = sbt(ph0, "dE", [128, 64], F32)
                PLre = sbt(ph0, "PLre", [64, 16, 15, 16], F32)
                PLim = sbt(ph0, "PLim", [64, 16, 15, 16], F32)
                QRre = sbt(ph0, "QRre", [64, 16, 9, 16], F32)
                QRni = sbt(ph0, "QRni", [64, 16, 9, 16], F32)
                tq1 = sbt(ph0, "tq1", [64, 16, 16], F32)
                tq2 = sbt(ph0, "tq2", [64, 16, 16], F32)
                tmsk = sbt(ph0, "tmsk", [128, 4, 128], F32)

                LD(lambda e: e.dma_start(out=lre[:], in_=lam_re.rearrange("g p -> p g"), allow_slow_non_contiguous=True), (), ["lre"])
                LD(lambda e: e.dma_start(out=lim[:], in_=lam_im.rearrange("g p -> p g"), allow_slow_non_contiguous=True), (), ["lim"])
                LD(lambda e: e.dma_start(out=dtb[:], in_=log_dt[0:1, :].partition_broadcast(64)), (), ["dtb"])
                LD(lambda e: e.dma_start(out=Bre[:], in_=b_re.rearrange("g p c -> p g c")), (), ["Bre"])
                LD(lambda e: e.dma_start(out=Bim[:], in_=b_im.rearrange("g p c -> p g c")), (), ["Bim"])
                for i, csrc in enumerate((c_re, c_im)):
                    LD(lambda e, i=i, csrc=csrc: e.dma_start(
                        out=Cn[i][:], in_=csrc.rearrange("(gb g8) c p -> (g8 c) gb p", g8=8)), (), ["Cn%d" % i])
                for j in range(8):
                    LD(lambda e, j=j: e.dma_start(out=dE[16 * j:16 * j + 16, :],
                                                  in_=ssm_d[0, :].rearrange("(g c) -> c g", c=16),
                                                  allow_slow_non_contiguous=True), (), ["dE"])
                A(lambda e: e.activation(out=dtb[:], in_=dtb[:], func=AF.Exp), ["dtb"], ["dtb"])
                V(lambda e: e.tensor_scalar_min(out=lre[:], in0=lre[:], scalar1=-1e-4), ["lre"], ["lre"])
                V(lambda e: e.tensor_mul(out=ldre[:], in0=lre[:], in1=dtb[:]), ["lre", "dtb"], ["ldre"])
                V(lambda e: e.tensor_mul(out=ldim[:], in0=lim[:], in1=dtb[:]), ["lim", "dtb"], ["ldim"])
                for kk in range(16):
                    k = kk - 7
                    A(lambda e, kk=kk, k=k: e.activation(out=E[:, kk, :], in_=ldre[:], func=AF.Exp, scale=float(k)), ["ldre"], ["E"])
                    V(lambda e, kk=kk, k=k: e.tensor_scalar_mul(out=Y[:, kk, :], in0=ldim[:], scalar1=float(k) / TWO_PI), ["ldim"], ["Y"])
                V(lambda e: e.tensor_copy(out=Ri[:], in_=Y[:]), ["Y"], ["Ri"])
                V(lambda e: e.tensor_copy(out=Rf[:], in_=Ri[:]), ["Ri"], ["Rf"])
                V(lambda e: e.tensor_sub(out=Y[:], in0=Y[:], in1=Rf[:]), ["Y", "Rf"], ["Y"])
                A(lambda e: e.activation(out=SI[:], in_=Y[:], func=AF.Sin, scale=SIN_SCALE), ["Y"], ["SI"])
                V(lambda e: e.tensor_scalar_add(out=Y[:], in0=Y[:], scalar1=0.25), ["Y", "SI"], ["Y"])
                V(lambda e: e.tensor_single_scalar(out=Rf[:], in_=Y[:], scalar=0.5, op=ALU.is_gt), ["Y"], ["Rf"])
                V(lambda e: e.tensor_sub(out=Y[:], in0=Y[:], in1=Rf[:]), ["Y", "Rf"], ["Y"])
                A(lambda e: e.activation(out=CO[:], in_=Y[:], func=AF.Sin, scale=SIN_SCALE), ["Y"], ["CO"])
                V(lambda e: e.tensor_mul(out=are[:], in0=E[:], in1=CO[:]), ["E", "CO"], ["are"])
                V(lambda e: e.tensor_mul(out=aim[:], in0=E[:], in1=SI[:]), ["E", "SI"], ["aim"])
                for h in range(2):
                    V(lambda e, h=h: e.tensor_copy(out=AR2[:, h, :], in_=are[:, 15, :]), ["are"], ["AR2"])
                V(lambda e: e.tensor_scalar_mul(out=AIn[:, 0, :], in0=aim[:, 15, :], scalar1=-1.0), ["aim"], ["AIn"])
                V(lambda e: e.tensor_copy(out=AIn[:, 1, :], in_=aim[:, 15, :]), ["aim"], ["AIn"])
                nre, a1i, den, u1, u2, u3 = t64
                V(lambda e: e.tensor_scalar_add(out=nre[:], in0=are[:, 8, :], scalar1=-1.0), ["are"], ["nre"])
                V(lambda e: e.tensor_copy(out=a1i[:], in_=aim[:, 8, :]), ["aim"], ["a1i"])
                V(lambda e: e.tensor_mul(out=den[:], in0=lre[:], in1=lre[:]), ["lre"], ["den"])
                V(lambda e: e.tensor_mul(out=u1[:], in0=lim[:], in1=lim[:]), ["lim"], ["u1"])
                V(lambda e: e.tensor_add(out=den[:], in0=den[:], in1=u1[:]), ["den", "u1"], ["den"])
                V(lambda e: e.reciprocal(out=den[:], in_=den[:]), ["den"], ["den"])
                V(lambda e: e.tensor_mul(out=u1[:], in0=nre[:], in1=lre[:]), ["nre", "lre"], ["u1"])
                V(lambda e: e.tensor_mul(out=u2[:], in0=a1i[:], in1=lim[:]), ["a1i", "lim"], ["u2"])
                V(lambda e: e.tensor_add(out=u1[:], in0=u1[:], in1=u2[:]), ["u1", "u2"], ["u1"])
                V(lambda e: e.tensor_mul(out=fre[:], in0=u1[:], in1=den[:]), ["u1", "den"], ["fre"])
                V(lambda e: e.tensor_mul(out=u2[:], in0=a1i[:], in1=lre[:]), ["a1i", "lre", "u1"], ["u2"])
                V(lambda e: e.tensor_mul(out=u3[:], in0=nre[:], in1=lim[:]), ["nre", "lim"], ["u3"])
                V(lambda e: e.tensor_sub(out=u2[:], in0=u2[:], in1=u3[:]), ["u2", "u3"], ["u2"])
                V(lambda e: e.tensor_mul(out=fim[:], in0=u2[:], in1=den[:]), ["u2", "den"], ["fim"])
                frb = bc_last(fre[:], 64, 64, 16)
                fib = bc_last(fim[:], 64, 64, 16)
                V(lambda e: e.tensor_mul(out=tb1[:], in0=Bre[:], in1=frb), ["Bre", "fre"], ["tb1"])
                V(lambda e: e.tensor_mul(out=tb2[:], in0=Bim[:], in1=fib), ["Bim", "fim"], ["tb2"])
                V(lambda e: e.tensor_sub(out=bbre[:], in0=tb1[:], in1=tb2[:]), ["tb1", "tb2"], ["bbre"])
                V(lambda e: e.tensor_mul(out=tb1[:], in0=Bim[:], in1=frb), ["Bim", "fre", "bbre"], ["tb1"])
                V(lambda e: e.tensor_mul(out=tb2[:], in0=Bre[:], in1=fib), ["Bre", "fim", "bbre"], ["tb2"])
                V(lambda e: e.tensor_add(out=bbim[:], in0=tb1[:], in1=tb2[:]), ["tb1", "tb2"], ["bbim"])
                for i in range(2):
                    for half in range(2):
                        for q4 in range(4):
                            gb = half * 4 + q4
                            T(lambda e, i=i, gb=gb, q4=q4: e.transpose(out=psF[0][0:64, q4 * 128:(q4 + 1) * 128],
                                                                       in_=Cn[i][:, gb, :], identity=identf[:]),
                              ["Cn%d" % i, "identf"], ["psF0"])
                        V(lambda e, i=i, half=half: e.tensor_copy(
                            out=Ct[i][:, half * 32:(half + 1) * 32, :].rearrange("p g c -> p (g c)"),
                            in_=psF[0][0:64, :]), ["psF0"], ["Ct%d" % i])
                V(lambda e: e.memset(maskf[:], 0.0), (), ["maskf"])
                for j in range(8):
                    V(lambda e, j=j: e.memset(maskf[0:16 * (j + 1), 16 * j:16 * j + 16], 1.0), (), ["maskf"])
                for gq in range(4):
                    g0 = gq * 16
                    for m in range(15):
                        kk = 14 - m
                        arb = bc_last(are[:, kk, g0:g0 + 16], 64, 16, 16)
                        aib = bc_last(aim[:, kk, g0:g0 + 16], 64, 16, 16)
                        V(lambda e, arb=arb, g0=g0: e.tensor_mul(out=tq1[:], in0=bbre[:, g0:g0 + 16, :], in1=arb), ["bbre", "are"], ["tq1"])
                        V(lambda e, aib=aib, g0=g0: e.tensor_mul(out=tq2[:], in0=bbim[:, g0:g0 + 16, :], in1=aib), ["bbim", "aim"], ["tq2"])
                        V(lambda e, m=m: e.tensor_sub(out=PLre[:, :, m, :], in0=tq1[:], in1=tq2[:]), ["tq1", "tq2"], ["PLre"])
                        V(lambda e, arb=arb, g0=g0: e.tensor_mul(out=tq1[:], in0=bbim[:, g0:g0 + 16, :], in1=arb), ["bbim", "are", "PLre"], ["tq1"])
                        V(lambda e, aib=aib, g0=g0: e.tensor_mul(out=tq2[:], in0=bbre[:, g0:g0 + 16, :], in1=aib), ["bbre", "aim", "PLre"], ["tq2"])
                        V(lambda e, m=m: e.tensor_add(out=PLim[:, :, m, :], in0=tq1[:], in1=tq2[:]), ["tq1", "tq2"], ["PLim"])
                    for k in range(9):
                        kk = k + 7
                        arb = bc_last(are[:, kk, g0:g0 + 16], 64, 16, 16)
                        aib = bc_last(aim[:, kk, g0:g0 + 16], 64, 16, 16)
                        V(lambda e, arb=arb, g0=g0: e.tensor_mul(out=tq1[:], in0=Ct[0][:, g0:g0 + 16, :], in1=arb), ["Ct0", "are", "PLim"], ["tq1"])
                        V(lambda e, aib=aib, g0=g0: e.tensor_mul(out=tq2[:], in0=Ct[1][:, g0:g0 + 16, :], in1=aib), ["Ct1", "aim", "PLim"], ["tq2"])
                        V(lambda e, k=k: e.tensor_sub(out=QRre[:, :, k, :], in0=tq1[:], in1=tq2[:]), ["tq1", "tq2"], ["QRre"])
                        V(lambda e, aib=aib, g0=g0: e.tensor_mul(out=tq1[:], in0=Ct[0][:, g0:g0 + 16, :], in1=aib), ["Ct0", "aim", "QRre"], ["tq1"])
                        V(lambda e, arb=arb, g0=g0: e.tensor_mul(out=tq2[:], in0=Ct[1][:, g0:g0 + 16, :], in1=arb), ["Ct1", "are", "QRre"], ["tq2"])
                        V(lambda e: e.tensor_add(out=tq1[:], in0=tq1[:], in1=tq2[:]), ["tq1", "tq2"], ["tq1"])
                        V(lambda e, k=k: e.tensor_scalar_mul(out=QRni[:, :, k, :], in0=tq1[:], scalar1=-1.0), ["tq1"], ["QRni"])
                    V(lambda e, g0=g0: e.tensor_copy(out=Qre[:, g0:g0 + 16, :].rearrange("p g (k c) -> p g k c", c=16),
                                                     in_=QRre[:, :, 1:9, :]), ["QRre"], ["Qre"])
                    V(lambda e, g0=g0: e.tensor_copy(out=Qni[:, g0:g0 + 16, :].rearrange("p g (k c) -> p g k c", c=16),
                                                     in_=QRni[:, :, 1:9, :]), ["QRni"], ["Qni"])
                    for q in range(4):
                        bank = psF[1 + (q % 2)]
                        bn = "psF%d" % (1 + (q % 2))
                        for gi in range(4):
                            gl = q * 4 + gi
                            T(lambda e, gl=gl, gi=gi, bank=bank: e.matmul(
                                bank[:, gi * 128:(gi + 1) * 128],
                                lhsT=PLre[:, gl, 7:15, :].rearrange("p m c -> p (m c)"),
                                rhs=QRre[:, gl, 0:8, :].rearrange("p k c -> p (k c)"), start=True, stop=False),
                              ["PLre", "QRre"], [bn])
                            T(lambda e, gl=gl, gi=gi, bank=bank: e.matmul(
                                bank[:, gi * 128:(gi + 1) * 128],
                                lhsT=PLim[:, gl, 7:15, :].rearrange("p m c -> p (m c)"),
                                rhs=QRni[:, gl, 0:8, :].rearrange("p k c -> p (k c)"), start=False, stop=True),
                              ["PLim", "QRni"], [bn])
                        V(lambda e, bank=bank: e.tensor_mul(out=tmsk[:], in0=bank[:, :].rearrange("p (g e) -> p g e", e=128),
                                                            in1=maskf[:].rearrange("p (o e) -> p o e", o=1).to_broadcast([128, 4, 128])),
                          [bn, "maskf"], ["tmsk"])
                        for gi in range(4):
                            g = g0 + q * 4 + gi
                            V(lambda e, g=g, gi=gi: e.scalar_tensor_tensor(out=Tw[:, g, :], in0=identf[:], scalar=dE[:, g:g + 1],
                                                                          in1=tmsk[:, gi, :], op0=ALU.mult, op1=ALU.add),
                              ["tmsk", "dE", "identf"], ["Tw"])
                    for half in range(2):
                        for i, (PLx, Px, nm) in enumerate(((PLre, Pre, "PLre"), (PLim, Pim, "PLim"))):
                            bank = psF[3 + i]
                            bn = "psF%d" % (3 + i)
                            for gi in range(8):
                                gl = half * 8 + gi
                                T(lambda e, gl=gl, gi=gi, bank=bank, PLx=PLx: e.transpose(
                                    out=bank[:, gi * 64:(gi + 1) * 64],
                                    in_=PLx[:, gl, 0:8, :].rearrange("p m c -> p (m c)"), identity=identf[0:64, 0:64]),
                                  [nm, "identf"], [bn])
                            gg = g0 + half * 8
                            A(lambda e, gg=gg, bank=bank, Px=Px: e.copy(out=Px[:, gg:gg + 8, :].rearrange("p g s -> p (g s)"),
                                                                        in_=bank[:, :]), [bn], ["P" + nm[2:]])
                dbg("are", are[:], [64, 16, 64], F32, ["are"])
                dbg("aim", aim[:], [64, 16, 64], F32, ["aim"])
                dbg("fre", fre[:], [64, 64], F32, ["fre"])
                dbg("bbre", bbre[:], [64, 64, 16], F32, ["bbre"])
                dbg("bbim", bbim[:], [64, 64, 16], F32, ["bbim"])
                dbg("Ct0", Ct[0][:], [64, 64, 16], F32, ["Ct0"])
                dbg("PLre", PLre[:], [64, 16, 15, 16], F32, ["PLre"])
                dbg("QRre", QRre[:], [64, 16, 9, 16], F32, ["QRre"])
                dbg("Tw", Tw[:], [128, 64, 128], BF16, ["Tw"])
                dbg("Pre", Pre[:], [128, 64, 64], BF16, ["Pre"])
                dbg("Pim", Pim[:], [128, 64, 64], BF16, ["Pim"])
                dbg("Qre", Qre[:], [64, 64, 128], BF16, ["Qre"])
                dbg("Qni", Qni[:], [64, 64, 128], BF16, ["Qni"])
            P.barrier(tiny)
            with contextlib.ExitStack() as ph1b:
                X = sbt(ph1b, "X", [64, 8, D], F32)
                sq = sbt(ph1b, "sq", [64, D], BF16)
                ss = sbt(ph1b, "ss", [64, 8], F32)
                uz = sbt(ph1b, "uz", [128, 8192], BF16)
                uE = uz[0:64, :].rearrange("p (g j c) -> p g j c", g=64, j=8)
                UEb = [sbt(ph1b, "UE%d" % i, [128, 64, 64], BF16) for i in range(2)]
                SSb = [sbt(ph1b, "SS%d" % i, [64, 2, 64, 64], BF16) for i in range(2)]
                HHbb = [sbt(ph1b, "HHb%d" % i, [64, 2, 64, 65], BF16) for i in range(2)]
                Xst = [sbt(ph1b, "Xst%d" % i, [64, 2, 64], F32) for i in range(8)]
                T1 = sbt(ph1b, "T1", [64, 2, 64], F32)
                T2 = sbt(ph1b, "T2", [64, 2, 64], F32)
                zE = uz[:, 0:4096].rearrange("p (g n) -> p g n", n=64)
                zsub = X[:, :, :].rearrange("p j d -> p (j d)").bitcast(BF16)[:, 0:8192].rearrange("p (j g c) -> p j g c", j=8, g=64)

                step = [0]

                def ssm_front(xsrc, nn, bi):
                    UE = UEb[bi]
                    UEn = "UE%d" % bi
                    SS = SSb[bi]
                    SSn = "SS%d" % bi
                    LD(lambda e: e.dma_start(out=X[:nn], in_=xsrc.rearrange("(n j) d -> n j d", j=8)), (), ["X"])
                    for j in range(8):
                        A(lambda e, j=j: e.activation(out=sq[:nn], in_=X[:nn, j, :], func=AF.Square, accum_out=ss[:nn, j:j + 1]),
                          ["X"], ["sq", "ss"])
                    V(lambda e: e.tensor_scalar(out=ss[:nn], in0=ss[:nn], scalar1=1.0 / D, scalar2=EPS, op0=ALU.mult, op1=ALU.add), ["ss"], ["ss"])
                    A(lambda e: e.sqrt(out=ss[:nn], in_=ss[:nn]), ["ss"], ["ss"])
                    yield
                    V(lambda e: e.reciprocal(out=ss[:nn], in_=ss[:nn]), ["ss"], ["ss"])
                    for j in range(8):
                        V(lambda e, j=j: e.scalar_tensor_tensor(
                            out=uE[:nn, :, j, :], in0=X[:nn, j, :].rearrange("p (g c) -> p g c", c=16), scalar=ss[:nn, j:j + 1],
                            in1=gmx[:nn].rearrange("p (g c) -> p g c", c=16), op0=ALU.mult, op1=ALU.mult),
                          ["X", "ss", "gmx"], ["uz"])
                    yield
                    for g8 in range(8):
                        bank = psB[g8 % 2]
                        bn = "psB%d" % (g8 % 2)
                        for gi in range(8):
                            g = g8 * 8 + gi
                            T(lambda e, g=g, gi=gi, bank=bank: e.transpose(
                                out=bank[:, gi * nn:(gi + 1) * nn], in_=uE[:nn, g, :, :].rearrange("p j c -> p (j c)"),
                                identity=identb[:nn, :nn]), ["uz", "identb"], [bn])
                        A(lambda e, g8=g8, bank=bank: e.copy(out=UE[:, g8 * 8:(g8 + 1) * 8, :nn],
                                                             in_=bank[:, 0:8 * nn].rearrange("p (g n) -> p g n", n=nn)), [bn], [UEn])
                    yield
                    for g8 in range(8):
                        for i, Px in enumerate((Pre, Pim)):
                            bank = psF[(g8 % 2) * 2 + i]
                            bn = "psF%d" % ((g8 % 2) * 2 + i)
                            for gi in range(8):
                                g = g8 * 8 + gi
                                T(lambda e, g=g, gi=gi, bank=bank, Px=Px: e.matmul(
                                    bank[0:64, gi * nn:(gi + 1) * nn], lhsT=Px[:, g, :], rhs=UE[:, g, :nn], start=True, stop=True),
                                  [UEn, "Pre", "Pim"], [bn])
                            A(lambda e, g8=g8, i=i, bank=bank: e.copy(
                                out=SS[:, i, g8 * 8:(g8 + 1) * 8, :nn], in_=bank[0:64, 0:8 * nn].rearrange("p (g n) -> p g n", n=nn)),
                              [bn], [SSn])

                def ssm_scan(nn, bi, first, init_dram=None, sched=None):
                    SS = SSb[bi]
                    SSn = "SS%d" % bi
                    HHb = HHbb[bi]
                    HHn = "HHb%d" % bi
                    sched = sched or {}
                    if first:
                        cur0 = Xst[step[0] % 8]
                        if init_dram is None:
                            V(lambda e: e.memset(cur0[:], 0.0), (), ["Xst%d" % (step[0] % 8)])
                        else:
                            for h in range(2):
                                LD(lambda e, h=h: e.dma_start(out=cur0[:, h, :], in_=init_dram[h].rearrange("g p -> p g"),
                                                              allow_slow_non_contiguous=True), (), ["Xst%d" % (step[0] % 8)])
                    prev_i = step[0] % 8
                    A(lambda e, prev_i=prev_i: e.copy(out=HHb[:, :, :, 0], in_=Xst[prev_i][:]), ["Xst%d" % prev_i], [HHn])
                    for n in range(nn):
                        pi = step[0] % 8
                        ci = (step[0] + 1) % 8
                        step[0] += 1
                        prv = Xst[pi]
                        cur = Xst[ci]
                        pn = "Xst%d" % pi
                        cn = "Xst%d" % ci
                        V(lambda e, prv=prv: e.tensor_mul(out=T1[:], in0=prv[:], in1=AR2[:]), [pn, "AR2"], ["T1"])
                        V(lambda e, prv=prv: e.tensor_mul(out=T2[:], in0=prv[:, ::-1, :], in1=AIn[:]), [pn, "AIn"], ["T2"])
                        V(lambda e: e.tensor_add(out=T1[:], in0=T1[:], in1=T2[:]), ["T1", "T2"], ["T1"])
                        V(lambda e, cur=cur, n=n: e.tensor_add(out=cur[:], in0=T1[:], in1=SS[:, :, :, n]), ["T1", SSn], [cn])
                        A(lambda e, cur=cur, n=n: e.copy(out=HHb[:, :, :, n + 1], in_=cur[:]), [cn], [HHn])
                        if n % 2 == 1:
                            conv_advance(1)
                        for g_ in sched.get(n, ()):
                            next(g_, None)
                    for gl in sched.values():
                        for g_ in gl:
                            for _ in g_:
                                pass

                def ssm_back(nn, bi, zdst):
                    UE = UEb[bi]
                    UEn = "UE%d" % bi
                    HHb = HHbb[bi]
                    HHn = "HHb%d" % bi
                    for g8 in range(8):
                        bank = psF[4 + (g8 % 2)]
                        bn = "psF%d" % (4 + (g8 % 2))
                        for gi in range(8):
                            g = g8 * 8 + gi
                            T(lambda e, g=g, gi=gi, bank=bank: e.matmul(bank[:, gi * nn:(gi + 1) * nn], lhsT=Tw[:, g, :],
                                                                        rhs=UE[:, g, :nn], start=True, stop=False), ["Tw", UEn], [bn])
                            T(lambda e, g=g, gi=gi, bank=bank: e.matmul(bank[:, gi * nn:(gi + 1) * nn], lhsT=Qre[:, g, :],
                                                                        rhs=HHb[:, 0, g, 0:nn], start=False, stop=False), ["Qre", HHn], [bn])
                            T(lambda e, g=g, gi=gi, bank=bank: e.matmul(bank[:, gi * nn:(gi + 1) * nn], lhsT=Qni[:, g, :],
                                                                        rhs=HHb[:, 1, g, 0:nn], start=False, stop=True), ["Qni", HHn], [bn])
                        A(lambda e, g8=g8, bank=bank: e.activation(out=zE[:, g8 * 8:(g8 + 1) * 8, :nn],
                                                                   in_=bank[:, 0:8 * nn].rearrange("p (g n) -> p g n", n=nn), func=AF.Gelu),
                          [bn], ["uz"])
                    yield
                    for g8 in range(8):
                        bank = psB[g8 % 2]
                        bn = "psB%d" % (g8 % 2)
                        for gi in range(8):
                            g = g8 * 8 + gi
                            T(lambda e, g=g, gi=gi, bank=bank: e.transpose(out=bank[:nn, gi * 128:(gi + 1) * 128], in_=zE[:, g, :nn],
                                                                           identity=identb[:]), ["uz", "identb"], [bn])
                        V(lambda e, g8=g8, bank=bank: e.tensor_copy(
                            out=zsub[:nn, :, g8 * 8:(g8 + 1) * 8, :].rearrange("n j g c -> n g j c"),
                            in_=bank[:nn, :].rearrange("n (g j c) -> n g j c", g=8, j=8)), [bn], ["X"])
                    ST(lambda e: e.dma_start(out=zdst.rearrange("(n j) d -> n j d", j=8),
                                             in_=zsub[:nn].rearrange("n j g c -> n j (g c)")), ["X"], ["Zscr"])

                def ssm_state_out(dst):
                    ci = step[0] % 8
                    for h in range(2):
                        ST(lambda e, h=h: e.dma_start(out=dst[h].rearrange("g p -> p g"), in_=Xst[ci][:, h, :],
                                                      allow_slow_non_contiguous=True), ["Xst%d" % ci], ["o_ssm"])

                for _ in ssm_front(xp[0:512, :], 64, 0):
                    pass
                prev_back = None
                for blk in range(8):
                    if blk + 1 < 8:
                        fg = ssm_front(xp[(blk + 1) * 512:(blk + 2) * 512, :], 64, (blk + 1) % 2)
                    else:
                        fg = ssm_front(xs, 4, 0)
                    sched = {24: [fg], 36: [fg], 44: [fg], 52: [fg]}
                    if prev_back is not None:
                        sched[1] = [prev_back]
                        sched[12] = [prev_back]
                    ssm_scan(64, blk % 2, first=(blk == 0), sched=sched)
                    prev_back = ssm_back(64, blk % 2, Zp[blk * 512:(blk + 1) * 512, :])
                for _ in prev_back:
                    pass
                ssm_state_out(o_ssm_p)
                step[0] += 1
                ssm_scan(4, 0, first=True, init_dram=(st_re, st_im))
                for _ in ssm_back(4, 0, Zs):
                    pass
                ssm_state_out(o_ssm_s)
                conv_advance(100000)
        P.barrier(tiny)

        def ffn_phase(layer, segs, with_glu):
            with contextlib.ExitStack() as ph:
                sbt = lambda st, name, shape, dt: st.enter_context(nc.sbuf_tensor(name + "_L%d" % layer, list(shape), dt))
                if with_glu:
                    Wglu = sbt(ph, "Wglu", [128, 8, 2 * D], BF16)
                    bglu = sbt(ph, "bglu", [128, 2 * D], F32)
                    zt = sbt(ph, "zt", [128, 4, D], BF16)
                    zT = sbt(ph, "zT", [128, 8, 512], BF16)
                    LD(lambda e: e.dma_start(out=Wglu[:], in_=Wglu_b.rearrange("(dk p) n -> p dk n", p=128)), ["Wglu_b"], ["Wglu"])
                    LD(lambda e: e.dma_start(out=bglu[:], in_=b_glu[0:1, :].partition_broadcast(128)), (), ["bglu"])
                Wd = sbt(ph, "Wd", [128, NFC, D], BF16)
                gffn = sbt(ph, "gffn", [128, D], F32)
                cw = sbt(ph, "cw", [128, NFC, 3], F32)
                cb = sbt(ph, "cb", [128, NFC], F32)
                halo = sbt(ph, "halo", [128, NFC, 2], F32)
                x1b = [sbt(ph, "x1_%d" % i, [128, 4, D], F32) for i in range(2)]
                stc = [0]
                xn = sbt(ph, "xn", [128, 4, D], BF16)
                xnT = sbt(ph, "xnT", [128, 8, 512], BF16)
                t1 = sbt(ph, "t1f", [128, D], F32)
                t2 = sbt(ph, "t2f", [128, D], F32)
                sqj = sbt(ph, "sqj", [128, D], BF16)
                ssq = sbt(ph, "ssq", [128, 4], F32)
                wblk = [sbt(ph, "wblk%d" % i, [128, 8, 256], BF16) for i in range(3)]
                aext = [sbt(ph, "aext%d" % i, [128, 514], F32) for i in range(2)]
                cbuf = [sbt(ph, "cbuf%d" % i, [128, 512], F32) for i in range(2)]
                hT = sbt(ph, "hT", [128, NFC, 512], BF16)
                LD(lambda e: e.dma_start(out=gffn[:], in_=norm_ffn[layer:layer + 1, :].partition_broadcast(128)), (), ["gffn"])
                late = [True]

                def late_loads():
                    if not late[0]:
                        return
                    late[0] = False
                    for j in range(3):
                        LD(lambda e, j=j: e.dma_start(out=cw[:, :, j], in_=conv_w[layer, j].rearrange("(b p) -> p b", p=128),
                                                      allow_slow_non_contiguous=True), (), ["cw"])
                    LD(lambda e: e.dma_start(out=cb[:], in_=conv_b[layer].rearrange("(b p) -> p b", p=128),
                                             allow_slow_non_contiguous=True), (), ["cb"])
                    LD(lambda e: e.dma_start(out=Wd[:], in_=Wd_b[layer].rearrange("(f p) n -> p f n", p=128)), ["Wd_b%d" % layer], ["Wd"])
                ublk = [0]
                for (xsrc, zsrc, dst, ntot, pp, nflag, halo_src, conv_dst) in segs:
                    TTfull = min(4, ntot // pp)
                    if isinstance(halo_src, str):
                        pass
                    elif halo_src is None:
                        G(lambda e: e.memset(halo[:], 0.0), (), ["halo"])
                    else:
                        for t_ in range(2):
                            LD(lambda e, halo_src=halo_src, t_=t_: e.dma_start(
                                out=halo[:, :, t_], in_=halo_src[t_].rearrange("(b p) -> p b", p=128),
                                allow_slow_non_contiguous=True), (), ["halo"])
                    nst = ntot // (TTfull * pp)
                    for sti in range(nst):
                        TT = TTfull
                        ntok = TT * pp
                        r0 = sti * ntok
                        do_flag = sti < nflag
                        x1 = x1b[stc[0] % 2]
                        x1n = "x1_%d" % (stc[0] % 2)
                        stc[0] += 1
                        LD(lambda e, x1=x1, r0=r0, ntok=ntok, pp=pp, TT=TT, xsrc=xsrc: e.dma_start(
                            out=x1[:pp, :TT, :], in_=xsrc[r0:r0 + ntok, :].rearrange("(tt p) d -> p tt d", p=pp)), ["Xsrc", "X3"], [x1n])
                        if with_glu:
                            LD(lambda e, x1=x1, r0=r0, ntok=ntok, pp=pp, TT=TT, zsrc=zsrc: e.dma_start(
                                out=zt[:pp, :TT, :], in_=zsrc[r0:r0 + ntok, :].rearrange("(tt p) d -> p tt d", p=pp)), ["Zscr"], ["zt"])
                            for tt in range(TT):
                                bank = psB[tt % 2]
                                bn = "psB%d" % (tt % 2)
                                for dk in range(8):
                                    T(lambda e, x1=x1, tt=tt, dk=dk, bank=bank, pp=pp: e.transpose(
                                        out=bank[:, dk * pp:(dk + 1) * pp], in_=zt[:pp, tt, dk * 128:(dk + 1) * 128],
                                        identity=identb[:pp, :pp]), ["zt", "identb"], [bn])
                                A(lambda e, x1=x1, tt=tt, bank=bank, pp=pp: e.copy(
                                    out=zT[:, :, tt * pp:(tt + 1) * pp], in_=bank[:, 0:8 * pp].rearrange("p (k n) -> p k n", n=pp)), [bn], ["zT"])
                            for tt in range(TT):
                                for nb in range(4):
                                    for dk in range(8):
                                        T(lambda e, x1=x1, tt=tt, nb=nb, dk=dk, pp=pp: e.matmul(
                                            psF[nb][:pp, :], lhsT=zT[:, dk, tt * pp:(tt + 1) * pp], rhs=Wglu[:, dk, nb * 512:(nb + 1) * 512],
                                            start=(dk == 0), stop=(dk == 7)), ["zT", "Wglu"], ["psF%d" % nb])
                                for nb in range(2):
                                    V(lambda e, x1=x1, nb=nb, pp=pp: e.tensor_add(out=t1[:pp, nb * 512:(nb + 1) * 512], in0=psF[nb][:pp, :],
                                                                           in1=bglu[:pp, nb * 512:(nb + 1) * 512]), ["psF%d" % nb, "bglu"], ["t1f"])
                                    V(lambda e, x1=x1, nb=nb, pp=pp: e.tensor_add(out=t2[:pp, nb * 512:(nb + 1) * 512], in0=psF[2 + nb][:pp, :],
                                                                           in1=bglu[:pp, D + nb * 512:D + (nb + 1) * 512]),
                                      ["psF%d" % (2 + nb), "bglu"], ["t2f"])
                                A(lambda e, x1=x1, pp=pp: e.activation(out=t2[:pp, :], in_=t2[:pp, :], func=AF.Sigmoid), ["t2f"], ["t2f"])
                                V(lambda e, x1=x1, pp=pp: e.tensor_mul(out=t1[:pp, :], in0=t1[:pp, :], in1=t2[:pp, :]), ["t1f", "t2f"], ["t1f"])
                                V(lambda e, x1=x1, tt=tt, pp=pp: e.tensor_add(out=x1[:pp, tt, :], in0=x1[:pp, tt, :], in1=t1[:pp, :]), ["t1f", x1n], [x1n])
                                if do_flag:
                                    V(lambda e, x1=x1, tt=tt, pp=pp: e.tensor_scalar_mul(out=x1[:pp, tt, :], in0=x1[:pp, tt, :], scalar1=flg[:pp, 0:1]),
                                      [x1n, "flg"], [x1n])
                        for tt in range(TT):
                            A(lambda e, x1=x1, tt=tt, pp=pp: e.activation(out=sqj[:pp, :], in_=x1[:pp, tt, :], func=AF.Square,
                                                                   accum_out=ssq[:pp, tt:tt + 1]), [x1n], ["sqj", "ssq%d" % tt])
                            V(lambda e, x1=x1, pp=pp, tt=tt: e.tensor_scalar(out=ssq[:pp, tt:tt + 1], in0=ssq[:pp, tt:tt + 1], scalar1=1.0 / D, scalar2=EPS,
                                                                      op0=ALU.mult, op1=ALU.add), ["ssq%d" % tt], ["ssq%d" % tt])
                            A(lambda e, x1=x1, pp=pp, tt=tt: e.sqrt(out=ssq[:pp, tt:tt + 1], in_=ssq[:pp, tt:tt + 1]), ["ssq%d" % tt], ["ssq%d" % tt])
                            V(lambda e, x1=x1, pp=pp, tt=tt: e.reciprocal(out=ssq[:pp, tt:tt + 1], in_=ssq[:pp, tt:tt + 1]), ["ssq%d" % tt], ["ssq%d" % tt])
                            V(lambda e, x1=x1, tt=tt, pp=pp: e.scalar_tensor_tensor(out=xn[:pp, tt, :], in0=x1[:pp, tt, :], scalar=ssq[:pp, tt:tt + 1],
                                                                             in1=gffn[:pp, :], op0=ALU.mult, op1=ALU.mult),
                              [x1n, "ssq%d" % tt, "gffn"], ["xn%d" % tt])
                        for tt in range(TT):
                            bank = psB[tt % 2]
                            bn = "psB%d" % (tt % 2)
                            for dk in range(8):
                                T(lambda e, x1=x1, tt=tt, dk=dk, bank=bank, pp=pp: e.transpose(
                                    out=bank[:, dk * pp:(dk + 1) * pp], in_=xn[:pp, tt, dk * 128:(dk + 1) * 128],
                                    identity=identb[:pp, :pp]), ["xn%d" % tt, "identb"], [bn])
                            A(lambda e, x1=x1, tt=tt, bank=bank, pp=pp: e.copy(
                                out=xnT[:, :, tt * pp:(tt + 1) * pp], in_=bank[:, 0:8 * pp].rearrange("p (k n) -> p k n", n=pp)), [bn], ["xnT"])
                        if dst is None:
                            for b in range(NFC):
                                wi = ublk[0] % 3
                                ublk[0] += 1
                                wb = wblk[wi]
                                wn = "wblk%d" % wi
                                LD(lambda e, x1=x1, b=b, wb=wb: e.dma_start(out=wb[:], in_=Wup_blk[layer][b]), ["Wup_blk%d" % layer], [wn])
                                for dk in range(8):
                                    T(lambda e, x1=x1, b=b, dk=dk, wb=wb, ntok=ntok: e.matmul(
                                        psF[0][:, 2 * b:2 * b + 2], lhsT=wb[:, dk, 0:128], rhs=xnT[:, dk, ntok - 2:ntok],
                                        start=(dk == 0), stop=(dk == 7)), [wn, "xnT"], ["psF0"])
                            A(lambda e, x1=x1: e.copy(out=halo[:, :, :], in_=psF[0][:, 0:2 * NFC].rearrange("p (b t) -> p b t", t=2)),
                              ["psF0"], ["halo"])
                            continue
                        late_loads()
                        for b in range(NFC):
                            wi = ublk[0] % 3
                            ai = ublk[0] % 2
                            ublk[0] += 1
                            wb = wblk[wi]
                            wn = "wblk%d" % wi
                            pa = psF[2 * wi]
                            pv = psF[2 * wi + 1]
                            pan = "psF%d" % (2 * wi)
                            pvn = "psF%d" % (2 * wi + 1)
                            ax = aext[ai]
                            axn = "aext%d" % ai
                            cbf = cbuf[ai]
                            cbn = "cbuf%d" % ai
                            LD(lambda e, x1=x1, b=b, wb=wb: e.dma_start(out=wb[:], in_=Wup_blk[layer][b]), ["Wup_blk%d" % layer], [wn])
                            for i, (pbank, pn) in enumerate(((pa, pan), (pv, pvn))):
                                for dk in range(8):
                                    T(lambda e, x1=x1, i=i, dk=dk, pbank=pbank, wb=wb, ntok=ntok: e.matmul(
                                        pbank[:, :ntok], lhsT=wb[:, dk, i * 128:(i + 1) * 128], rhs=xnT[:, dk, :ntok],
                                        start=(dk == 0), stop=(dk == 7)), [wn, "xnT"], [pn])
                            A(lambda e, x1=x1, ax=ax, pa=pa, ntok=ntok: e.copy(out=ax[:, 2:2 + ntok], in_=pa[:, :ntok]), [pan], [axn])
                            G(lambda e, x1=x1, ax=ax, b=b: e.tensor_copy(out=ax[:, 0:2], in_=halo[:, b, :]), ["halo"], [axn])
                            G(lambda e, x1=x1, ax=ax, b=b, ntok=ntok: e.tensor_copy(out=halo[:, b, :], in_=ax[:, ntok:ntok + 2]), [axn], ["halo"])
                            V(lambda e, x1=x1, ax=ax, cbf=cbf, b=b, ntok=ntok: e.tensor_scalar(
                                out=cbf[:, :ntok], in0=ax[:, 2:2 + ntok], scalar1=cw[:, b, 2:3], scalar2=cb[:, b:b + 1],
                                op0=ALU.mult, op1=ALU.add), [axn, "cw", "cb"], [cbn])
                            V(lambda e, x1=x1, ax=ax, cbf=cbf, b=b, ntok=ntok: e.scalar_tensor_tensor(
                                out=cbf[:, :ntok], in0=ax[:, 1:1 + ntok], scalar=cw[:, b, 1:2], in1=cbf[:, :ntok],
                                op0=ALU.mult, op1=ALU.add), [axn, "cw", cbn], [cbn])
                            V(lambda e, x1=x1, ax=ax, cbf=cbf, b=b, ntok=ntok: e.scalar_tensor_tensor(
                                out=cbf[:, :ntok], in0=ax[:, 0:ntok], scalar=cw[:, b, 0:1], in1=cbf[:, :ntok],
                                op0=ALU.mult, op1=ALU.add), [axn, "cw", cbn], [cbn])
                            A(lambda e, x1=x1, cbf=cbf, ntok=ntok: e.activation(out=cbf[:, :ntok], in_=cbf[:, :ntok], func=AF.Gelu), [cbn], [cbn])
                            V(lambda e, x1=x1, cbf=cbf, b=b, pv=pv, ntok=ntok: e.tensor_mul(out=hT[:, b, :ntok], in0=cbf[:, :ntok], in1=pv[:, :ntok]),
                              [cbn, pvn], ["hT"])
                        for tt in range(TT if dst is not None else 0):
                            for nb in range(2):
                                bi = (tt * 2 + nb) % 6
                                for fch in range(NFC):
                                    T(lambda e, x1=x1, tt=tt, nb=nb, fch=fch, bi=bi, pp=pp: e.matmul(
                                        psF[bi][:pp, :], lhsT=hT[:, fch, tt * pp:(tt + 1) * pp], rhs=Wd[:, fch, nb * 512:(nb + 1) * 512],
                                        start=(fch == 0), stop=(fch == NFC - 1)), ["hT", "Wd"], ["psF%d" % bi])
                                V(lambda e, x1=x1, tt=tt, nb=nb, bi=bi, pp=pp: e.tensor_add(
                                    out=x1[:pp, tt, nb * 512:(nb + 1) * 512], in0=x1[:pp, tt, nb * 512:(nb + 1) * 512], in1=psF[bi][:pp, :]),
                                  ["psF%d" % bi, x1n], [x1n])
                        if dst is not None:
                            ST(lambda e, x1=x1, r0=r0, ntok=ntok, pp=pp, TT=TT, dst=dst: e.dma_start(
                                out=dst[r0:r0 + ntok, :].rearrange("(tt p) d -> p tt d", p=pp), in_=x1[:pp, :TT, :]), [x1n], ["Xdst"])
                    for t_ in range(2 if conv_dst is not None else 0):
                        ST(lambda e, x1=x1, conv_dst=conv_dst, t_=t_: e.dma_start(
                            out=conv_dst[t_].rearrange("(b p) -> p b", p=128), in_=halo[:, :, t_],
                            allow_slow_non_contiguous=True), ["halo"], ["o_conv"])

        ffn_phase(0, [(xp, Zp, X2p, 4096, 128, 4, None, o_conv_p[0]),
                      (xs, Zs, X2s, 32, 32, 0, cconv[0], o_conv_s[0])], True)
        P.barrier(tiny)

        with contextlib.ExitStack() as ph3:
            Win = sbt(ph3, "Win", [128, 8, 2120], BF16)
            Wo = sbt(ph3, "Wo", [128, 8, D], BF16)
            g1T = sbt(ph3, "g1T", [128, 8], F32)
            est = sbt(ph3, "est", [128, 2], F32)
            hstb = [sbt(ph3, "hst%d" % i, [128, 20], F32) for i in range(2)]
            epsN = sbt(ph3, "epsN", [128, 1], F32)
            qg = sbt(ph3, "qg", [128, 64], F32)
            kg = sbt(ph3, "kg", [128, 64], F32)
            identL = sbt(ph3, "identL", [128, 128], BF16)
            iop = sbt(ph3, "iop", [128, 1], F32)
            inv8 = sbt(ph3, "inv8", [128, 8], F32)
            negC = sbt(ph3, "negC", [128, 1], F32)
            pw2 = sbt(ph3, "pw2", [128, 30], F32)
            kT_all = sbt(ph3, "kT_all", [128, 4, 4224], BF16)
            kiT_all = sbt(ph3, "kiT_all", [128, 4224], BF16)
            V1 = sbt(ph3, "V1", [128, 33, 4, 65], BF16)
            x2tb = [sbt(ph3, "x2t%d" % i, [128, D], F32) for i in range(2)]
            xn1 = sbt(ph3, "xn1", [128, D], BF16)
            xnT1 = sbt(ph3, "xnT1", [128, 8, 128], BF16)
            proj = sbt(ph3, "proj", [128, 2120], F32)
            qb = sbt(ph3, "qb", [128, D], BF16)
            kvb = sbt(ph3, "kvb", [128, 512], BF16)
            qib = sbt(ph3, "qib", [128, 512], BF16)
            kib = sbt(ph3, "kib", [128, 64], BF16)
            qTb = [sbt(ph3, "qT%d" % i, [128, 16 * 128], BF16) for i in range(2)]
            qiT = sbt(ph3, "qiT", [128, 8 * 128], BF16)
            rt = [sbt(ph3, "rt%d" % i, [128, 16, 8], F32) for i in range(4)]
            sm = sbt(ph3, "sm", [128, 64], F32)
            smi = sbt(ph3, "smi", [128, 8], I32)
            csall = sbt(ph3, "csall", [128, 33, 16], F32)
            posall = sbt(ph3, "posall", [128, 33], F32)
            yall = sbt(ph3, "yall", [128, 33, 8], F32)
            yalli = sbt(ph3, "yalli", [128, 33, 8], I32)
            yallf = sbt(ph3, "yallf", [128, 33, 8], F32)
            wis = sbt(ph3, "wis", [128, 8], F32)
            scores = sbt(ph3, "scores", [128, 4224], F32)
            mb = sbt(ph3, "mb", [128, 4224], BF16)
            mbT = sbt(ph3, "mbT", [128, 33, 128], BF16)
            rl = [sbt(ph3, "rl%d" % i, [128, 512], F32) for i in range(4)]
            ET = [sbt(ph3, "ET%d" % i, [128, 512], BF16) for i in range(4)]
            mrep = [sbt(ph3, "mrep%d" % i, [128, 512], BF16) for i in range(2)]
            epsC = sbt(ph3, "epsC", [128, 1], F32)
            ob = sbt(ph3, "ob", [128, D], BF16)
            oT = sbt(ph3, "oT", [128, 8, 128], BF16)
            x3t = sbt(ph3, "x3t", [128, D], F32)
            bis = sbt(ph3, "bis", [128, 40], F32)
            rec = sbt(ph3, "rec", [128, 16], F32)

            LD(lambda e: e.dma_start(out=Win[:], in_=Win_b.rearrange("(dk p) n -> p dk n", p=128)), ["Win_b"], ["Win"])
            LD(lambda e: e.dma_start(out=Wo[:], in_=Wo_b.rearrange("(dk p) n -> p dk n", p=128)), ["Wo_b"], ["Wo"])
            LD(lambda e: e.dma_start(out=g1T[:], in_=norm_mix[1, :].rearrange("(dk p) -> p dk", p=128), allow_slow_non_contiguous=True), (), ["g1T"])
            for dk in range(8):
                V(lambda e, dk=dk: e.tensor_scalar_mul(out=Win[:, dk, :], in0=Win[:, dk, :], scalar1=g1T[:, dk:dk + 1]), ["Win", "g1T"], ["Win"])
            G(lambda e: e.memset(epsN[:], EPS), (), ["epsN"])
            LD(lambda e: e.dma_start(out=qg[:], in_=q_norm[0:1, :].partition_broadcast(128)), (), ["qg"])
            LD(lambda e: e.dma_start(out=kg[:], in_=k_norm[0:1, :].partition_broadcast(128)), (), ["kg"])
            V(lambda e: e.tensor_scalar_mul(out=proj[:, 0:128], in0=identf[:], scalar1=65536.0), ["identf"], ["proj"])
            V(lambda e: e.tensor_copy(out=identL[:], in_=proj[:, 0:128]), ["proj"], ["identL"])
            G(lambda e: e.iota(iop[:], pattern=[[0, 1]], base=0, channel_multiplier=1, allow_small_or_imprecise_dtypes=True), (), ["iop"])
            for k in range(30):
                V(lambda e, k=k: e.memset(pw2[:, k:k + 1], 2.0 ** (-(k + 1))), (), ["pw2"])
            G(lambda e: e.memset(negC[:], -8.0), (), ["negC"])
            V(lambda e: e.memset(kT_all[64:128, :, :], 0.0), (), ["kT_all"])
            V(lambda e: e.memset(kiT_all[64:128, :], 0.0), (), ["kiT_all"])
            V(lambda e: e.memset(qiT[64:128, :], 0.0), (), ["qiT"])
            for i in range(2):
                V(lambda e, i=i: e.memset(qTb[i][64:128, :], 0.0), (), ["qT%d" % i])
            G(lambda e: e.memset(epsC[:], 1e-30), (), ["epsC"])
            G(lambda e: e.memset(V1[:, :, :, 64:65], 1.0), (), ["V1"])

            ringb = [psF[3], psF[4], psF[5], psB[1][:, :].bitcast(F32)]
            ringn = ["psF3", "psF4", "psF5", "psB1"]
            ringi = [0]

            def ring():
                i = ringi[0] % 4
                ringi[0] += 1
                return ringb[i], ringn[i]

            SCN = ["sc%d" % i for i in range(9)]
            sqb = scores[:, 0:1024]
            SQN = ["sc0", "sc1"]
            qn = scores[:, 1024:2048]
            QNN = ["sc2", "sc3"]

            def sc_names(c0, c1):
                return SCN[c0 // 512:(c1 + 511) // 512]

            G(lambda e: e.iota(posall[:], pattern=[[128, 33]], base=0, channel_multiplier=1, allow_small_or_imprecise_dtypes=True), (), ["posall"])
            V(lambda e: e.tensor_scalar(out=posall[:, 16:32], in0=posall[:, 16:32], scalar1=flg[:, 2:3], scalar2=-2048.0, op0=ALU.add, op1=ALU.add),
              ["posall", "flg"], ["posall"])
            for i in range(8):
                V(lambda e, i=i: e.tensor_scalar_mul(out=yall[:, :, i], in0=posall[:], scalar1=(500000.0 ** (-i / 8.0)) / TWO_PI), ["posall"], ["yall"])
            V(lambda e: e.tensor_copy(out=yalli[:], in_=yall[:]), ["yall"], ["yalli"])
            V(lambda e: e.tensor_copy(out=yallf[:], in_=yalli[:]), ["yalli"], ["yallf"])
            V(lambda e: e.tensor_sub(out=yall[:], in0=yall[:], in1=yallf[:]), ["yall", "yallf"], ["yall"])
            A(lambda e: e.activation(out=csall[:, :, 8:16], in_=yall[:], func=AF.Sin, scale=SIN_SCALE), ["yall"], ["csall"])
            V(lambda e: e.tensor_scalar_add(out=yall[:], in0=yall[:], scalar1=0.25), ["yall", "csall"], ["yall"])
            V(lambda e: e.tensor_single_scalar(out=yallf[:], in_=yall[:], scalar=0.5, op=ALU.is_gt), ["yall"], ["yallf"])
            V(lambda e: e.tensor_sub(out=yall[:], in0=yall[:], in1=yallf[:]), ["yall", "yallf"], ["yall"])
            A(lambda e: e.activation(out=csall[:, :, 0:8], in_=yall[:], func=AF.Sin, scale=SIN_SCALE), ["yall"], ["csall"])

            def rope_tables(pp, base, use_flag):
                pass

            def rope(pp, t3, H, nms, ti):
                cosb = csall[:pp, ti, 0:8].rearrange("p (o f) -> p o f", o=1).to_broadcast([pp, H, 8])
                sinb = csall[:pp, ti, 8:16].rearrange("p (o f) -> p o f", o=1).to_broadcast([pp, H, 8])
                x1 = t3[:, :, 0:8]
                x2 = t3[:, :, 8:16]
                a_, b_, c_, d_ = [r[:pp, :H, :] for r in rt]
                V(lambda e: e.tensor_mul(out=a_, in0=x1, in1=cosb), nms + ["csall"], ["rt0"])
                V(lambda e: e.tensor_mul(out=b_, in0=x2, in1=sinb), nms + ["csall"], ["rt1"])
                V(lambda e: e.tensor_mul(out=c_, in0=x2, in1=cosb), nms + ["csall"], ["rt2"])
                V(lambda e: e.tensor_mul(out=d_, in0=x1, in1=sinb), nms + ["csall"], ["rt3"])
                V(lambda e: e.tensor_sub(out=x1, in0=a_, in1=b_), ["rt0", "rt1"], nms)
                V(lambda e: e.tensor_add(out=x2, in0=c_, in1=d_), ["rt2", "rt3"], nms)

            def head_norm(pp, src2d, dst2d, H, gain, nm_src, nm_dst, rstd, rn):
                V(lambda e: e.tensor_mul(out=dst2d.rearrange("p (h d) -> p h d", d=64), in0=src2d.rearrange("p (h d) -> p h d", d=64),
                                         in1=bc_last(rstd, pp, H, 64)), nm_src + [rn], nm_dst)
                V(lambda e: e.tensor_mul(out=dst2d.rearrange("p (h d) -> p h d", d=64), in0=dst2d.rearrange("p (h d) -> p h d", d=64),
                                         in1=gain[:pp, :].rearrange("p (o d) -> p o d", o=1).to_broadcast([pp, H, 64])),
                  nm_dst + ["qg", "kg"], nm_dst)

            def store_keys(pp, kb, kv_src_bf, ki_src_bf, nm_kv, nm_ki):
                c0 = kb * 128
                for kv in range(4):
                    T(lambda e, kv=kv: e.transpose(out=psB[0][0:64, kv * pp:(kv + 1) * pp], in_=kv_src_bf[:, kv * 64:(kv + 1) * 64],
                                                   identity=identb[:pp, :pp]), [nm_kv, "identb"], ["psB0"])
                T(lambda e: e.transpose(out=psB[0][0:64, 4 * pp:5 * pp], in_=ki_src_bf, identity=identb[:pp, :pp]), [nm_ki, "identb"], ["psB0"])
                A(lambda e: e.copy(out=kT_all[0:64, :, c0:c0 + pp], in_=psB[0][0:64, 0:4 * pp].rearrange("p (k n) -> p k n", n=pp)),
                  ["psB0"], ["kT_all"])
                A(lambda e: e.copy(out=kiT_all[0:64, c0:c0 + pp], in_=psB[0][0:64, 4 * pp:5 * pp]), ["psB0"], ["kiT_all"])
                G(lambda e: e.tensor_copy(out=V1[:pp, kb, :, 0:64], in_=kv_src_bf[:, 256:512].rearrange("p (k d) -> p k d", d=64)),
                  [nm_kv], ["V1"])

            def emitE(t, buf):
                pp, xsrc_ap, full = t["pp"], t["xsrc"], t["full"]
                x2t = x2tb[buf]
                x2n = "x2t%d" % buf
                LD(lambda e: e.dma_start(out=x2t[:pp, :], in_=xsrc_ap), ["Xdst"], [x2n])
                A(lambda e: e.activation(out=ob[:pp, :], in_=x2t[:pp, :], func=AF.Square, accum_out=est[:pp, 0:1]), [x2n], ["ob", "est"])
                A(lambda e: e.activation(out=est[:pp, 0:1], in_=est[:pp, 0:1], func=AF.Ln, scale=1.0 / D, bias=epsN[:pp, 0:1]), ["est", "epsN"], ["est"])
                A(lambda e: e.activation(out=est[:pp, 0:1], in_=est[:pp, 0:1], func=AF.Exp, scale=-0.5), ["est"], ["est"])
                A(lambda e: e.activation(out=xn1[:pp, :], in_=x2t[:pp, :], func=AF.Copy, scale=est[:pp, 0:1]), [x2n, "est"], ["xn1"])
                for dk in range(8):
                    T(lambda e, dk=dk: e.transpose(out=psB[0][:, dk * pp:(dk + 1) * pp], in_=xn1[:pp, dk * 128:(dk + 1) * 128],
                                                   identity=identb[:pp, :pp]), ["xn1", "identb"], ["psB0"])
                A(lambda e: e.copy(out=xnT1[:, :, :pp], in_=psB[0][:, 0:8 * pp].rearrange("p (k n) -> p k n", n=pp)), ["psB0"], ["xnT1"])
                ko, kio, pn = t.get("ko", 1024), t.get("kio", 2048), t.get("pn", "proj")
                blocks = [(1024, 512, ko), (2048, 72, kio)]
                if full:
                    blocks = [(0, 512, 0), (512, 512, 512), (1024, 512, 1024), (1536, 512, 1536), (2048, 72, 2048)]
                for (c0, w, d0) in blocks:
                    bank, bn = ring()
                    for dk in range(8):
                        T(lambda e, dk=dk, c0=c0, w=w, bank=bank: e.matmul(bank[:pp, :w], lhsT=xnT1[:, dk, :pp], rhs=Win[:, dk, c0:c0 + w],
                                                                           start=(dk == 0), stop=(dk == 7)), ["xnT1", "Win"], [bn])
                    A(lambda e, d0=d0, w=w, bank=bank: e.copy(out=proj[:pp, d0:d0 + w], in_=bank[:pp, :w]), [bn], [pn])
                hst = hstb[buf]
                hn = "hst%d" % buf
                nst = 20 if full else 4
                for j in range(nst):
                    c0 = (ko + j * 64) if j < 4 else ((j - 4) * 64)
                    A(lambda e, j=j, c0=c0: e.activation(out=ob[:pp, 0:64], in_=proj[:pp, c0:c0 + 64], func=AF.Square, accum_out=hst[:pp, j:j + 1]),
                      [pn], ["ob", hn])
                A(lambda e: e.activation(out=hst[:pp, 0:nst], in_=hst[:pp, 0:nst], func=AF.Ln, scale=1.0 / 64, bias=epsN[:pp, 0:1]), [hn, "epsN"], [hn])
                A(lambda e: e.activation(out=hst[:pp, 0:nst], in_=hst[:pp, 0:nst], func=AF.Exp, scale=-0.5), [hn], [hn])

            def genA(t, buf):
                pp, xsrc_ap, full, pos_base, use_flag, kb, outs = t["pp"], t["xsrc"], t["full"], t["pos_base"], t["use_flag"], t["kb"], t["outs"]
                x2t = x2tb[buf]
                x2n = "x2t%d" % buf
                qT = qTb[buf]
                qTn = "qT%d" % buf
                ko, kio, pn = t.get("ko", 1024), t.get("kio", 2048), t.get("pn", "proj")
                rope_tables(pp, pos_base, use_flag)
                head_norm(pp, proj[:pp, ko:ko + 256], proj[:pp, ko:ko + 256], 4, kg, [pn], [pn], hstb[buf][:pp, 0:4], "hst%d" % buf)
                rope(pp, proj[:pp, ko:ko + 256].rearrange("p (h d) -> p h d", d=64), 4, [pn], kb)
                rope(pp, proj[:pp, kio:kio + 64].rearrange("p (h d) -> p h d", d=64), 1, [pn], kb)
                A(lambda e: e.copy(out=kvb[:pp, :], in_=proj[:pp, ko:ko + 512]), [pn], ["kvb"])
                A(lambda e: e.copy(out=kib[:pp, :], in_=proj[:pp, kio:kio + 64]), [pn], ["kib"])
                yield 2
                store_keys(pp, kb, kvb[:pp, :], kib[:pp, :], "kvb", "kib")
                if outs is not None:
                    ok_, ov_, oki_ = outs
                    ST(lambda e: e.dma_start(out=ok_, in_=proj[:pp, ko:ko + 256]), [pn], ["o_k"])
                    ST(lambda e: e.dma_start(out=ov_, in_=proj[:pp, ko + 256:ko + 512]), [pn], ["o_v"])
                    ST(lambda e: e.dma_start(out=oki_, in_=proj[:pp, kio:kio + 64]), [pn], ["o_ki"])
                if not full:
                    return
                head_norm(pp, proj[:pp, 0:1024], qn[:pp, :], 16, qg, ["proj"], QNN, hstb[buf][:pp, 4:20], "hst%d" % buf)
                rope(pp, qn[:pp, :].rearrange("p (h d) -> p h d", d=64), 16, QNN, kb)
                rope(pp, proj[:pp, 1536:2048].rearrange("p (h d) -> p h d", d=64), 8, ["proj"], kb)
                A(lambda e: e.copy(out=qb[:pp, :], in_=qn[:pp, :]), QNN, ["qb"])
                A(lambda e: e.copy(out=qib[:pp, :], in_=proj[:pp, 1536:2048]), ["proj"], ["qib"])
                V(lambda e: e.tensor_scalar_mul(out=wis[:pp, :], in0=proj[:pp, 2112:2120], scalar1=(64.0 ** -0.5) * (8.0 ** -0.5)),
                  ["proj"], ["wis"])
                yield 3
                for h8 in range(2):
                    for hh in range(8):
                        h = h8 * 8 + hh
                        T(lambda e, h=h, hh=hh: e.transpose(out=psB[0][0:64, hh * pp:(hh + 1) * pp], in_=qb[:pp, h * 64:(h + 1) * 64],
                                                            identity=identb[:pp, :pp]), ["qb", "identb"], ["psB0"])
                    A(lambda e, h8=h8: e.copy(out=qT[0:64, h8 * 8 * pp:(h8 + 1) * 8 * pp], in_=psB[0][0:64, 0:8 * pp]), ["psB0"], [qTn])
                    yield 1
                for hh in range(8):
                    T(lambda e, hh=hh: e.transpose(out=psB[0][0:64, hh * pp:(hh + 1) * pp], in_=qib[:pp, hh * 64:(hh + 1) * 64],
                                                   identity=identb[:pp, :pp]), ["qib", "identb"], ["psB0"])
                A(lambda e: e.copy(out=qiT[0:64, 0:8 * pp], in_=psB[0][0:64, 0:8 * pp]), ["psB0"], ["qiT"])
                yield 1
                nkb, last_kw = t["nkb"], t["last_kw"]
                NK = (nkb - 1) * 128 + last_kw
                rli = 0
                for bi, c0 in enumerate(range(0, NK, 512)):
                    w = min(512, NK - c0)
                    on_pool = False
                    scn = [SCN[c0 // 512]]
                    for h in range(8):
                        bank, bn = ring()
                        r_ = rl[rli % 4]
                        rn = "rl%d" % (rli % 4)
                        rli += 1
                        T(lambda e, h=h, c0=c0, w=w, bank=bank: e.matmul(bank[:pp, :w], lhsT=qiT[:, h * pp:(h + 1) * pp], rhs=kiT_all[:, c0:c0 + w],
                                                                         start=True, stop=True), ["qiT", "kiT_all"], [bn])
                        A(lambda e, w=w, bank=bank, r_=r_: e.activation(out=r_[:pp, :w], in_=bank[:pp, :w], func=AF.Relu), [bn], [rn])
                        if on_pool:
                            if h == 0:
                                G(lambda e, c0=c0, w=w, r_=r_: e.tensor_scalar_mul(out=scores[:pp, c0:c0 + w], in0=r_[:pp, :w], scalar1=wis[:pp, 0:1]),
                                  [rn, "wis"], scn)
                            else:
                                G(lambda e, h=h, w=w, r_=r_: e.tensor_scalar_mul(out=r_[:pp, :w], in0=r_[:pp, :w], scalar1=wis[:pp, h:h + 1]),
                                  [rn, "wis"], [rn])
                                G(lambda e, c0=c0, w=w, r_=r_: e.tensor_add(out=scores[:pp, c0:c0 + w], in0=scores[:pp, c0:c0 + w], in1=r_[:pp, :w]),
                                  [rn] + scn, scn)
                        elif h == 0:
                            A(lambda e, c0=c0, w=w, r_=r_: e.mul(out=scores[:pp, c0:c0 + w], in_=r_[:pp, :w], mul=wis[:pp, 0:1]),
                              [rn, "wis"], scn)
                        else:
                            V(lambda e, h=h, c0=c0, w=w, r_=r_: e.scalar_tensor_tensor(
                                out=scores[:pp, c0:c0 + w], in0=r_[:pp, :w], scalar=wis[:pp, h:h + 1], in1=scores[:pp, c0:c0 + w],
                                op0=ALU.mult, op1=ALU.add), [rn, "wis"] + scn, scn)
                        if h % 4 == 3:
                            yield 1
                alln = sc_names(0, NK)
                V(lambda e: e.tensor_reduce(out=bis[:pp, 5:6], in_=scores[:pp, :NK], axis=AX.X, op=ALU.min), alln, ["bis"])
                if t["prefix_cols"]:
                    pc = t["prefix_cols"]
                    V(lambda e: e.tensor_scalar_add(out=scores[:pp, 0:pc], in0=scores[:pp, 0:pc], scalar1=flg[:pp, 1:2]),
                      sc_names(0, pc) + ["flg"], sc_names(0, pc))
                if t["diag"]:
                    V(lambda e: e.memset(scores[0:64, NK - 64:NK], -1e30), sc_names(NK - 64, NK), sc_names(NK - 64, NK))
                lo = bis[:pp, 0:1]
                mid = bis[:pp, 1:2]
                cnt = bis[:pp, 2:3]
                ge = bis[:pp, 3:4]
                w0 = bis[:pp, 4:5]
                HW = bis[:pp, 8:38]
                V(lambda e: e.reduce_max(out=w0, in_=scores[:pp, :NK], axis=AX.X), alln, ["bis"])
                V(lambda e: e.tensor_scalar_add(out=lo, in0=bis[:pp, 5:6], scalar1=-1.0), ["bis"], ["bis"])
                V(lambda e: e.tensor_sub(out=w0, in0=w0, in1=lo), ["bis"], ["bis"])
                V(lambda e: e.tensor_scalar_mul(out=HW, in0=pw2[:pp, :], scalar1=w0), ["bis", "pw2"], ["bis"])
                V(lambda e: e.tensor_add(out=mid, in0=lo, in1=bis[:pp, 8:9]), ["bis"], ["bis"])
                for k in range(NBIS):
                    V(lambda e: e.tensor_scalar(out=mb[:pp, :NK], in0=scores[:pp, :NK], scalar1=mid, scalar2=0.0, op0=ALU.is_ge, op1=ALU.add,
                                                accum_out=cnt), alln + ["bis"], ["mb", "bis"])
                    V(lambda e: e.tensor_scalar(out=ge, in0=cnt, scalar1=256.0, scalar2=0.5, op0=ALU.is_ge, op1=ALU.subtract), ["bis"], ["bis"])
                    V(lambda e, k=k: e.scalar_tensor_tensor(out=mid, in0=ge, scalar=bis[:pp, 8 + k:9 + k], in1=mid, op0=ALU.mult, op1=ALU.add),
                      ["bis"], ["bis"])
                V(lambda e: e.scalar_tensor_tensor(out=lo, in0=bis[:pp, 8 + NBIS - 1:9 + NBIS - 1], scalar=-0.5, in1=mid, op0=ALU.mult, op1=ALU.add),
                  ["bis"], ["bis"])
                V(lambda e: e.tensor_scalar(out=mb[:pp, :NK], in0=scores[:pp, :NK], scalar1=lo, scalar2=1.0, op0=ALU.is_ge, op1=ALU.subtract),
                  alln + ["bis"], ["mb"])

            def att_A2b(t):
                pp, nkb, last_kw = t["pp"], t["nkb"], t["last_kw"]
                for k8 in range(0, nkb, 8):
                    nb_ = min(8, nkb - k8)
                    for i in range(nb_):
                        kb = k8 + i
                        kw = last_kw if kb == nkb - 1 else 128
                        T(lambda e, kb=kb, kw=kw, i=i: e.transpose(out=psB[0][:kw, i * pp:(i + 1) * pp], in_=mb[:pp, kb * 128:kb * 128 + kw],
                                                                   identity=identb[:pp, :pp]), ["mb", "identb"], ["psB0"])
                    A(lambda e, k8=k8, nb_=nb_: e.copy(out=mbT[:, k8:k8 + nb_, :pp],
                                                       in_=psB[0][:, 0:nb_ * pp].rearrange("p (k n) -> p k n", n=pp)), ["psB0"], ["mbT"])

            def ohead(h):
                return psF[h // 7], "psF%d" % (h // 7), (h % 7) * 65

            def att_B(t, buf, gen):
                pp, nkb, last_kw, dst_ap = t["pp"], t["nkb"], t["last_kw"], t["dst"]
                x2t = x2tb[buf]
                x2n = "x2t%d" % buf
                qT = qTb[buf]
                qTn = "qT%d" % buf
                countdown = [1]

                def hook():
                    if gen[0] is None:
                        return
                    countdown[0] -= 1
                    if countdown[0] <= 0:
                        try:
                            countdown[0] = next(gen[0])
                        except StopIteration:
                            gen[0] = None
                iters = [(kb, kv) for kb in range(nkb) for kv in range(4)]
                slots = {}

                def emit_qk(it):
                    kb, kv = iters[it]
                    kw = last_kw if kb == nkb - 1 else 128
                    mr = mrep[kb % 2]
                    mrn = "mrep%d" % (kb % 2)
                    if kv == 0:
                        G(lambda e, kb=kb, kw=kw, mr=mr: e.tensor_copy(
                            out=mr[:kw, :4 * pp].rearrange("p (h t) -> p h t", h=4),
                            in_=mbT[:kw, kb, :pp].rearrange("p (o t) -> p o t", o=1).to_broadcast([kw, 4, pp])), ["mbT"], [mrn])
                    bank, bn = ring()
                    et = ET[it % 4]
                    en = "ET%d" % (it % 4)
                    slots[it] = (et, en)
                    T(lambda e, kb=kb, kw=kw, kv=kv, bank=bank: e.matmul(
                        bank[:kw, :4 * pp], lhsT=kT_all[:, kv, kb * 128:kb * 128 + kw], rhs=qT[:, kv * 4 * pp:(kv + 1) * 4 * pp],
                        start=True, stop=False), ["kT_all", qTn], [bn])
                    T(lambda e, kw=kw, bank=bank, mr=mr: e.matmul(bank[:kw, :4 * pp], lhsT=identL[:kw, :kw], rhs=mr[:kw, :4 * pp],
                                                                  start=False, stop=True), ["identL", mrn], [bn])
                    A(lambda e, kw=kw, bank=bank, et=et: e.activation(out=et[:kw, :4 * pp], in_=bank[:kw, :4 * pp], func=AF.Exp,
                                                                      scale=0.125, bias=negC[:kw, 0:1]), [bn, "negC"], [en])

                def emit_pv(it):
                    kb, kv = iters[it]
                    kw = last_kw if kb == nkb - 1 else 128
                    et, en = slots.pop(it)
                    for hh in range(4):
                        ob_, obn, oc = ohead(kv * 4 + hh)
                        T(lambda e, kb=kb, kw=kw, kv=kv, hh=hh, et=et, ob_=ob_, oc=oc: e.matmul(
                            ob_[:pp, oc:oc + 65], lhsT=et[:kw, hh * pp:(hh + 1) * pp], rhs=V1[:kw, kb, kv, :],
                            start=(kb == 0), stop=(kb == nkb - 1)), [en, "V1"], [obn])

                SKEW = 3
                for it in range(min(SKEW, len(iters))):
                    emit_qk(it)
                for it in range(len(iters)):
                    if it + SKEW < len(iters):
                        emit_qk(it + SKEW)
                    emit_pv(it)
                    hook()
                for hb, nh in ((0, 7), (1, 7), (2, 2)):
                    A(lambda e, hb=hb, nh=nh: e.activation(out=rec[:pp, hb * 7:hb * 7 + nh],
                                                           in_=psF[hb][:pp, 0:nh * 65].rearrange("p (h c) -> p h c", c=65)[:, :, 64],
                                                           func=AF.Ln, bias=epsC[:pp, 0:1]), ["psF%d" % hb, "epsC"], ["rec"])
                A(lambda e: e.activation(out=rec[:pp, 0:16], in_=rec[:pp, 0:16], func=AF.Exp, scale=-1.0), ["rec"], ["rec"])
                for h in range(16):
                    ob_, obn, oc = ohead(h)
                    A(lambda e, h=h, ob_=ob_, oc=oc: e.activation(out=ob[:pp, h * 64:(h + 1) * 64], in_=ob_[:pp, oc:oc + 64],
                                                                  func=AF.Copy, scale=rec[:pp, h:h + 1]), [obn, "rec"], ["ob"])
                for dk in range(8):
                    T(lambda e, dk=dk: e.transpose(out=psB[0][:, dk * pp:(dk + 1) * pp], in_=ob[:pp, dk * 128:(dk + 1) * 128],
                                                   identity=identb[:pp, :pp]), ["ob", "identb"], ["psB0"])
                A(lambda e: e.copy(out=oT[:, :, :pp], in_=psB[0][:, 0:8 * pp].rearrange("p (k n) -> p k n", n=pp)), ["psB0"], ["oT"])
                for nb in range(2):
                    bank, bn = ring()
                    for dk in range(8):
                        T(lambda e, dk=dk, nb=nb, bank=bank: e.matmul(bank[:pp, :], lhsT=oT[:, dk, :pp], rhs=Wo[:, dk, nb * 512:(nb + 1) * 512],
                                                                      start=(dk == 0), stop=(dk == 7)), ["oT", "Wo"], [bn])
                    A(lambda e, nb=nb, bank=bank: e.copy(out=x3t[:pp, nb * 512:(nb + 1) * 512], in_=bank[:pp, :]), [bn], ["x3t"])
                    G(lambda e, nb=nb: e.tensor_add(out=x3t[:pp, nb * 512:(nb + 1) * 512], in0=x3t[:pp, nb * 512:(nb + 1) * 512],
                                                    in1=x2t[:pp, nb * 512:(nb + 1) * 512]), ["x3t", x2n], ["x3t"])
                ST(lambda e: e.dma_start(out=dst_ap, in_=x3t[:pp, :]), ["x3t"], ["X3"])

            def drain(g):
                if g[0] is not None:
                    for _ in g[0]:
                        pass
                    g[0] = None

            def run_pipeline(tiles):
                n = len(tiles)
                emitE(tiles[0], 0)
                g = [genA(tiles[0], 0)]
                drain(g)
                att_A2b(tiles[0])
                if n > 1:
                    emitE(tiles[1], 1)
                for i in range(n):
                    g = [genA(tiles[i + 1], (i + 1) % 2)] if i + 1 < n else [None]
                    att_B(tiles[i], i % 2, g)
                    drain(g)
                    if i + 2 < n:
                        emitE(tiles[i + 2], i % 2)
                    if i + 1 < n:
                        att_A2b(tiles[i + 1])

            def ktile(kt):
                alt = (kt % 2 == 1)
                return dict(pp=128, xsrc=X2p[kt * 128:(kt + 1) * 128, :], full=False, pos_base=kt * 128, use_flag=False, kb=kt, outs=None,
                            ko=(0 if alt else 1024), kio=(512 if alt else 2048), pn=("projB" if alt else "proj"))
            V(lambda e: e.memset(sm[:, 62:63], 0.0), (), ["proj", "projB", "sm"])
            emitE(ktile(0), 0)
            for kt in range(15):
                if kt + 1 < 15:
                    emitE(ktile(kt + 1), (kt + 1) % 2)
                g = [genA(ktile(kt), kt % 2)]
                drain(g)
            V(lambda e: e.memset(sm[:, 62:63], 0.0), ["projB"], ["proj", "sm"])
            tiles = []
            for kt in range(15, 32):
                r0 = kt * 128
                if kt == 15:
                    tiles.append(dict(pp=128, xsrc=X2p[r0:r0 + 128, :], full=True, pos_base=r0, use_flag=False, kb=kt, outs=None,
                                      nkb=kt + 1, last_kw=128, prefix_cols=2048, diag=True, dst=X3p[0:128, :]))
                else:
                    o0 = (kt - 16) * 128
                    outs = (o_k[o0:o0 + 128, :], o_v[o0:o0 + 128, :], o_ki[o0:o0 + 128, :])
                    tiles.append(dict(pp=128, xsrc=X2p[r0:r0 + 128, :], full=True, pos_base=o0, use_flag=True, kb=kt, outs=outs,
                                      nkb=kt + 1, last_kw=128, prefix_cols=2048, diag=True, dst=X3p[128 + o0:128 + o0 + 128, :]))
            run_pipeline(tiles)
            V(lambda e: e.memset(sm[:, 61:62], 0.0), (), ["proj", "projS0", "projS1", "projS2", "sm"])
            for kb in range(32):
                r0 = kb * 128
                o = (kb % 3) * 576
                pn = "projS%d" % (kb % 3)
                cb_, cbn = (kvb, "kvb") if kb % 2 == 0 else (qb, "qb")
                ib_, ibn = (kib, "kib") if kb % 2 == 0 else (qib, "qib")
                LD(lambda e, r0=r0, o=o: e.dma_start(out=proj[:, o:o + 256], in_=ck[r0:r0 + 128, :]), (), [pn])
                LD(lambda e, r0=r0, o=o: e.dma_start(out=proj[:, o + 256:o + 512], in_=cv[r0:r0 + 128, :]), (), [pn])
                LD(lambda e, r0=r0, o=o: e.dma_start(out=proj[:, o + 512:o + 576], in_=cki[r0:r0 + 128, :]), (), [pn])
                V(lambda e, o=o, cb_=cb_: e.tensor_copy(out=cb_[:, 0:512], in_=proj[:, o:o + 512]), [pn], [cbn])
                V(lambda e, o=o, ib_=ib_: e.tensor_copy(out=ib_[:, 0:64], in_=proj[:, o + 512:o + 576]), [pn], [ibn])
                store_keys(128, kb, cb_[:, 0:512], ib_[:, 0:64], cbn, ibn)
            V(lambda e: e.memset(sm[:, 60:61], 0.0), ["projS0", "projS1", "projS2"], ["proj", "sm"])
            run_pipeline([dict(pp=32, xsrc=X2s, full=True, pos_base=4096, use_flag=False, kb=32, outs=(o_ks, o_vs, o_kis),
                               nkb=33, last_kw=32, prefix_cols=0, diag=False, dst=X3s)])
        P.barrier(tiny)
        ffn_phase(1, [(X3p[0:128, :], None, None, 128, 128, 0, None, None),
                      (X3p[128:2176, :], None, y_own, 2048, 128, 0, "keep", o_conv_p[1]),
                      (X3s, None, y_s, 32, 32, 0, cconv[1], o_conv_s[1])], False)
        P.barrier(tiny)
        dbgd("X2p", X2p[2048:4096, :], [2048, D], F32, ["Xdst"])
        dbgd("X2s", X2s, [32, D], F32, ["Xdst"])
        P.emit()
    return nc


_NC = None


def _get_nc():
    global _NC
    if _NC is None:
        _NC = build()
    return _NC


def kernel(x_prompt, x_sample, state_ssm_re, state_ssm_im, cache_k, cache_v, cache_kidx, cache_conv,
           norm_mix, norm_ffn, ssm_lambda_re, ssm_lambda_im, ssm_log_dt, ssm_b_re, ssm_b_im,
           ssm_c_re, ssm_c_im, ssm_d, ssm_w_glu, ssm_b_glu, attn_w_in, attn_q_norm, attn_k_norm,
           attn_w_o, ffn_w_up, ffn_conv_w, ffn_conv_b, ffn_w_down):
    f = lambda a: np.ascontiguousarray(np.asarray(a, dtype=np.float32))
    x_prompt = f(x_prompt)
    nc = _get_nc()
    shared = {
        "norm_mix": f(norm_mix), "norm_ffn": f(norm_ffn), "lam_re": f(ssm_lambda_re)[0], "lam_im": f(ssm_lambda_im)[0],
        "log_dt": f(ssm_log_dt), "b_re": f(ssm_b_re)[0], "b_im": f(ssm_b_im)[0], "c_re": f(ssm_c_re)[0], "c_im": f(ssm_c_im)[0],
        "ssm_d": f(ssm_d), "w_glu": f(ssm_w_glu)[0], "b_glu": f(ssm_b_glu), "w_in": f(attn_w_in)[0], "q_norm": f(attn_q_norm),
        "k_norm": f(attn_k_norm), "w_o": f(attn_w_o)[0], "w_up": f(ffn_w_up), "conv_w": f(ffn_conv_w), "conv_b": f(ffn_conv_b),
        "w_down": f(ffn_w_down),
    }
    in_maps = []
    for c in range(8):
        b, h = c // 2, c % 2
        xpc = np.zeros((4096, D), np.float32)
        if h == 1:
            xpc[:2048] = x_prompt[b, :2048]
        xpc[2048:] = x_prompt[b, 2048 * h:2048 * (h + 1)]
        fl = np.zeros((128, 4), np.float32)
        fl[:, 0] = float(h)
        fl[:, 1] = 0.0 if h == 1 else -1e30
        fl[:, 2] = 2048.0 * h
        m = dict(shared)
        m.update({
            "xp": xpc, "xs": f(x_sample)[c], "flag": fl,
            "st_re": f(state_ssm_re)[0, c], "st_im": f(state_ssm_im)[0, c],
            "ck": f(cache_k)[0, c].reshape(4096, 256), "cv": f(cache_v)[0, c].reshape(4096, 256),
            "cki": f(cache_kidx)[0, c], "cconv": f(cache_conv)[:, c],
        })
        in_maps.append(m)
    res = run_bass_kernel_spmd(nc, in_maps, core_ids=list(range(8))).results
    global _LAST
    _LAST = res
    y_p = np.zeros((4, 4096, D), np.float32)
    k_p = np.zeros((1, 4, 4096, 4, 64), np.float32)
    v_p = np.zeros((1, 4, 4096, 4, 64), np.float32)
    ki_p = np.zeros((1, 4, 4096, 64), np.float32)
    for c in range(8):
        b, h = c // 2, c % 2
        sl = slice(2048 * h, 2048 * (h + 1))
        y_p[b, sl] = res[c]["y_own"]
        k_p[0, b, sl] = res[c]["o_k"].reshape(2048, 4, 64)
        v_p[0, b, sl] = res[c]["o_v"].reshape(2048, 4, 64)
        ki_p[0, b, sl] = res[c]["o_ki"]
    y_s = np.stack([res[c]["y_s"] for c in range(8)])
    ssm_re_p = np.stack([res[2 * b + 1]["o_ssm_p"][0] for b in range(4)])[None]
    ssm_im_p = np.stack([res[2 * b + 1]["o_ssm_p"][1] for b in range(4)])[None]
    ssm_re_s = np.stack([res[c]["o_ssm_s"][0] for c in range(8)])[None]
    ssm_im_s = np.stack([res[c]["o_ssm_s"][1] for c in range(8)])[None]
    k_s = np.stack([res[c]["o_ks"].reshape(32, 4, 64) for c in range(8)])[None]
    v_s = np.stack([res[c]["o_vs"].reshape(32, 4, 64) for c in range(8)])[None]
    ki_s = np.stack([res[c]["o_kis"] for c in range(8)])[None]
    conv_p = np.stack([res[2 * b + 1]["o_conv_p"] for b in range(4)], axis=1)
    conv_s = np.stack([res[c]["o_conv_s"] for c in range(8)], axis=1)
    return (y_p, y_s, ssm_re_p, ssm_im_p, ssm_re_s, ssm_im_s, k_p, v_p, ki_p, k_s, v_s, ki_s, conv_p, conv_s)
```

```python
import contextlib
import math
import numpy as np
import concourse.bass as bass
import concourse.mybir as mybir
from concourse.bass_utils import run_bass_kernel_spmd

F32 = mybir.dt.float32
BF16 = mybir.dt.bfloat16
I32 = mybir.dt.int32
AF = mybir.ActivationFunctionType
ALU = mybir.AluOpType
AX = mybir.AxisListType

NDS = 24
DEBUG = False
D = 1024
DFF = 2816
NFC = 22
NBIS = 16
EPS = 1e-6
TWO_PI = 2.0 * math.pi
SIN_SCALE = TWO_PI * (1.0 - 1e-6)


class Buf:
    __slots__ = ("name", "w", "r")

    def __init__(self, name):
        self.name = name
        self.w = None
        self.r = []


class Op:
    __slots__ = ("eng", "fn", "deps", "sig", "cnt", "idx", "dma", "dsem", "dval")


class Prog:
    ENG = ["pe", "act", "dve", "pool", "sp"]

    def __init__(self, nc):
        self.nc = nc
        self.q = {e: [] for e in self.ENG}
        self.ndma_sp = 0
        self.ndma_pool = 0
        self.dma_last = [None] * NDS
        self.dma_cnt = [0] * NDS
        self.bufs = {}

    def _B(self, x):
        if isinstance(x, Buf):
            return x
        b = self.bufs.get(x)
        if b is None:
            b = Buf(x)
            self.bufs[x] = b
        return b

    def op(self, eng, fn, reads=(), writes=(), dma=False, extra=()):
        o = Op()
        o.eng = eng
        o.fn = fn
        o.sig = False
        o.cnt = None
        o.dma = dma
        o.idx = len(self.q[eng])
        deps = {}
        reads = [self._B(b) for b in reads]
        writes = [self._B(b) for b in writes]
        for b in reads:
            if b.w is not None:
                deps[id(b.w)] = (b.w, True)
        for b in writes:
            if b.w is not None and id(b.w) not in deps:
                deps[id(b.w)] = (b.w, False)
            for r in b.r:
                if id(r) not in deps:
                    deps[id(r)] = (r, False)
        for d in extra:
            if d is not None and id(d) not in deps:
                deps[id(d)] = (d, True)
        if dma:
            half = NDS // 2
            if eng == "sp":
                j = self.ndma_sp % half
                self.ndma_sp += 1
            else:
                j = half + (self.ndma_pool % half)
                self.ndma_pool += 1
            prev = self.dma_last[j]
            if prev is not None:
                deps[id(prev)] = (prev, False)
            self.dma_cnt[j] += 16
            o.dsem = j
            o.dval = self.dma_cnt[j]
            self.dma_last[j] = o
        o.deps = []
        for d, raw in deps.values():
            if d is o:
                continue
            if d.dma:
                o.deps.append(d)
            elif d.eng != eng:
                d.sig = True
                o.deps.append(d)
            else:
                if raw and (o.idx - d.idx) <= 2 and eng != "pe":
                    d.sig = True
                    o.deps.append(d)
        for b in reads:
            b.r.append(o)
        for b in writes:
            b.w = o
            b.r = []
        self.q[eng].append(o)
        return o

    def barrier(self, tiny):
        firsts = []
        alld = [d for d in self.dma_last if d is not None]
        for e in self.ENG:
            firsts.append(self.op(e, tiny[e], reads=(), writes=["_barA_" + e], dma=(e == "sp"), extra=alld))
        alld = [d for d in self.dma_last if d is not None]
        for e in self.ENG:
            self.op(e, tiny[e], reads=["_barA_" + x for x in self.ENG], writes=["_barB_" + e], dma=(e == "sp"),
                    extra=alld)
        for b in self.bufs.values():
            if not b.name.startswith("_bar"):
                b.w = None
                b.r = []

    def emit(self):
        nc = self.nc
        for e in self.ENG:
            c = 0
            for o in self.q[e]:
                if o.sig and not o.dma:
                    c += 1
                    o.cnt = c
        with contextlib.ExitStack() as st:
            esem = {e: st.enter_context(nc.semaphore("s_" + e)) for e in self.ENG}
            dsem = [st.enter_context(nc.semaphore("d_%d" % j)) for j in range(NDS)]
            block = st.enter_context(nc.Block())
            engobj = {"pe": block.tensor, "act": block.scalar, "dve": block.vector,
                      "pool": block.gpsimd, "sp": block.sync}

            def make(e):
                def body(eng):
                    seen = {}
                    for o in self.q[e]:
                        for d in o.deps:
                            if d.dma:
                                key = ("d", d.dsem)
                                val = d.dval
                                sem = dsem[d.dsem]
                            else:
                                key = ("e", d.eng)
                                val = d.cnt
                                sem = esem[d.eng]
                            if seen.get(key, 0) >= val:
                                continue
                            seen[key] = val
                            eng.wait_ge(sem, val)
                        ins = o.fn(eng)
                        if o.dma:
                            ins.then_inc(dsem[o.dsem], 16)
                        elif o.sig:
                            ins.then_inc(esem[e], 1)
                    for j in range(NDS):
                        last = self.dma_last[j]
                        if last is not None and last.eng == e:
                            if seen.get(("d", j), 0) < last.dval:
                                eng.wait_ge(dsem[j], last.dval)
                return body

            for e in self.ENG:
                engobj[e](make(e))


def bc_last(ap2d, p, a, b):
    return ap2d.rearrange("p (a o) -> p a o", o=1).to_broadcast([p, a, b])


def build():
    nc = bass.Bass("TRN2", target_bir_lowering=False)

    def din(name, shape, dtype=F32):
        return nc.dram_tensor(name, list(shape), dtype, kind="ExternalInput").ap()

    def dout(name, shape, dtype=F32):
        return nc.dram_tensor(name, list(shape), dtype, kind="ExternalOutput").ap()

    def dscr(name, shape, dtype=F32):
        return nc.dram_tensor(name, list(shape), dtype, kind="Internal").ap()

    xp = din("xp", [4096, D])
    xs = din("xs", [32, D])
    flag = din("flag", [128, 4])
    st_re = din("st_re", [64, 64])
    st_im = din("st_im", [64, 64])
    ck = din("ck", [4096, 256])
    cv = din("cv", [4096, 256])
    cki = din("cki", [4096, 64])
    cconv = din("cconv", [2, 2, DFF])
    norm_mix = din("norm_mix", [2, D])
    norm_ffn = din("norm_ffn", [2, D])
    lam_re = din("lam_re", [64, 64])
    lam_im = din("lam_im", [64, 64])
    log_dt = din("log_dt", [1, 64])
    b_re = din("b_re", [64, 64, 16])
    b_im = din("b_im", [64, 64, 16])
    c_re = din("c_re", [64, 16, 64])
    c_im = din("c_im", [64, 16, 64])
    ssm_d = din("ssm_d", [1, D])
    w_glu = din("w_glu", [D, 2 * D])
    b_glu = din("b_glu", [1, 2 * D])
    w_in = din("w_in", [D, 2120])
    q_norm = din("q_norm", [1, 64])
    k_norm = din("k_norm", [1, 64])
    w_o = din("w_o", [D, D])
    w_up = din("w_up", [2, D, 2 * DFF])
    conv_w = din("conv_w", [2, 3, DFF])
    conv_b = din("conv_b", [2, DFF])
    w_down = din("w_down", [2, DFF, D])

    y_own = dout("y_own", [2048, D])
    y_s = dout("y_s", [32, D])
    o_ssm_p = dout("o_ssm_p", [2, 64, 64])
    o_ssm_s = dout("o_ssm_s", [2, 64, 64])
    o_k = dout("o_k", [2048, 256])
    o_v = dout("o_v", [2048, 256])
    o_ki = dout("o_ki", [2048, 64])
    o_ks = dout("o_ks", [32, 256])
    o_vs = dout("o_vs", [32, 256])
    o_kis = dout("o_kis", [32, 64])
    o_conv_p = dout("o_conv_p", [2, 2, DFF])
    o_conv_s = dout("o_conv_s", [2, 2, DFF])

    Zp = dscr("Zp", [4096, D], BF16)
    Zs = dscr("Zs", [32, D], BF16)
    X2p = dscr("X2p", [4096, D])
    X2s = dscr("X2s", [32, D])
    X3p = dscr("X3p", [2176, D])
    X3s = dscr("X3s", [32, D])
    Wglu_b = dscr("Wglu_b", [D, 2 * D], BF16)
    Win_b = dscr("Win_b", [D, 2120], BF16)
    Wo_b = dscr("Wo_b", [D, D], BF16)
    Wd_b = [dscr("Wd_b%d" % l, [DFF, D], BF16) for l in range(2)]
    Wup_blk = [dscr("Wup_blk%d" % l, [NFC, 128, 8, 256], BF16) for l in range(2)]

    P = Prog(nc)
    DBG = {}

    def dbgd(name, src_ap, shape, dtype, reads):
        if not DEBUG:
            return
        o = nc.dram_tensor("dbg_" + name, list(shape), dtype, kind="ExternalOutput").ap()
        P.op("pool", lambda e: e.dma_start(out=o, in_=src_ap), reads, ["dbgout_" + name], dma=True)

    def dbg(name, tile_ap, shape, dtype, reads):
        if not DEBUG:
            return
        o = nc.dram_tensor("dbg_" + name, list(shape), dtype, kind="ExternalOutput").ap()
        P.op("pool", lambda e: e.dma_start(out=o, in_=tile_ap), reads, ["dbgout_" + name], dma=True)
    V = lambda fn, r=(), w=(): P.op("dve", fn, r, w)
    A = lambda fn, r=(), w=(): P.op("act", fn, r, w)
    G = lambda fn, r=(), w=(): P.op("pool", fn, r, w)
    T = lambda fn, r=(), w=(): P.op("pe", fn, r, w)
    LD = lambda fn, r=(), w=(): P.op("sp", fn, r, w, dma=True)
    ST = lambda fn, r=(), w=(): P.op("pool", fn, r, w, dma=True)

    with contextlib.ExitStack() as top:
        sbt = lambda st, name, shape, dt: st.enter_context(nc.sbuf_tensor(name, list(shape), dt))
        psF = [top.enter_context(nc.psum_tensor("psF%d" % i, [128, 512], F32)) for i in range(6)]
        psB = [top.enter_context(nc.psum_tensor("psB%d" % i, [128, 1024], BF16)) for i in range(2)]
        identf = sbt(top, "identf", [128, 128], F32)
        identb = sbt(top, "identb", [128, 128], BF16)
        iot = sbt(top, "iot", [128, 128], F32)
        flg = sbt(top, "flg", [128, 4], F32)
        tinyt = sbt(top, "tinyt", [128, 8], F32)
        tiny = {
            "pe": lambda e: e.matmul(psF[5][0:1, 0:1], lhsT=identb[0:1, 0:1], rhs=identb[0:1, 0:1], start=True, stop=True),
            "act": lambda e: e.copy(out=tinyt[0:1, 0:1], in_=tinyt[0:1, 4:5]),
            "dve": lambda e: e.memset(tinyt[0:1, 1:2], 0.0),
            "pool": lambda e: e.memset(tinyt[0:1, 2:3], 0.0),
            "sp": lambda e: e.dma_start(out=tinyt[0:1, 3:4], in_=tinyt[0:1, 5:6]),
        }
        G(lambda e: e.memset(tinyt[:], 0.0), (), ["tinyt"])
        G(lambda e: e.iota(iot[:], pattern=[[1, 128]], base=0, channel_multiplier=-1,
                           allow_small_or_imprecise_dtypes=True), (), ["iot"])
        V(lambda e: e.tensor_single_scalar(out=identf[:], in_=iot[:], scalar=0.0, op=ALU.is_equal), ["iot"], ["identf"])
        V(lambda e: e.tensor_copy(out=identb[:], in_=identf[:]), ["identf"], ["identb"])
        LD(lambda e: e.dma_start(out=flg[:], in_=flag), (), ["flg"])


        if True:
            phc = top
            CW = 704
            def conv_dma(src, dst, dname):
                ST(lambda e: e.dma_start(out=dst, in_=src), (), [dname])

            def conv_plain(src2d, dst2d, nrows, ncols, dname):
                for r in range(nrows // 128):
                    conv_dma(src2d[r * 128:(r + 1) * 128, :], dst2d[r * 128:(r + 1) * 128, :], dname)
                    yield

            def conv_all():
                yield from conv_plain(w_glu, Wglu_b, D, 2 * D, "Wglu_b")
                for l in range(2):
                    for dk in range(8):
                        for half in range(2):
                            conv_dma(w_up[l, dk * 128:(dk + 1) * 128, half * DFF:(half + 1) * DFF].rearrange("p (b c) -> p b c", c=128),
                                     Wup_blk[l].rearrange("b p dk c -> p b dk c")[:, :, dk, half * 128:(half + 1) * 128], "Wup_blk%d" % l)
                            yield
                    yield from conv_plain(w_down[l], Wd_b[l], DFF, D, "Wd_b%d" % l)
                yield from conv_plain(w_in, Win_b, D, 2120, "Win_b")
                yield from conv_plain(w_o, Wo_b, D, D, "Wo_b")
            conv_gen = [conv_all()]

            def conv_advance(n):
                for _ in range(n):
                    if conv_gen[0] is None:
                        return
                    try:
                        next(conv_gen[0])
                    except StopIteration:
                        conv_gen[0] = None
                        return
        with contextlib.ExitStack() as ph1:
            Tw = sbt(ph1, "Tw", [128, 64, 128], BF16)
            Pre = sbt(ph1, "Pre", [128, 64, 64], BF16)
            Pim = sbt(ph1, "Pim", [128, 64, 64], BF16)
            Qre = sbt(ph1, "Qre", [64, 64, 128], BF16)
            Qni = sbt(ph1, "Qni", [64, 64, 128], BF16)
            AR2 = sbt(ph1, "AR2", [64, 2, 64], F32)
            AIn = sbt(ph1, "AIn", [64, 2, 64], F32)
            gmx = sbt(ph1, "gmx", [64, D], F32)
            LD(lambda e: e.dma_start(out=gmx[:], in_=norm_mix[0:1, :].partition_broadcast(64)), (), ["gmx"])

            conv_advance(24)
            with contextlib.ExitStack() as ph0:
                lre = sbt(ph0, "lre", [64, 64], F32)
                lim = sbt(ph0, "lim", [64, 64], F32)
                dtb = sbt(ph0, "dtb", [64, 64], F32)
                ldre = sbt(ph0, "ldre", [64, 64], F32)
                ldim = sbt(ph0, "ldim", [64, 64], F32)
                E = sbt(ph0, "E", [64, 16, 64], F32)
                Y = sbt(ph0, "Y", [64, 16, 64], F32)
                Ri = sbt(ph0, "Ri", [64, 16, 64], I32)
                Rf = sbt(ph0, "Rf", [64, 16, 64], F32)
                CO = sbt(ph0, "CO", [64, 16, 64], F32)
                SI = sbt(ph0, "SI", [64, 16, 64], F32)
                are = sbt(ph0, "are", [64, 16, 64], F32)
                aim = sbt(ph0, "aim", [64, 16, 64], F32)
                t64 = [sbt(ph0, "t64_%d" % i, [64, 64], F32) for i in range(6)]
                fre = sbt(ph0, "fre", [64, 64], F32)
                fim = sbt(ph0, "fim", [64, 64], F32)
                Bre = sbt(ph0, "Bre", [64, 64, 16], F32)
                Bim = sbt(ph0, "Bim", [64, 64, 16], F32)
                bbre = sbt(ph0, "bbre", [64, 64, 16], F32)
                bbim = sbt(ph0, "bbim", [64, 64, 16], F32)
                tb1 = sbt(ph0, "tb1", [64, 64, 16], F32)
                tb2 = sbt(ph0, "tb2", [64, 64, 16], F32)
                Cn = [sbt(ph0, "Cn%d" % i, [128, 8, 64], F32) for i in range(2)]
                Ct = [sbt(ph0, "Ct%d" % i, [64, 64, 16], F32) for i in range(2)]
                maskf = sbt(ph0, "maskf", [128, 128], F32)
                dE = sbt(ph0, "dE", [128, 64], F32)
                PLre = sbt(ph0, "PLre", [64, 16, 15, 16], F32)
                PLim = sbt(ph0, "PLim", [64, 16, 15, 16], F32)
                QRre = sbt(ph0, "QRre", [64, 16, 9, 16], F32)
                QRni = sbt(ph0, "QRni", [64, 16, 9, 16], F32)
                tq1 = sbt(ph0, "tq1", [64, 16, 16], F32)
                tq2 = sbt(ph0, "tq2", [64, 16, 16], F32)
                tmsk = sbt(ph0, "tmsk", [128, 4, 128], F32)

                LD(lambda e: e.dma_start(out=lre[:], in_=lam_re.rearrange("g p -> p g"), allow_slow_non_contiguous=True), (), ["lre"])
                LD(lambda e: e.dma_start(out=lim[:], in_=lam_im.rearrange("g p -> p g"), allow_slow_non_contiguous=True), (), ["lim"])
                LD(lambda e: e.dma_start(out=dtb[:], in_=log_dt[0:1, :].partition_broadcast(64)), (), ["dtb"])
                LD(lambda e: e.dma_start(out=Bre[:], in_=b_re.rearrange("g p c -> p g c")), (), ["Bre"])
                LD(lambda e: e.dma_start(out=Bim[:], in_=b_im.rearrange("g p c -> p g c")), (), ["Bim"])
                for i, csrc in enumerate((c_re, c_im)):
                    LD(lambda e, i=i, csrc=csrc: e.dma_start(
                        out=Cn[i][:], in_=csrc.rearrange("(gb g8) c p -> (g8 c) gb p", g8=8)), (), ["Cn%d" % i])
                for j in range(8):
                    LD(lambda e, j=j: e.dma_start(out=dE[16 * j:16 * j + 16, :],
                                                  in_=ssm_d[0, :].rearrange("(g c) -> c g", c=16),
                                                  allow_slow_non_contiguous=True), (), ["dE"])
                A(lambda e: e.activation(out=dtb[:], in_=dtb[:], func=AF.Exp), ["dtb"], ["dtb"])
                V(lambda e: e.tensor_scalar_min(out=lre[:], in0=lre[:], scalar1=-1e-4), ["lre"], ["lre"])
                V(lambda e: e.tensor_mul(out=ldre[:], in0=lre[:], in1=dtb[:]), ["lre", "dtb"], ["ldre"])
                V(lambda e: e.tensor_mul(out=ldim[:], in0=lim[:], in1=dtb[:]), ["lim", "dtb"], ["ldim"])
                for kk in range(16):
                    k = kk - 7
                    A(lambda e, kk=kk, k=k: e.activation(out=E[:, kk, :], in_=ldre[:], func=AF.Exp, scale=float(k)), ["ldre"], ["E"])
                    V(lambda e, kk=kk, k=k: e.tensor_scalar_mul(out=Y[:, kk, :], in0=ldim[:], scalar1=float(k) / TWO_PI), ["ldim"], ["Y"])
                V(lambda e: e.tensor_copy(out=Ri[:], in_=Y[:]), ["Y"], ["Ri"])
                V(lambda e: e.tensor_copy(out=Rf[:], in_=Ri[:]), ["Ri"], ["Rf"])
                V(lambda e: e.tensor_sub(out=Y[:], in0=Y[:], in1=Rf[:]), ["Y", "Rf"], ["Y"])
                A(lambda e: e.activation(out=SI[:], in_=Y[:], func=AF.Sin, scale=SIN_SCALE), ["Y"], ["SI"])
                V(lambda e: e.tensor_scalar_add(out=Y[:], in0=Y[:], scalar1=0.25), ["Y", "SI"], ["Y"])
                V(lambda e: e.tensor_single_scalar(out=Rf[:], in_=Y[:], scalar=0.5, op=ALU.is_gt), ["Y"], ["Rf"])
                V(lambda e: e.tensor_sub(out=Y[:], in0=Y[:], in1=Rf[:]), ["Y", "Rf"], ["Y"])
                A(lambda e: e.activation(out=CO[:], in_=Y[:], func=AF.Sin, scale=SIN_SCALE), ["Y"], ["CO"])
                V(lambda e: e.tensor_mul(out=are[:], in0=E[:], in1=CO[:]), ["E", "CO"], ["are"])
                V(lambda e: e.tensor_mul(out=aim[:], in0=E[:], in1=SI[:]), ["E", "SI"], ["aim"])
                for h in range(2):
                    V(lambda e, h=h: e.tensor_copy(out=AR2[:, h, :], in_=are[:, 15, :]), ["are"], ["AR2"])
                V(lambda e: e.tensor_scalar_mul(out=AIn[:, 0, :], in0=aim[:, 15, :], scalar1=-1.0), ["aim"], ["AIn"])
                V(lambda e: e.tensor_copy(out=AIn[:, 1, :], in_=aim[:, 15, :]), ["aim"], ["AIn"])
                nre, a1i, den, u1, u2, u3 = t64
                V(lambda e: e.tensor_scalar_add(out=nre[:], in0=are[:, 8, :], scalar1=-1.0), ["are"], ["nre"])
                V(lambda e: e.tensor_copy(out=a1i[:], in_=aim[:, 8, :]), ["aim"], ["a1i"])
                V(lambda e: e.tensor_mul(out=den[:], in0=lre[:], in1=lre[:]), ["lre"], ["den"])
                V(lambda e: e.tensor_mul(out=u1[:], in0=lim[:], in1=lim[:]), ["lim"], ["u1"])
                V(lambda e: e.tensor_add(out=den[:], in0=den[:], in1=u1[:]), ["den", "u1"], ["den"])
                V(lambda e: e.reciprocal(out=den[:], in_=den[:]), ["den"], ["den"])
                V(lambda e: e.tensor_mul(out=u1[:], in0=nre[:], in1=lre[:]), ["nre", "lre"], ["u1"])
                V(lambda e: e.tensor_mul(out=u2[:], in0=a1i[:], in1=lim[:]), ["a1i", "lim"], ["u2"])
                V(lambda e: e.tensor_add(out=u1[:], in0=u1[:], in1=u2[:]), ["u1", "u2"], ["u1"])
                V(lambda e: e.tensor_mul(out=fre[:], in0=u1[:], in1=den[:]), ["u1", "den"], ["fre"])
                V(lambda e: e.tensor_mul(out=u2[:], in0=a1i[:], in1=lre[:]), ["a1i", "lre", "u1"], ["u2"])
                V(lambda e: e.tensor_mul(out=u3[:], in0=nre[:], in1=lim[:]), ["nre", "lim"], ["u3"])
                V(lambda e: e.tensor_sub(out=u2[:], in0=u2[:], in1=u3[:]), ["u2", "u3"], ["u2"])
                V(lambda e: e.tensor_mul(out=fim[:], in0=u2[:], in1=den[:]), ["u2", "den"], ["fim"])
                frb = bc_last(fre[:], 64, 64, 16)
                fib = bc_last(fim[:], 64, 64, 16)
                V(lambda e: e.tensor_mul(out=tb1[:], in0=Bre[:], in1=frb), ["Bre", "fre"], ["tb1"])
                V(lambda e: e.tensor_mul(out=tb2[:], in0=Bim[:], in1=fib), ["Bim", "fim"], ["tb2"])
                V(lambda e: e.tensor_sub(out=bbre[:], in0=tb1[:], in1=tb2[:]), ["tb1", "tb2"], ["bbre"])
                V(lambda e: e.tensor_mul(out=tb1[:], in0=Bim[:], in1=frb), ["Bim", "fre", "bbre"], ["tb1"])
                V(lambda e: e.tensor_mul(out=tb2[:], in0=Bre[:], in1=fib), ["Bre", "fim", "bbre"], ["tb2"])
                V(lambda e: e.tensor_add(out=bbim[:], in0=tb1[:], in1=tb2[:]), ["tb1", "tb2"], ["bbim"])
                for i in range(2):
                    for half in range(2):
                        for q4 in range(4):
                            gb = half * 4 + q4
                            T(lambda e, i=i, gb=gb, q4=q4: e.transpose(out=psF[0][0:64, q4 * 128:(q4 + 1) * 128],
                                                                       in_=Cn[i][:, gb, :], identity=identf[:]),
                              ["Cn%d" % i, "identf"], ["psF0"])
                        V(lambda e, i=i, half=half: e.tensor_copy(
                            out=Ct[i][:, half * 32:(half + 1) * 32, :].rearrange("p g c -> p (g c)"),
                            in_=psF[0][0:64, :]), ["psF0"], ["Ct%d" % i])
                V(lambda e: e.memset(maskf[:], 0.0), (), ["maskf"])
                for j in range(8):
                    V(lambda e, j=j: e.memset(maskf[0:16 * (j + 1), 16 * j:16 * j + 16], 1.0), (), ["maskf"])
                for gq in range(4):
                    g0 = gq * 16
                    for m in range(15):
                        kk = 14 - m
                        arb = bc_last(are[:, kk, g0:g0 + 16], 64, 16, 16)
                        aib = bc_last(aim[:, kk, g0:g0 + 16], 64, 16, 16)
                        V(lambda e, arb=arb, g0=g0: e.tensor_mul(out=tq1[:], in0=bbre[:, g0:g0 + 16, :], in1=arb), ["bbre", "are"], ["tq1"])
                        V(lambda e, aib=aib, g0=g0: e.tensor_mul(out=tq2[:], in0=bbim[:, g0:g0 + 16, :], in1=aib), ["bbim", "aim"], ["tq2"])
                        V(lambda e, m=m: e.tensor_sub(out=PLre[:, :, m, :], in0=tq1[:], in1=tq2[:]), ["tq1", "tq2"], ["PLre"])
                        V(lambda e, arb=arb, g0=g0: e.tensor_mul(out=tq1[:], in0=bbim[:, g0:g0 + 16, :], in1=arb), ["bbim", "are", "PLre"], ["tq1"])
                        V(lambda e, aib=aib, g0=g0: e.tensor_mul(out=tq2[:], in0=bbre[:, g0:g0 + 16, :], in1=aib), ["bbre", "aim", "PLre"], ["tq2"])
                        V(lambda e, m=m: e.tensor_add(out=PLim[:, :, m, :], in0=tq1[:], in1=tq2[:]), ["tq1", "tq2"], ["PLim"])
                    for k in range(9):
                        kk = k + 7
                        arb = bc_last(are[:, kk, g0:g0 + 16], 64, 16, 16)
                        aib = bc_last(aim[:, kk, g0:g0 + 16], 64, 16, 16)
                        V(lambda e, arb=arb, g0=g0: e.tensor_mul(out=tq1[:], in0=Ct[0][:, g0:g0 + 16, :], in1=arb), ["Ct0", "are", "PLim"], ["tq1"])
                        V(lambda e, aib=aib, g0=g0: e.tensor_mul(out=tq2[:], in0=Ct[1][:, g0:g0 + 16, :], in1=aib), ["Ct1", "aim", "PLim"], ["tq2"])
                        V(lambda e, k=k: e.tensor_sub(out=QRre[:, :, k, :], in0=tq1[:], in1=tq2[:]), ["tq1", "tq2"], ["QRre"])
                        V(lambda e, aib=aib, g0=g0: e.tensor_mul(out=tq1[:], in0=Ct[0][:, g0:g0 + 16, :], in1=aib), ["Ct0", "aim", "QRre"], ["tq1"])
                        V(lambda e, arb=arb, g0=g0: e.tensor_mul(out=tq2[:], in0=Ct[1][:, g0:g0 + 16, :], in1=arb), ["Ct1", "are", "QRre"], ["tq2"])
                        V(lambda e: e.tensor_add(out=tq1[:], in0=tq1[:], in1=tq2[:]), ["tq1", "tq2"], ["tq1"])
                        V(lambda e, k=k: e.tensor_scalar_mul(out=QRni[:, :, k, :], in0=tq1[:], scalar1=-1.0), ["tq1"], ["QRni"])
                    V(lambda e, g0=g0: e.tensor_copy(out=Qre[:, g0:g0 + 16, :].rearrange("p g (k c) -> p g k c", c=16),
                                                     in_=QRre[:, :, 1:9, :]), ["QRre"], ["Qre"])
                    V(lambda e, g0=g0: e.tensor_copy(out=Qni[:, g0:g0 + 16, :].rearrange("p g (k c) -> p g k c", c=16),
                                                     in_=QRni[:, :, 1:9, :]), ["QRni"], ["Qni"])
                    for q in range(4):
                        bank = psF[1 + (q % 2)]
                        bn = "psF%d" % (1 + (q % 2))
                        for gi in range(4):
                            gl = q * 4 + gi
                            T(lambda e, gl=gl, gi=gi, bank=bank: e.matmul(
                                bank[:, gi * 128:(gi + 1) * 128],
                                lhsT=PLre[:, gl, 7:15, :].rearrange("p m c -> p (m c)"),
                                rhs=QRre[:, gl, 0:8, :].rearrange("p k c -> p (k c)"), start=True, stop=False),
                              ["PLre", "QRre"], [bn])
                            T(lambda e, gl=gl, gi=gi, bank=bank: e.matmul(
                                bank[:, gi * 128:(gi + 1) * 128],
                                lhsT=PLim[:, gl, 7:15, :].rearrange("p m c -> p (m c)"),
                                rhs=QRni[:, gl, 0:8, :].rearrange("p k c -> p (k c)"), start=False, stop=True),
                              ["PLim", "QRni"], [bn])
                        V(lambda e, bank=bank: e.tensor_mul(out=tmsk[:], in0=bank[:, :].rearrange("p (g e) -> p g e", e=128),
                                                            in1=maskf[:].rearrange("p (o e) -> p o e", o=1).to_broadcast([128, 4, 128])),
                          [bn, "maskf"], ["tmsk"])
                        for gi in range(4):
                            g = g0 + q * 4 + gi
                            V(lambda e, g=g, gi=gi: e.scalar_tensor_tensor(out=Tw[:, g, :], in0=identf[:], scalar=dE[:, g:g + 1],
                                                                          in1=tmsk[:, gi, :], op0=ALU.mult, op1=ALU.add),
                              ["tmsk", "dE", "identf"], ["Tw"])
                    for half in range(2):
                        for i, (PLx, Px, nm) in enumerate(((PLre, Pre, "PLre"), (PLim, Pim, "PLim"))):
                            bank = psF[3 + i]
                            bn = "psF%d" % (3 + i)
                            for gi in range(8):
                                gl = half * 8 + gi
                                T(lambda e, gl=gl, gi=gi, bank=bank, PLx=PLx: e.transpose(
                                    out=bank[:, gi * 64:(gi + 1) * 64],
                                    in_=PLx[:, gl, 0:8, :].rearrange("p m c -> p (m c)"), identity=identf[0:64, 0:64]),
                                  [nm, "identf"], [bn])
                            gg = g0 + half * 8
                            A(lambda e, gg=gg, bank=bank, Px=Px: e.copy(out=Px[:, gg:gg + 8, :].rearrange("p g s -> p (g s)"),
                                                                        in_=bank[:, :]), [bn], ["P" + nm[2:]])
                dbg("are", are[:], [64, 16, 64], F32, ["are"])
                dbg("aim", aim[:], [64, 16, 64], F32, ["aim"])
                dbg("fre", fre[:], [64, 64], F32, ["fre"])
                dbg("bbre", bbre[:], [64, 64, 16], F32, ["bbre"])
                dbg("bbim", bbim[:], [64, 64, 16], F32, ["bbim"])
                dbg("Ct0", Ct[0][:], [64, 64, 16], F32, ["Ct0"])
                dbg("PLre", PLre[:], [64, 16, 15, 16], F32, ["PLre"])
                dbg("QRre", QRre[:], [64, 16, 9, 16], F32, ["QRre"])
                dbg("Tw", Tw[:], [128, 64, 128], BF16, ["Tw"])
                dbg("Pre", Pre[:], [128, 64, 64], BF16, ["Pre"])
                dbg("Pim", Pim[:], [128, 64, 64], BF16, ["Pim"])
                dbg("Qre", Qre[:], [64, 64, 128], BF16, ["Qre"])
                dbg("Qni", Qni[:], [64, 64, 128], BF16, ["Qni"])
            P.barrier(tiny)
            with contextlib.ExitStack() as ph1b:
                X = sbt(ph1b, "X", [64, 8, D], F32)
                sq = sbt(ph1b, "sq", [64, D], BF16)
                ss = sbt(ph1b, "ss", [64, 8], F32)
                uz = sbt(ph1b, "uz", [128, 8192], BF16)
                uE = uz[0:64, :].rearrange("p (g j c) -> p g j c", g=64, j=8)
                UEb = [sbt(ph1b, "UE%d" % i, [128, 64, 64], BF16) for i in range(2)]
                SSb = [sbt(ph1b, "SS%d" % i, [64, 2, 64, 64], BF16) for i in range(2)]
                HHbb = [sbt(ph1b, "HHb%d" % i, [64, 2, 64, 65], BF16) for i in range(2)]
                Xst = [sbt(ph1b, "Xst%d" % i, [64, 2, 64], F32) for i in range(8)]
                T1 = sbt(ph1b, "T1", [64, 2, 64], F32)
                T2 = sbt(ph1b, "T2", [64, 2, 64], F32)
                zE = uz[:, 0:4096].rearrange("p (g n) -> p g n", n=64)
                zsub = X[:, :, :].rearrange("p j d -> p (j d)").bitcast(BF16)[:, 0:8192].rearrange("p (j g c) -> p j g c", j=8, g=64)

                step = [0]

                def ssm_front(xsrc, nn, bi):
                    UE = UEb[bi]
                    UEn = "UE%d" % bi
                    SS = SSb[bi]
                    SSn = "SS%d" % bi
                    LD(lambda e: e.dma_start(out=X[:nn], in_=xsrc.rearrange("(n j) d -> n j d", j=8)), (), ["X"])
                    for j in range(8):
                        A(lambda e, j=j: e.activation(out=sq[:nn], in_=X[:nn, j, :], func=AF.Square, accum_out=ss[:nn, j:j + 1]),
                          ["X"], ["sq", "ss"])
                    V(lambda e: e.tensor_scalar(out=ss[:nn], in0=ss[:nn], scalar1=1.0 / D, scalar2=EPS, op0=ALU.mult, op1=ALU.add), ["ss"], ["ss"])
                    A(lambda e: e.sqrt(out=ss[:nn], in_=ss[:nn]), ["ss"], ["ss"])
                    yield
                    V(lambda e: e.reciprocal(out=ss[:nn], in_=ss[:nn]), ["ss"], ["ss"])
                    for j in range(8):
                        V(lambda e, j=j: e.scalar_tensor_tensor(
                            out=uE[:nn, :, j, :], in0=X[:nn, j, :].rearrange("p (g c) -> p g c", c=16), scalar=ss[:nn, j:j + 1],
                            in1=gmx[:nn].rearrange("p (g c) -> p g c", c=16), op0=ALU.mult, op1=ALU.mult),
                          ["X", "ss", "gmx"], ["uz"])
                    yield
                    for g8 in range(8):
                        bank = psB[g8 % 2]
                        bn = "psB%d" % (g8 % 2)
                        for gi in range(8):
                            g = g8 * 8 + gi
                            T(lambda e, g=g, gi=gi, bank=bank: e.transpose(
                                out=bank[:, gi * nn:(gi + 1) * nn], in_=uE[:nn, g, :, :].rearrange("p j c -> p (j c)"),
                                identity=identb[:nn, :nn]), ["uz", "identb"], [bn])
                        A(lambda e, g8=g8, bank=bank: e.copy(out=UE[:, g8 * 8:(g8 + 1) * 8, :nn],
                                                             in_=bank[:, 0:8 * nn].rearrange("p (g n) -> p g n", n=nn)), [bn], [UEn])
                    yield
                    for g8 in range(8):
                        for i, Px in enumerate((Pre, Pim)):
                            bank = psF[(g8 % 2) * 2 + i]
                            bn = "psF%d" % ((g8 % 2) * 2 + i)
                            for gi in range(8):
                                g = g8 * 8 + gi
                                T(lambda e, g=g, gi=gi, bank=bank, Px=Px: e.matmul(
                                    bank[0:64, gi * nn:(gi + 1) * nn], lhsT=Px[:, g, :], rhs=UE[:, g, :nn], start=True, stop=True),
                                  [UEn, "Pre", "Pim"], [bn])
                            A(lambda e, g8=g8, i=i, bank=bank: e.copy(
                                out=SS[:, i, g8 * 8:(g8 + 1) * 8, :nn], in_=bank[0:64, 0:8 * nn].rearrange("p (g n) -> p g n", n=nn)),
                              [bn], [SSn])

                def ssm_scan(nn, bi, first, init_dram=None, sched=None):
                    SS = SSb[bi]
                    SSn = "SS%d" % bi
                    HHb = HHbb[bi]
                    HHn = "HHb%d" % bi
                    sched = sched or {}
                    if first:
                        cur0 = Xst[step[0] % 8]
                        if init_dram is None:
                            V(lambda e: e.memset(cur0[:], 0.0), (), ["Xst%d" % (step[0] % 8)])
                        else:
                            for h in range(2):
                                LD(lambda e, h=h: e.dma_start(out=cur0[:, h, :], in_=init_dram[h].rearrange("g p -> p g"),
                                                              allow_slow_non_contiguous=True), (), ["Xst%d" % (step[0] % 8)])
                    prev_i = step[0] % 8
                    A(lambda e, prev_i=prev_i: e.copy(out=HHb[:, :, :, 0], in_=Xst[prev_i][:]), ["Xst%d" % prev_i], [HHn])
                    for n in range(nn):
                        pi = step[0] % 8
                        ci = (step[0] + 1) % 8
                        step[0] += 1
                        prv = Xst[pi]
                        cur = Xst[ci]
                        pn = "Xst%d" % pi
                        cn = "Xst%d" % ci
                        V(lambda e, prv=prv: e.tensor_mul(out=T1[:], in0=prv[:], in1=AR2[:]), [pn, "AR2"], ["T1"])
                        V(lambda e, prv=prv: e.tensor_mul(out=T2[:], in0=prv[:, ::-1, :], in1=AIn[:]), [pn, "AIn"], ["T2"])
                        V(lambda e: e.tensor_add(out=T1[:], in0=T1[:], in1=T2[:]), ["T1", "T2"], ["T1"])
                        V(lambda e, cur=cur, n=n: e.tensor_add(out=cur[:], in0=T1[:], in1=SS[:, :, :, n]), ["T1", SSn], [cn])
                        A(lambda e, cur=cur, n=n: e.copy(out=HHb[:, :, :, n + 1], in_=cur[:]), [cn], [HHn])
                        if n % 2 == 1:
                            conv_advance(1)
                        for g_ in sched.get(n, ()):
                            next(g_, None)
                    for gl in sched.values():
                        for g_ in gl:
                            for _ in g_:
                                pass

                def ssm_back(nn, bi, zdst):
                    UE = UEb[bi]
                    UEn = "UE%d" % bi
                    HHb = HHbb[bi]
                    HHn = "HHb%d" % bi
                    for g8 in range(8):
                        bank = psF[4 + (g8 % 2)]
                        bn = "psF%d" % (4 + (g8 % 2))
                        for gi in range(8):
                            g = g8 * 8 + gi
                            T(lambda e, g=g, gi=gi, bank=bank: e.matmul(bank[:, gi * nn:(gi + 1) * nn], lhsT=Tw[:, g, :],
                                                                        rhs=UE[:, g, :nn], start=True, stop=False), ["Tw", UEn], [bn])
                            T(lambda e, g=g, gi=gi, bank=bank: e.matmul(bank[:, gi * nn:(gi + 1) * nn], lhsT=Qre[:, g, :],
                                                                        rhs=HHb[:, 0, g, 0:nn], start=False, stop=False), ["Qre", HHn], [bn])
                            T(lambda e, g=g, gi=gi, bank=bank: e.matmul(bank[:, gi * nn:(gi + 1) * nn], lhsT=Qni[:, g, :],
                                                                        rhs=HHb[:, 1, g, 0:nn], start=False, stop=True), ["Qni", HHn], [bn])
                        A(lambda e, g8=g8, bank=bank: e.activation(out=zE[:, g8 * 8:(g8 + 1) * 8, :nn],
                                                                   in_=bank[:, 0:8 * nn].rearrange("p (g n) -> p g n", n=nn), func=AF.Gelu),
                          [bn], ["uz"])
                    yield
                    for g8 in range(8):
                        bank = psB[g8 % 2]
                        bn = "psB%d" % (g8 % 2)
                        for gi in range(8):
                            g = g8 * 8 + gi
                            T(lambda e, g=g, gi=gi, bank=bank: e.transpose(out=bank[:nn, gi * 128:(gi + 1) * 128], in_=zE[:, g, :nn],
                                                                           identity=identb[:]), ["uz", "identb"], [bn])
                        V(lambda e, g8=g8, bank=bank: e.tensor_copy(
                            out=zsub[:nn, :, g8 * 8:(g8 + 1) * 8, :].rearrange("n j g c -> n g j c"),
                            in_=bank[:nn, :].rearrange("n (g j c) -> n g j c", g=8, j=8)), [bn], ["X"])
                    ST(lambda e: e.dma_start(out=zdst.rearrange("(n j) d -> n j d", j=8),
                                             in_=zsub[:nn].rearrange("n j g c -> n j (g c)")), ["X"], ["Zscr"])

                def ssm_state_out(dst):
                    ci = step[0] % 8
                    for h in range(2):
                        ST(lambda e, h=h: e.dma_start(out=dst[h].rearrange("g p -> p g"), in_=Xst[ci][:, h, :],
                                                      allow_slow_non_contiguous=True), ["Xst%d" % ci], ["o_ssm"])

                for _ in ssm_front(xp[0:512, :], 64, 0):
                    pass
                prev_back = None
                for blk in range(8):
                    if blk + 1 < 8:
                        fg = ssm_front(xp[(blk + 1) * 512:(blk + 2) * 512, :], 64, (blk + 1) % 2)
                    else:
                        fg = ssm_front(xs, 4, 0)
                    sched = {24: [fg], 36: [fg], 44: [fg], 52: [fg]}
                    if prev_back is not None:
                        sched[1] = [prev_back]
                        sched[12] = [prev_back]
                    ssm_scan(64, blk % 2, first=(blk == 0), sched=sched)
                    prev_back = ssm_back(64, blk % 2, Zp[blk * 512:(blk + 1) * 512, :])
                for _ in prev_back:
                    pass
                ssm_state_out(o_ssm_p)
                step[0] += 1
                ssm_scan(4, 0, first=True, init_dram=(st_re, st_im))
                for _ in ssm_back(4, 0, Zs):
                    pass
                ssm_state_out(o_ssm_s)
                conv_advance(100000)
        P.barrier(tiny)

        def ffn_phase(layer, segs, with_glu):
            with contextlib.ExitStack() as ph:
                sbt = lambda st, name, shape, dt: st.enter_context(nc.sbuf_tensor(name + "_L%d" % layer, list(shape), dt))
                if with_glu:
                    Wglu = sbt(ph, "Wglu", [128, 8, 2 * D], BF16)
                    bglu = sbt(ph, "bglu", [128, 2 * D], F32)
                    zt = sbt(ph, "zt", [128, 4, D], BF16)
                    zT = sbt(ph, "zT", [128, 8, 512], BF16)
                    LD(lambda e: e.dma_start(out=Wglu[:], in_=Wglu_b.rearrange("(dk p) n -> p dk n", p=128)), ["Wglu_b"], ["Wglu"])
                    LD(lambda e: e.dma_start(out=bglu[:], in_=b_glu[0:1, :].partition_broadcast(128)), (), ["bglu"])
                Wd = sbt(ph, "Wd", [128, NFC, D], BF16)
                gffn = sbt(ph, "gffn", [128, D], F32)
                cw = sbt(ph, "cw", [128, NFC, 3], F32)
                cb = sbt(ph, "cb", [128, NFC], F32)
                halo = sbt(ph, "halo", [128, NFC, 2], F32)
                x1b = [sbt(ph, "x1_%d" % i, [128, 4, D], F32) for i in range(2)]
                stc = [0]
                xn = sbt(ph, "xn", [128, 4, D], BF16)
                xnT = sbt(ph, "xnT", [128, 8, 512], BF16)
                t1 = sbt(ph, "t1f", [128, D], F32)
                t2 = sbt(ph, "t2f", [128, D], F32)
                sqj = sbt(ph, "sqj", [128, D], BF16)
                ssq = sbt(ph, "ssq", [128, 4], F32)
                wblk = [sbt(ph, "wblk%d" % i, [128, 8, 256], BF16) for i in range(3)]
                aext = [sbt(ph, "aext%d" % i, [128, 514], F32) for i in range(2)]
                cbuf = [sbt(ph, "cbuf%d" % i, [128, 512], F32) for i in range(2)]
                hT = sbt(ph, "hT", [128, NFC, 512], BF16)
                LD(lambda e: e.dma_start(out=gffn[:], in_=norm_ffn[layer:layer + 1, :].partition_broadcast(128)), (), ["gffn"])
                late = [True]

                def late_loads():
                    if not late[0]:
                        return
                    late[0] = False
                    for j in range(3):
                        LD(lambda e, j=j: e.dma_start(out=cw[:, :, j], in_=conv_w[layer, j].rearrange("(b p) -> p b", p=128),
                                                      allow_slow_non_contiguous=True), (), ["cw"])
                    LD(lambda e: e.dma_start(out=cb[:], in_=conv_b[layer].rearrange("(b p) -> p b", p=128),
                                             allow_slow_non_contiguous=True), (), ["cb"])
                    LD(lambda e: e.dma_start(out=Wd[:], in_=Wd_b[layer].rearrange("(f p) n -> p f n", p=128)), ["Wd_b%d" % layer], ["Wd"])
                ublk = [0]
                for (xsrc, zsrc, dst, ntot, pp, nflag, halo_src, conv_dst) in segs:
                    TTfull = min(4, ntot // pp)
                    if isinstance(halo_src, str):
                        pass
                    elif halo_src is None:
                        G(lambda e: e.memset(halo[:], 0.0), (), ["halo"])
                    else:
                        for t_ in range(2):
                            LD(lambda e, halo_src=halo_src, t_=t_: e.dma_start(
                                out=halo[:, :, t_], in_=halo_src[t_].rearrange("(b p) -> p b", p=128),
                                allow_slow_non_contiguous=True), (), ["halo"])
                    nst = ntot // (TTfull * pp)

                    def st_params(sti):
                            TT = TTfull
                            ntok = TT * pp
                            r0 = sti * ntok
                            do_flag = sti < nflag
                            x1 = x1b[stc[0] % 2]
                            x1n = "x1_%d" % (stc[0] % 2)
                            stc[0] += 1
                            return (TT, ntok, r0, do_flag, x1, x1n)

                    def st_front(p_):
                            TT, ntok, r0, do_flag, x1, x1n = p_
                            LD(lambda e, x1=x1, r0=r0, ntok=ntok, pp=pp, TT=TT, xsrc=xsrc: e.dma_start(
                                out=x1[:pp, :TT, :], in_=xsrc[r0:r0 + ntok, :].rearrange("(tt p) d -> p tt d", p=pp)), ["Xsrc", "X3"], [x1n])
                            if with_glu:
                                LD(lambda e, x1=x1, r0=r0, ntok=ntok, pp=pp, TT=TT, zsrc=zsrc: e.dma_start(
                                    out=zt[:pp, :TT, :], in_=zsrc[r0:r0 + ntok, :].rearrange("(tt p) d -> p tt d", p=pp)), ["Zscr"], ["zt"])
                                for tt in range(TT):
                                    bank = psB[tt % 2]
                                    bn = "psB%d" % (tt % 2)
                                    for dk in range(8):
                                        T(lambda e, x1=x1, tt=tt, dk=dk, bank=bank, pp=pp: e.transpose(
                                            out=bank[:, dk * pp:(dk + 1) * pp], in_=zt[:pp, tt, dk * 128:(dk + 1) * 128],
                                            identity=identb[:pp, :pp]), ["zt", "identb"], [bn])
                                    A(lambda e, x1=x1, tt=tt, bank=bank, pp=pp: e.copy(
                                        out=zT[:, :, tt * pp:(tt + 1) * pp], in_=bank[:, 0:8 * pp].rearrange("p (k n) -> p k n", n=pp)), [bn], ["zT"])
                                for tt in range(TT):
                                    for nb in range(4):
                                        for dk in range(8):
                                            T(lambda e, x1=x1, tt=tt, nb=nb, dk=dk, pp=pp: e.matmul(
                                                psF[nb][:pp, :], lhsT=zT[:, dk, tt * pp:(tt + 1) * pp], rhs=Wglu[:, dk, nb * 512:(nb + 1) * 512],
                                                start=(dk == 0), stop=(dk == 7)), ["zT", "Wglu"], ["psF%d" % nb])
                                    for nb in range(2):
                                        V(lambda e, x1=x1, nb=nb, pp=pp: e.tensor_add(out=t1[:pp, nb * 512:(nb + 1) * 512], in0=psF[nb][:pp, :],
                                                                               in1=bglu[:pp, nb * 512:(nb + 1) * 512]), ["psF%d" % nb, "bglu"], ["t1f"])
                                        V(lambda e, x1=x1, nb=nb, pp=pp: e.tensor_add(out=t2[:pp, nb * 512:(nb + 1) * 512], in0=psF[2 + nb][:pp, :],
                                                                               in1=bglu[:pp, D + nb * 512:D + (nb + 1) * 512]),
                                          ["psF%d" % (2 + nb), "bglu"], ["t2f"])
                                    A(lambda e, x1=x1, pp=pp: e.activation(out=t2[:pp, :], in_=t2[:pp, :], func=AF.Sigmoid), ["t2f"], ["t2f"])
                                    V(lambda e, x1=x1, pp=pp: e.tensor_mul(out=t1[:pp, :], in0=t1[:pp, :], in1=t2[:pp, :]), ["t1f", "t2f"], ["t1f"])
                                    V(lambda e, x1=x1, tt=tt, pp=pp: e.tensor_add(out=x1[:pp, tt, :], in0=x1[:pp, tt, :], in1=t1[:pp, :]), ["t1f", x1n], [x1n])
                                    if do_flag:
                                        V(lambda e, x1=x1, tt=tt, pp=pp: e.tensor_scalar_mul(out=x1[:pp, tt, :], in0=x1[:pp, tt, :], scalar1=flg[:pp, 0:1]),
                                          [x1n, "flg"], [x1n])
                            for tt in range(TT):
                                A(lambda e, x1=x1, tt=tt, pp=pp: e.activation(out=sqj[:pp, :], in_=x1[:pp, tt, :], func=AF.Square,
                                                                       accum_out=ssq[:pp, tt:tt + 1]), [x1n], ["sqj", "ssq%d" % tt])
                                V(lambda e, x1=x1, pp=pp, tt=tt: e.tensor_scalar(out=ssq[:pp, tt:tt + 1], in0=ssq[:pp, tt:tt + 1], scalar1=1.0 / D, scalar2=EPS,
                                                                          op0=ALU.mult, op1=ALU.add), ["ssq%d" % tt], ["ssq%d" % tt])
                                A(lambda e, x1=x1, pp=pp, tt=tt: e.sqrt(out=ssq[:pp, tt:tt + 1], in_=ssq[:pp, tt:tt + 1]), ["ssq%d" % tt], ["ssq%d" % tt])
                                V(lambda e, x1=x1, pp=pp, tt=tt: e.reciprocal(out=ssq[:pp, tt:tt + 1], in_=ssq[:pp, tt:tt + 1]), ["ssq%d" % tt], ["ssq%d" % tt])
                                V(lambda e, x1=x1, tt=tt, pp=pp: e.scalar_tensor_tensor(out=xn[:pp, tt, :], in0=x1[:pp, tt, :], scalar=ssq[:pp, tt:tt + 1],
                                                                                 in1=gffn[:pp, :], op0=ALU.mult, op1=ALU.mult),
                                  [x1n, "ssq%d" % tt, "gffn"], ["xn%d" % tt])
                            for tt in range(TT):
                                bank = psB[tt % 2]
                                bn = "psB%d" % (tt % 2)
                                for dk in range(8):
                                    T(lambda e, x1=x1, tt=tt, dk=dk, bank=bank, pp=pp: e.transpose(
                                        out=bank[:, dk * pp:(dk + 1) * pp], in_=xn[:pp, tt, dk * 128:(dk + 1) * 128],
                                        identity=identb[:pp, :pp]), ["xn%d" % tt, "identb"], [bn])
                                A(lambda e, x1=x1, tt=tt, bank=bank, pp=pp: e.copy(
                                    out=xnT[:, :, tt * pp:(tt + 1) * pp], in_=bank[:, 0:8 * pp].rearrange("p (k n) -> p k n", n=pp)), [bn], ["xnT"])

                    def st_up(p_):
                            TT, ntok, r0, do_flag, x1, x1n = p_
                            if dst is None:
                                for b in range(NFC):
                                    wi = ublk[0] % 3
                                    ublk[0] += 1
                                    wb = wblk[wi]
                                    wn = "wblk%d" % wi
                                    LD(lambda e, x1=x1, b=b, wb=wb: e.dma_start(out=wb[:], in_=Wup_blk[layer][b]), ["Wup_blk%d" % layer], [wn])
                                    for dk in range(8):
                                        T(lambda e, x1=x1, b=b, dk=dk, wb=wb, ntok=ntok: e.matmul(
                                            psF[0][:, 2 * b:2 * b + 2], lhsT=wb[:, dk, 0:128], rhs=xnT[:, dk, ntok - 2:ntok],
                                            start=(dk == 0), stop=(dk == 7)), [wn, "xnT"], ["psF0"])
                                A(lambda e, x1=x1: e.copy(out=halo[:, :, :], in_=psF[0][:, 0:2 * NFC].rearrange("p (b t) -> p b t", t=2)),
                                  ["psF0"], ["halo"])
                                return
                            late_loads()
                            for b in range(NFC):
                                wi = ublk[0] % 3
                                ai = ublk[0] % 2
                                ublk[0] += 1
                                wb = wblk[wi]
                                wn = "wblk%d" % wi
                                pa = psF[2 * wi]
                                pv = psF[2 * wi + 1]
                                pan = "psF%d" % (2 * wi)
                                pvn = "psF%d" % (2 * wi + 1)
                                ax = aext[ai]
                                axn = "aext%d" % ai
                                cbf = cbuf[ai]
                                cbn = "cbuf%d" % ai
                                LD(lambda e, x1=x1, b=b, wb=wb: e.dma_start(out=wb[:], in_=Wup_blk[layer][b]), ["Wup_blk%d" % layer], [wn])
                                for i, (pbank, pn) in enumerate(((pa, pan), (pv, pvn))):
                                    for dk in range(8):
                                        T(lambda e, x1=x1, i=i, dk=dk, pbank=pbank, wb=wb, ntok=ntok: e.matmul(
                                            pbank[:, :ntok], lhsT=wb[:, dk, i * 128:(i + 1) * 128], rhs=xnT[:, dk, :ntok],
                                            start=(dk == 0), stop=(dk == 7)), [wn, "xnT"], [pn])
                                A(lambda e, x1=x1, ax=ax, pa=pa, ntok=ntok: e.copy(out=ax[:, 2:2 + ntok], in_=pa[:, :ntok]), [pan], [axn])
                                G(lambda e, x1=x1, ax=ax, b=b: e.tensor_copy(out=ax[:, 0:2], in_=halo[:, b, :]), ["halo"], [axn])
                                G(lambda e, x1=x1, ax=ax, b=b, ntok=ntok: e.tensor_copy(out=halo[:, b, :], in_=ax[:, ntok:ntok + 2]), [axn], ["halo"])
                                V(lambda e, x1=x1, ax=ax, cbf=cbf, b=b, ntok=ntok: e.tensor_scalar(
                                    out=cbf[:, :ntok], in0=ax[:, 2:2 + ntok], scalar1=cw[:, b, 2:3], scalar2=cb[:, b:b + 1],
                                    op0=ALU.mult, op1=ALU.add), [axn, "cw", "cb"], [cbn])
                                V(lambda e, x1=x1, ax=ax, cbf=cbf, b=b, ntok=ntok: e.scalar_tensor_tensor(
                                    out=cbf[:, :ntok], in0=ax[:, 1:1 + ntok], scalar=cw[:, b, 1:2], in1=cbf[:, :ntok],
                                    op0=ALU.mult, op1=ALU.add), [axn, "cw", cbn], [cbn])
                                V(lambda e, x1=x1, ax=ax, cbf=cbf, b=b, ntok=ntok: e.scalar_tensor_tensor(
                                    out=cbf[:, :ntok], in0=ax[:, 0:ntok], scalar=cw[:, b, 0:1], in1=cbf[:, :ntok],
                                    op0=ALU.mult, op1=ALU.add), [axn, "cw", cbn], [cbn])
                                A(lambda e, x1=x1, cbf=cbf, ntok=ntok: e.activation(out=cbf[:, :ntok], in_=cbf[:, :ntok], func=AF.Gelu), [cbn], [cbn])
                                V(lambda e, x1=x1, cbf=cbf, b=b, pv=pv, ntok=ntok: e.tensor_mul(out=hT[:, b, :ntok], in0=cbf[:, :ntok], in1=pv[:, :ntok]),
                                  [cbn, pvn], ["hT"])

                    def st_down(p_):
                            TT, ntok, r0, do_flag, x1, x1n = p_
                            for tt in range(TT if dst is not None else 0):
                                for nb in range(2):
                                    bi = (tt * 2 + nb) % 6
                                    for fch in range(NFC):
                                        T(lambda e, x1=x1, tt=tt, nb=nb, fch=fch, bi=bi, pp=pp: e.matmul(
                                            psF[bi][:pp, :], lhsT=hT[:, fch, tt * pp:(tt + 1) * pp], rhs=Wd[:, fch, nb * 512:(nb + 1) * 512],
                                            start=(fch == 0), stop=(fch == NFC - 1)), ["hT", "Wd"], ["psF%d" % bi])
                                    V(lambda e, x1=x1, tt=tt, nb=nb, bi=bi, pp=pp: e.tensor_add(
                                        out=x1[:pp, tt, nb * 512:(nb + 1) * 512], in0=x1[:pp, tt, nb * 512:(nb + 1) * 512], in1=psF[bi][:pp, :]),
                                      ["psF%d" % bi, x1n], [x1n])
                            if dst is not None:
                                ST(lambda e, x1=x1, r0=r0, ntok=ntok, pp=pp, TT=TT, dst=dst: e.dma_start(
                                    out=dst[r0:r0 + ntok, :].rearrange("(tt p) d -> p tt d", p=pp), in_=x1[:pp, :TT, :]), [x1n], ["Xdst"])

                    ps_ = [st_params(sti) for sti in range(nst)]
                    st_front(ps_[0])
                    for i_ in range(nst):
                        st_up(ps_[i_])
                        if i_ + 1 < nst:
                            st_front(ps_[i_ + 1])
                        st_down(ps_[i_])
                    for t_ in range(2 if conv_dst is not None else 0):
                        ST(lambda e, conv_dst=conv_dst, t_=t_: e.dma_start(
                            out=conv_dst[t_].rearrange("(b p) -> p b", p=128), in_=halo[:, :, t_],
                            allow_slow_non_contiguous=True), ["halo"], ["o_conv"])

        ffn_phase(0, [(xp, Zp, X2p, 4096, 128, 4, None, o_conv_p[0]),
                      (xs, Zs, X2s, 32, 32, 0, cconv[0], o_conv_s[0])], True)
        P.barrier(tiny)

        with contextlib.ExitStack() as ph3:
            Win = sbt(ph3, "Win", [128, 8, 2120], BF16)
            Wo = sbt(ph3, "Wo", [128, 8, D], BF16)
            g1T = sbt(ph3, "g1T", [128, 8], F32)
            est = sbt(ph3, "est", [128, 2], F32)
            hstb = [sbt(ph3, "hst%d" % i, [128, 20], F32) for i in range(2)]
            epsN = sbt(ph3, "epsN", [128, 1], F32)
            qg = sbt(ph3, "qg", [128, 64], F32)
            kg = sbt(ph3, "kg", [128, 64], F32)
            identL = sbt(ph3, "identL", [128, 128], BF16)
            iop = sbt(ph3, "iop", [128, 1], F32)
            inv8 = sbt(ph3, "inv8", [128, 8], F32)
            negC = sbt(ph3, "negC", [128, 1], F32)
            pw2 = sbt(ph3, "pw2", [128, 30], F32)
            kT_all = sbt(ph3, "kT_all", [128, 4, 4224], BF16)
            kiT_all = sbt(ph3, "kiT_all", [128, 4224], BF16)
            V1 = sbt(ph3, "V1", [128, 33, 4, 65], BF16)
            x2tb = [sbt(ph3, "x2t%d" % i, [128, D], F32) for i in range(2)]
            xn1 = sbt(ph3, "xn1", [128, D], BF16)
            xnT1 = sbt(ph3, "xnT1", [128, 8, 128], BF16)
            proj = sbt(ph3, "proj", [128, 2120], F32)
            qb = sbt(ph3, "qb", [128, D], BF16)
            kvb = sbt(ph3, "kvb", [128, 512], BF16)
            qib = sbt(ph3, "qib", [128, 512], BF16)
            kib = sbt(ph3, "kib", [128, 64], BF16)
            qTb = [sbt(ph3, "qT%d" % i, [128, 16 * 128], BF16) for i in range(2)]
            qiT = sbt(ph3, "qiT", [128, 8 * 128], BF16)
            rt = [sbt(ph3, "rt%d" % i, [128, 16, 8], F32) for i in range(4)]
            sm = sbt(ph3, "sm", [128, 64], F32)
            smi = sbt(ph3, "smi", [128, 8], I32)
            csall = sbt(ph3, "csall", [128, 33, 16], F32)
            posall = sbt(ph3, "posall", [128, 33], F32)
            yall = sbt(ph3, "yall", [128, 33, 8], F32)
            yalli = sbt(ph3, "yalli", [128, 33, 8], I32)
            yallf = sbt(ph3, "yallf", [128, 33, 8], F32)
            wis = sbt(ph3, "wis", [128, 8], F32)
            scores = sbt(ph3, "scores", [128, 4224], F32)
            mb = sbt(ph3, "mb", [128, 4224], BF16)
            mbT = sbt(ph3, "mbT", [128, 33, 128], BF16)
            rl = [sbt(ph3, "rl%d" % i, [128, 512], F32) for i in range(4)]
            ET = [sbt(ph3, "ET%d" % i, [128, 512], BF16) for i in range(4)]
            mrep = [sbt(ph3, "mrep%d" % i, [128, 512], BF16) for i in range(2)]
            epsC = sbt(ph3, "epsC", [128, 1], F32)
            ob = sbt(ph3, "ob", [128, D], BF16)
            oT = sbt(ph3, "oT", [128, 8, 128], BF16)
            x3t = sbt(ph3, "x3t", [128, D], F32)
            bis = sbt(ph3, "bis", [128, 40], F32)
            rec = sbt(ph3, "rec", [128, 16], F32)

            LD(lambda e: e.dma_start(out=Win[:], in_=Win_b.rearrange("(dk p) n -> p dk n", p=128)), ["Win_b"], ["Win"])
            LD(lambda e: e.dma_start(out=Wo[:], in_=Wo_b.rearrange("(dk p) n -> p dk n", p=128)), ["Wo_b"], ["Wo"])
            LD(lambda e: e.dma_start(out=g1T[:], in_=norm_mix[1, :].rearrange("(dk p) -> p dk", p=128), allow_slow_non_contiguous=True), (), ["g1T"])
            for dk in range(8):
                V(lambda e, dk=dk: e.tensor_scalar_mul(out=Win[:, dk, :], in0=Win[:, dk, :], scalar1=g1T[:, dk:dk + 1]), ["Win", "g1T"], ["Win"])
            G(lambda e: e.memset(epsN[:], EPS), (), ["epsN"])
            LD(lambda e: e.dma_start(out=qg[:], in_=q_norm[0:1, :].partition_broadcast(128)), (), ["qg"])
            LD(lambda e: e.dma_start(out=kg[:], in_=k_norm[0:1, :].partition_broadcast(128)), (), ["kg"])
            V(lambda e: e.tensor_scalar_mul(out=proj[:, 0:128], in0=identf[:], scalar1=65536.0), ["identf"], ["proj"])
            V(lambda e: e.tensor_copy(out=identL[:], in_=proj[:, 0:128]), ["proj"], ["identL"])
            G(lambda e: e.iota(iop[:], pattern=[[0, 1]], base=0, channel_multiplier=1, allow_small_or_imprecise_dtypes=True), (), ["iop"])
            for k in range(30):
                V(lambda e, k=k: e.memset(pw2[:, k:k + 1], 2.0 ** (-(k + 1))), (), ["pw2"])
            G(lambda e: e.memset(negC[:], -8.0), (), ["negC"])
            V(lambda e: e.memset(kT_all[64:128, :, :], 0.0), (), ["kT_all"])
            V(lambda e: e.memset(kiT_all[64:128, :], 0.0), (), ["kiT_all"])
            V(lambda e: e.memset(qiT[64:128, :], 0.0), (), ["qiT"])
            for i in range(2):
                V(lambda e, i=i: e.memset(qTb[i][64:128, :], 0.0), (), ["qT%d" % i])
            G(lambda e: e.memset(epsC[:], 1e-30), (), ["epsC"])
            G(lambda e: e.memset(V1[:, :, :, 64:65], 1.0), (), ["V1"])

            ringb = [psF[3], psF[4], psF[5], psB[1][:, :].bitcast(F32)]
            ringn = ["psF3", "psF4", "psF5", "psB1"]
            ringi = [0]

            def ring():
                i = ringi[0] % 4
                ringi[0] += 1
                return ringb[i], ringn[i]

            SCN = ["sc%d" % i for i in range(9)]
            sqb = scores[:, 0:1024]
            SQN = ["sc0", "sc1"]
            qn = scores[:, 1024:2048]
            QNN = ["sc2", "sc3"]

            def sc_names(c0, c1):
                return SCN[c0 // 512:(c1 + 511) // 512]

            G(lambda e: e.iota(posall[:], pattern=[[128, 33]], base=0, channel_multiplier=1, allow_small_or_imprecise_dtypes=True), (), ["posall"])
            V(lambda e: e.tensor_scalar(out=posall[:, 16:32], in0=posall[:, 16:32], scalar1=flg[:, 2:3], scalar2=-2048.0, op0=ALU.add, op1=ALU.add),
              ["posall", "flg"], ["posall"])
            for i in range(8):
                V(lambda e, i=i: e.tensor_scalar_mul(out=yall[:, :, i], in0=posall[:], scalar1=(500000.0 ** (-i / 8.0)) / TWO_PI), ["posall"], ["yall"])
            V(lambda e: e.tensor_copy(out=yalli[:], in_=yall[:]), ["yall"], ["yalli"])
            V(lambda e: e.tensor_copy(out=yallf[:], in_=yalli[:]), ["yalli"], ["yallf"])
            V(lambda e: e.tensor_sub(out=yall[:], in0=yall[:], in1=yallf[:]), ["yall", "yallf"], ["yall"])
            A(lambda e: e.activation(out=csall[:, :, 8:16], in_=yall[:], func=AF.Sin, scale=SIN_SCALE), ["yall"], ["csall"])
            V(lambda e: e.tensor_scalar_add(out=yall[:], in0=yall[:], scalar1=0.25), ["yall", "csall"], ["yall"])
            V(lambda e: e.tensor_single_scalar(out=yallf[:], in_=yall[:], scalar=0.5, op=ALU.is_gt), ["yall"], ["yallf"])
            V(lambda e: e.tensor_sub(out=yall[:], in0=yall[:], in1=yallf[:]), ["yall", "yallf"], ["yall"])
            A(lambda e: e.activation(out=csall[:, :, 0:8], in_=yall[:], func=AF.Sin, scale=SIN_SCALE), ["yall"], ["csall"])

            def rope_tables(pp, base, use_flag):
                pass

            def rope(pp, t3, H, nms, ti):
                cosb = csall[:pp, ti, 0:8].rearrange("p (o f) -> p o f", o=1).to_broadcast([pp, H, 8])
                sinb = csall[:pp, ti, 8:16].rearrange("p (o f) -> p o f", o=1).to_broadcast([pp, H, 8])
                x1 = t3[:, :, 0:8]
                x2 = t3[:, :, 8:16]
                a_, b_, c_, d_ = [r[:pp, :H, :] for r in rt]
                V(lambda e: e.tensor_mul(out=a_, in0=x1, in1=cosb), nms + ["csall"], ["rt0"])
                V(lambda e: e.tensor_mul(out=b_, in0=x2, in1=sinb), nms + ["csall"], ["rt1"])
                V(lambda e: e.tensor_mul(out=c_, in0=x2, in1=cosb), nms + ["csall"], ["rt2"])
                V(lambda e: e.tensor_mul(out=d_, in0=x1, in1=sinb), nms + ["csall"], ["rt3"])
                V(lambda e: e.tensor_sub(out=x1, in0=a_, in1=b_), ["rt0", "rt1"], nms)
                V(lambda e: e.tensor_add(out=x2, in0=c_, in1=d_), ["rt2", "rt3"], nms)

            def head_norm(pp, src2d, dst2d, H, gain, nm_src, nm_dst, rstd, rn):
                V(lambda e: e.tensor_mul(out=dst2d.rearrange("p (h d) -> p h d", d=64), in0=src2d.rearrange("p (h d) -> p h d", d=64),
                                         in1=bc_last(rstd, pp, H, 64)), nm_src + [rn], nm_dst)
                V(lambda e: e.tensor_mul(out=dst2d.rearrange("p (h d) -> p h d", d=64), in0=dst2d.rearrange("p (h d) -> p h d", d=64),
                                         in1=gain[:pp, :].rearrange("p (o d) -> p o d", o=1).to_broadcast([pp, H, 64])),
                  nm_dst + ["qg", "kg"], nm_dst)

            def store_keys(pp, kb, kv_src_bf, ki_src_bf, nm_kv, nm_ki):
                c0 = kb * 128
                for kv in range(4):
                    T(lambda e, kv=kv: e.transpose(out=psB[0][0:64, kv * pp:(kv + 1) * pp], in_=kv_src_bf[:, kv * 64:(kv + 1) * 64],
                                                   identity=identb[:pp, :pp]), [nm_kv, "identb"], ["psB0"])
                T(lambda e: e.transpose(out=psB[0][0:64, 4 * pp:5 * pp], in_=ki_src_bf, identity=identb[:pp, :pp]), [nm_ki, "identb"], ["psB0"])
                A(lambda e: e.copy(out=kT_all[0:64, :, c0:c0 + pp], in_=psB[0][0:64, 0:4 * pp].rearrange("p (k n) -> p k n", n=pp)),
                  ["psB0"], ["kT_all"])
                A(lambda e: e.copy(out=kiT_all[0:64, c0:c0 + pp], in_=psB[0][0:64, 4 * pp:5 * pp]), ["psB0"], ["kiT_all"])
                G(lambda e: e.tensor_copy(out=V1[:pp, kb, :, 0:64], in_=kv_src_bf[:, 256:512].rearrange("p (k d) -> p k d", d=64)),
                  [nm_kv], ["V1"])

            def emitE(t, buf):
                pp, xsrc_ap, full = t["pp"], t["xsrc"], t["full"]
                x2t = x2tb[buf]
                x2n = "x2t%d" % buf
                LD(lambda e: e.dma_start(out=x2t[:pp, :], in_=xsrc_ap), ["Xdst"], [x2n])
                A(lambda e: e.activation(out=ob[:pp, :], in_=x2t[:pp, :], func=AF.Square, accum_out=est[:pp, 0:1]), [x2n], ["ob", "est"])
                A(lambda e: e.activation(out=est[:pp, 0:1], in_=est[:pp, 0:1], func=AF.Ln, scale=1.0 / D, bias=epsN[:pp, 0:1]), ["est", "epsN"], ["est"])
                A(lambda e: e.activation(out=est[:pp, 0:1], in_=est[:pp, 0:1], func=AF.Exp, scale=-0.5), ["est"], ["est"])
                A(lambda e: e.activation(out=xn1[:pp, :], in_=x2t[:pp, :], func=AF.Copy, scale=est[:pp, 0:1]), [x2n, "est"], ["xn1"])
                for dk in range(8):
                    T(lambda e, dk=dk: e.transpose(out=psB[0][:, dk * pp:(dk + 1) * pp], in_=xn1[:pp, dk * 128:(dk + 1) * 128],
                                                   identity=identb[:pp, :pp]), ["xn1", "identb"], ["psB0"])
                A(lambda e: e.copy(out=xnT1[:, :, :pp], in_=psB[0][:, 0:8 * pp].rearrange("p (k n) -> p k n", n=pp)), ["psB0"], ["xnT1"])
                ko, kio, pn = t.get("ko", 1024), t.get("kio", 2048), t.get("pn", "proj")
                blocks = [(1024, 512, ko), (2048, 72, kio)]
                if full:
                    blocks = [(0, 512, 0), (512, 512, 512), (1024, 512, 1024), (1536, 512, 1536), (2048, 72, 2048)]
                for (c0, w, d0) in blocks:
                    bank, bn = ring()
                    for dk in range(8):
                        T(lambda e, dk=dk, c0=c0, w=w, bank=bank: e.matmul(bank[:pp, :w], lhsT=xnT1[:, dk, :pp], rhs=Win[:, dk, c0:c0 + w],
                                                                           start=(dk == 0), stop=(dk == 7)), ["xnT1", "Win"], [bn])
                    A(lambda e, d0=d0, w=w, bank=bank: e.copy(out=proj[:pp, d0:d0 + w], in_=bank[:pp, :w]), [bn], [pn])
                hst = hstb[buf]
                hn = "hst%d" % buf
                nst = 20 if full else 4
                for j in range(nst):
                    c0 = (ko + j * 64) if j < 4 else ((j - 4) * 64)
                    A(lambda e, j=j, c0=c0: e.activation(out=ob[:pp, 0:64], in_=proj[:pp, c0:c0 + 64], func=AF.Square, accum_out=hst[:pp, j:j + 1]),
                      [pn], ["ob", hn])
                A(lambda e: e.activation(out=hst[:pp, 0:nst], in_=hst[:pp, 0:nst], func=AF.Ln, scale=1.0 / 64, bias=epsN[:pp, 0:1]), [hn, "epsN"], [hn])
                A(lambda e: e.activation(out=hst[:pp, 0:nst], in_=hst[:pp, 0:nst], func=AF.Exp, scale=-0.5), [hn], [hn])

            def genA(t, buf):
                pp, xsrc_ap, full, pos_base, use_flag, kb, outs = t["pp"], t["xsrc"], t["full"], t["pos_base"], t["use_flag"], t["kb"], t["outs"]
                x2t = x2tb[buf]
                x2n = "x2t%d" % buf
                qT = qTb[buf]
                qTn = "qT%d" % buf
                ko, kio, pn = t.get("ko", 1024), t.get("kio", 2048), t.get("pn", "proj")
                rope_tables(pp, pos_base, use_flag)
                head_norm(pp, proj[:pp, ko:ko + 256], proj[:pp, ko:ko + 256], 4, kg, [pn], [pn], hstb[buf][:pp, 0:4], "hst%d" % buf)
                rope(pp, proj[:pp, ko:ko + 256].rearrange("p (h d) -> p h d", d=64), 4, [pn], kb)
                rope(pp, proj[:pp, kio:kio + 64].rearrange("p (h d) -> p h d", d=64), 1, [pn], kb)
                A(lambda e: e.copy(out=kvb[:pp, :], in_=proj[:pp, ko:ko + 512]), [pn], ["kvb"])
                A(lambda e: e.copy(out=kib[:pp, :], in_=proj[:pp, kio:kio + 64]), [pn], ["kib"])
                yield 2
                store_keys(pp, kb, kvb[:pp, :], kib[:pp, :], "kvb", "kib")
                if outs is not None:
                    ok_, ov_, oki_ = outs
                    ST(lambda e: e.dma_start(out=ok_, in_=proj[:pp, ko:ko + 256]), [pn], ["o_k"])
                    ST(lambda e: e.dma_start(out=ov_, in_=proj[:pp, ko + 256:ko + 512]), [pn], ["o_v"])
                    ST(lambda e: e.dma_start(out=oki_, in_=proj[:pp, kio:kio + 64]), [pn], ["o_ki"])
                if not full:
                    return
                head_norm(pp, proj[:pp, 0:1024], qn[:pp, :], 16, qg, ["proj"], QNN, hstb[buf][:pp, 4:20], "hst%d" % buf)
                rope(pp, qn[:pp, :].rearrange("p (h d) -> p h d", d=64), 16, QNN, kb)
                rope(pp, proj[:pp, 1536:2048].rearrange("p (h d) -> p h d", d=64), 8, ["proj"], kb)
                A(lambda e: e.copy(out=qb[:pp, :], in_=qn[:pp, :]), QNN, ["qb"])
                A(lambda e: e.copy(out=qib[:pp, :], in_=proj[:pp, 1536:2048]), ["proj"], ["qib"])
                V(lambda e: e.tensor_scalar_mul(out=wis[:pp, :], in0=proj[:pp, 2112:2120], scalar1=(64.0 ** -0.5) * (8.0 ** -0.5)),
                  ["proj"], ["wis"])
                yield 3
                for h8 in range(2):
                    for hh in range(8):
                        h = h8 * 8 + hh
                        T(lambda e, h=h, hh=hh: e.transpose(out=psB[0][0:64, hh * pp:(hh + 1) * pp], in_=qb[:pp, h * 64:(h + 1) * 64],
                                                            identity=identb[:pp, :pp]), ["qb", "identb"], ["psB0"])
                    A(lambda e, h8=h8: e.copy(out=qT[0:64, h8 * 8 * pp:(h8 + 1) * 8 * pp], in_=psB[0][0:64, 0:8 * pp]), ["psB0"], [qTn])
                    yield 1
                for hh in range(8):
                    T(lambda e, hh=hh: e.transpose(out=psB[0][0:64, hh * pp:(hh + 1) * pp], in_=qib[:pp, hh * 64:(hh + 1) * 64],
                                                   identity=identb[:pp, :pp]), ["qib", "identb"], ["psB0"])
                A(lambda e: e.copy(out=qiT[0:64, 0:8 * pp], in_=psB[0][0:64, 0:8 * pp]), ["psB0"], ["qiT"])
                yield 1
                nkb, last_kw = t["nkb"], t["last_kw"]
                NK = (nkb - 1) * 128 + last_kw
                rli = 0
                for bi, c0 in enumerate(range(0, NK, 512)):
                    w = min(512, NK - c0)
                    on_pool = False
                    scn = [SCN[c0 // 512]]
                    for h in range(8):
                        bank, bn = ring()
                        r_ = rl[rli % 4]
                        rn = "rl%d" % (rli % 4)
                        rli += 1
                        T(lambda e, h=h, c0=c0, w=w, bank=bank: e.matmul(bank[:pp, :w], lhsT=qiT[:, h * pp:(h + 1) * pp], rhs=kiT_all[:, c0:c0 + w],
                                                                         start=True, stop=True), ["qiT", "kiT_all"], [bn])
                        A(lambda e, w=w, bank=bank, r_=r_: e.activation(out=r_[:pp, :w], in_=bank[:pp, :w], func=AF.Relu), [bn], [rn])
                        if on_pool:
                            if h == 0:
                                G(lambda e, c0=c0, w=w, r_=r_: e.tensor_scalar_mul(out=scores[:pp, c0:c0 + w], in0=r_[:pp, :w], scalar1=wis[:pp, 0:1]),
                                  [rn, "wis"], scn)
                            else:
                                G(lambda e, h=h, w=w, r_=r_: e.tensor_scalar_mul(out=r_[:pp, :w], in0=r_[:pp, :w], scalar1=wis[:pp, h:h + 1]),
                                  [rn, "wis"], [rn])
                                G(lambda e, c0=c0, w=w, r_=r_: e.tensor_add(out=scores[:pp, c0:c0 + w], in0=scores[:pp, c0:c0 + w], in1=r_[:pp, :w]),
                                  [rn] + scn, scn)
                        elif h == 0:
                            A(lambda e, c0=c0, w=w, r_=r_: e.mul(out=scores[:pp, c0:c0 + w], in_=r_[:pp, :w], mul=wis[:pp, 0:1]),
                              [rn, "wis"], scn)
                        else:
                            V(lambda e, h=h, c0=c0, w=w, r_=r_: e.scalar_tensor_tensor(
                                out=scores[:pp, c0:c0 + w], in0=r_[:pp, :w], scalar=wis[:pp, h:h + 1], in1=scores[:pp, c0:c0 + w],
                                op0=ALU.mult, op1=ALU.add), [rn, "wis"] + scn, scn)
                        if h % 4 == 3:
                            yield 1
                alln = sc_names(0, NK)
                V(lambda e: e.tensor_reduce(out=bis[:pp, 5:6], in_=scores[:pp, :NK], axis=AX.X, op=ALU.min), alln, ["bis"])
                if t["prefix_cols"]:
                    pc = t["prefix_cols"]
                    V(lambda e: e.tensor_scalar_add(out=scores[:pp, 0:pc], in0=scores[:pp, 0:pc], scalar1=flg[:pp, 1:2]),
                      sc_names(0, pc) + ["flg"], sc_names(0, pc))
                if t["diag"]:
                    V(lambda e: e.memset(scores[0:64, NK - 64:NK], -1e30), sc_names(NK - 64, NK), sc_names(NK - 64, NK))
                lo = bis[:pp, 0:1]
                mid = bis[:pp, 1:2]
                cnt = bis[:pp, 2:3]
                ge = bis[:pp, 3:4]
                w0 = bis[:pp, 4:5]
                HW = bis[:pp, 8:38]
                V(lambda e: e.reduce_max(out=w0, in_=scores[:pp, :NK], axis=AX.X), alln, ["bis"])
                V(lambda e: e.tensor_scalar_add(out=lo, in0=bis[:pp, 5:6], scalar1=-1.0), ["bis"], ["bis"])
                V(lambda e: e.tensor_sub(out=w0, in0=w0, in1=lo), ["bis"], ["bis"])
                V(lambda e: e.tensor_scalar_mul(out=HW, in0=pw2[:pp, :], scalar1=w0), ["bis", "pw2"], ["bis"])
                V(lambda e: e.tensor_add(out=mid, in0=lo, in1=bis[:pp, 8:9]), ["bis"], ["bis"])
                for k in range(NBIS):
                    V(lambda e: e.tensor_scalar(out=mb[:pp, :NK], in0=scores[:pp, :NK], scalar1=mid, scalar2=0.0, op0=ALU.is_ge, op1=ALU.add,
                                                accum_out=cnt), alln + ["bis"], ["mb", "bis"])
                    V(lambda e: e.tensor_scalar(out=ge, in0=cnt, scalar1=256.0, scalar2=0.5, op0=ALU.is_ge, op1=ALU.subtract), ["bis"], ["bis"])
                    V(lambda e, k=k: e.scalar_tensor_tensor(out=mid, in0=ge, scalar=bis[:pp, 8 + k:9 + k], in1=mid, op0=ALU.mult, op1=ALU.add),
                      ["bis"], ["bis"])
                V(lambda e: e.scalar_tensor_tensor(out=lo, in0=bis[:pp, 8 + NBIS - 1:9 + NBIS - 1], scalar=-0.5, in1=mid, op0=ALU.mult, op1=ALU.add),
                  ["bis"], ["bis"])
                V(lambda e: e.tensor_scalar(out=mb[:pp, :NK], in0=scores[:pp, :NK], scalar1=lo, scalar2=1.0, op0=ALU.is_ge, op1=ALU.subtract),
                  alln + ["bis"], ["mb"])

            def att_A2b(t):
                pp, nkb, last_kw = t["pp"], t["nkb"], t["last_kw"]
                for k8 in range(0, nkb, 8):
                    nb_ = min(8, nkb - k8)
                    for i in range(nb_):
                        kb = k8 + i
                        kw = last_kw if kb == nkb - 1 else 128
                        T(lambda e, kb=kb, kw=kw, i=i: e.transpose(out=psB[0][:kw, i * pp:(i + 1) * pp], in_=mb[:pp, kb * 128:kb * 128 + kw],
                                                                   identity=identb[:pp, :pp]), ["mb", "identb"], ["psB0"])
                    A(lambda e, k8=k8, nb_=nb_: e.copy(out=mbT[:, k8:k8 + nb_, :pp],
                                                       in_=psB[0][:, 0:nb_ * pp].rearrange("p (k n) -> p k n", n=pp)), ["psB0"], ["mbT"])

            def ohead(h):
                return psF[h // 7], "psF%d" % (h // 7), (h % 7) * 65

            def att_B(t, buf, gen):
                pp, nkb, last_kw, dst_ap = t["pp"], t["nkb"], t["last_kw"], t["dst"]
                x2t = x2tb[buf]
                x2n = "x2t%d" % buf
                qT = qTb[buf]
                qTn = "qT%d" % buf
                countdown = [1]

                def hook():
                    if gen[0] is None:
                        return
                    countdown[0] -= 1
                    if countdown[0] <= 0:
                        try:
                            countdown[0] = next(gen[0])
                        except StopIteration:
                            gen[0] = None
                iters = [(kb, kv) for kb in range(nkb) for kv in range(4)]
                slots = {}

                def emit_qk(it):
                    kb, kv = iters[it]
                    kw = last_kw if kb == nkb - 1 else 128
                    mr = mrep[kb % 2]
                    mrn = "mrep%d" % (kb % 2)
                    if kv == 0:
                        G(lambda e, kb=kb, kw=kw, mr=mr: e.tensor_copy(
                            out=mr[:kw, :4 * pp].rearrange("p (h t) -> p h t", h=4),
                            in_=mbT[:kw, kb, :pp].rearrange("p (o t) -> p o t", o=1).to_broadcast([kw, 4, pp])), ["mbT"], [mrn])
                    bank, bn = ring()
                    et = ET[it % 4]
                    en = "ET%d" % (it % 4)
                    slots[it] = (et, en)
                    T(lambda e, kb=kb, kw=kw, kv=kv, bank=bank: e.matmul(
                        bank[:kw, :4 * pp], lhsT=kT_all[:, kv, kb * 128:kb * 128 + kw], rhs=qT[:, kv * 4 * pp:(kv + 1) * 4 * pp],
                        start=True, stop=False), ["kT_all", qTn], [bn])
                    T(lambda e, kw=kw, bank=bank, mr=mr: e.matmul(bank[:kw, :4 * pp], lhsT=identL[:kw, :kw], rhs=mr[:kw, :4 * pp],
                                                                  start=False, stop=True), ["identL", mrn], [bn])
                    A(lambda e, kw=kw, bank=bank, et=et: e.activation(out=et[:kw, :4 * pp], in_=bank[:kw, :4 * pp], func=AF.Exp,
                                                                      scale=0.125, bias=negC[:kw, 0:1]), [bn, "negC"], [en])

                def emit_pv(it):
                    kb, kv = iters[it]
                    kw = last_kw if kb == nkb - 1 else 128
                    et, en = slots.pop(it)
                    for hh in range(4):
                        ob_, obn, oc = ohead(kv * 4 + hh)
                        T(lambda e, kb=kb, kw=kw, kv=kv, hh=hh, et=et, ob_=ob_, oc=oc: e.matmul(
                            ob_[:pp, oc:oc + 65], lhsT=et[:kw, hh * pp:(hh + 1) * pp], rhs=V1[:kw, kb, kv, :],
                            start=(kb == 0), stop=(kb == nkb - 1)), [en, "V1"], [obn])

                SKEW = 3
                for it in range(min(SKEW, len(iters))):
                    emit_qk(it)
                for it in range(len(iters)):
                    if it + SKEW < len(iters):
                        emit_qk(it + SKEW)
                    emit_pv(it)
                    hook()
                for hb, nh in ((0, 7), (1, 7), (2, 2)):
                    A(lambda e, hb=hb, nh=nh: e.activation(out=rec[:pp, hb * 7:hb * 7 + nh],
                                                           in_=psF[hb][:pp, 0:nh * 65].rearrange("p (h c) -> p h c", c=65)[:, :, 64],
                                                           func=AF.Ln, bias=epsC[:pp, 0:1]), ["psF%d" % hb, "epsC"], ["rec"])
                A(lambda e: e.activation(out=rec[:pp, 0:16], in_=rec[:pp, 0:16], func=AF.Exp, scale=-1.0), ["rec"], ["rec"])
                for h in range(16):
                    ob_, obn, oc = ohead(h)
                    A(lambda e, h=h, ob_=ob_, oc=oc: e.activation(out=ob[:pp, h * 64:(h + 1) * 64], in_=ob_[:pp, oc:oc + 64],
                                                                  func=AF.Copy, scale=rec[:pp, h:h + 1]), [obn, "rec"], ["ob"])
                for dk in range(8):
                    T(lambda e, dk=dk: e.transpose(out=psB[0][:, dk * pp:(dk + 1) * pp], in_=ob[:pp, dk * 128:(dk + 1) * 128],
                                                   identity=identb[:pp, :pp]), ["ob", "identb"], ["psB0"])
                A(lambda e: e.copy(out=oT[:, :, :pp], in_=psB[0][:, 0:8 * pp].rearrange("p (k n) -> p k n", n=pp)), ["psB0"], ["oT"])
                for nb in range(2):
                    bank, bn = ring()
                    for dk in range(8):
                        T(lambda e, dk=dk, nb=nb, bank=bank: e.matmul(bank[:pp, :], lhsT=oT[:, dk, :pp], rhs=Wo[:, dk, nb * 512:(nb + 1) * 512],
                                                                      start=(dk == 0), stop=(dk == 7)), ["oT", "Wo"], [bn])
                    A(lambda e, nb=nb, bank=bank: e.copy(out=x3t[:pp, nb * 512:(nb + 1) * 512], in_=bank[:pp, :]), [bn], ["x3t"])
                    G(lambda e, nb=nb: e.tensor_add(out=x3t[:pp, nb * 512:(nb + 1) * 512], in0=x3t[:pp, nb * 512:(nb + 1) * 512],
                                                    in1=x2t[:pp, nb * 512:(nb + 1) * 512]), ["x3t", x2n], ["x3t"])
                ST(lambda e: e.dma_start(out=dst_ap, in_=x3t[:pp, :]), ["x3t"], ["X3"])

            def drain(g):
                if g[0] is not None:
                    for _ in g[0]:
                        pass
                    g[0] = None

            def run_pipeline(tiles):
                n = len(tiles)
                emitE(tiles[0], 0)
                g = [genA(tiles[0], 0)]
                drain(g)
                att_A2b(tiles[0])
                if n > 1:
                    emitE(tiles[1], 1)
                for i in range(n):
                    g = [genA(tiles[i + 1], (i + 1) % 2)] if i + 1 < n else [None]
                    att_B(tiles[i], i % 2, g)
                    drain(g)
                    if i + 2 < n:
                        emitE(tiles[i + 2], i % 2)
                    if i + 1 < n:
                        att_A2b(tiles[i + 1])

            def ktile(kt):
                alt = (kt % 2 == 1)
                return dict(pp=128, xsrc=X2p[kt * 128:(kt + 1) * 128, :], full=False, pos_base=kt * 128, use_flag=False, kb=kt, outs=None,
                            ko=(0 if alt else 1024), kio=(512 if alt else 2048), pn=("projB" if alt else "proj"))
            V(lambda e: e.memset(sm[:, 62:63], 0.0), (), ["proj", "projB", "sm"])
            emitE(ktile(0), 0)
            for kt in range(15):
                if kt + 1 < 15:
                    emitE(ktile(kt + 1), (kt + 1) % 2)
                g = [genA(ktile(kt), kt % 2)]
                drain(g)
            V(lambda e: e.memset(sm[:, 62:63], 0.0), ["projB"], ["proj", "sm"])
            tiles = []
            for kt in range(15, 32):
                r0 = kt * 128
                if kt == 15:
                    tiles.append(dict(pp=128, xsrc=X2p[r0:r0 + 128, :], full=True, pos_base=r0, use_flag=False, kb=kt, outs=None,
                                      nkb=kt + 1, last_kw=128, prefix_cols=2048, diag=True, dst=X3p[0:128, :]))
                else:
                    o0 = (kt - 16) * 128
                    outs = (o_k[o0:o0 + 128, :], o_v[o0:o0 + 128, :], o_ki[o0:o0 + 128, :])
                    tiles.append(dict(pp=128, xsrc=X2p[r0:r0 + 128, :], full=True, pos_base=o0, use_flag=True, kb=kt, outs=outs,
                                      nkb=kt + 1, last_kw=128, prefix_cols=2048, diag=True, dst=X3p[128 + o0:128 + o0 + 128, :]))
            run_pipeline(tiles)
            V(lambda e: e.memset(sm[:, 61:62], 0.0), (), ["proj", "projS0", "projS1", "projS2", "sm"])
            for kb in range(32):
                r0 = kb * 128
                o = (kb % 3) * 576
                pn = "projS%d" % (kb % 3)
                cb_, cbn = (kvb, "kvb") if kb % 2 == 0 else (qb, "qb")
                ib_, ibn = (kib, "kib") if kb % 2 == 0 else (qib, "qib")
                LD(lambda e, r0=r0, o=o: e.dma_start(out=proj[:, o:o + 256], in_=ck[r0:r0 + 128, :]), (), [pn])
                LD(lambda e, r0=r0, o=o: e.dma_start(out=proj[:, o + 256:o + 512], in_=cv[r0:r0 + 128, :]), (), [pn])
                LD(lambda e, r0=r0, o=o: e.dma_start(out=proj[:, o + 512:o + 576], in_=cki[r0:r0 + 128, :]), (), [pn])
                V(lambda e, o=o, cb_=cb_: e.tensor_copy(out=cb_[:, 0:512], in_=proj[:, o:o + 512]), [pn], [cbn])
                V(lambda e, o=o, ib_=ib_: e.tensor_copy(out=ib_[:, 0:64], in_=proj[:, o + 512:o + 576]), [pn], [ibn])
                store_keys(128, kb, cb_[:, 0:512], ib_[:, 0:64], cbn, ibn)
            V(lambda e: e.memset(sm[:, 60:61], 0.0), ["projS0", "projS1", "projS2"], ["proj", "sm"])
            run_pipeline([dict(pp=32, xsrc=X2s, full=True, pos_base=4096, use_flag=False, kb=32, outs=(o_ks, o_vs, o_kis),
                               nkb=33, last_kw=32, prefix_cols=0, diag=False, dst=X3s)])
        P.barrier(tiny)
        ffn_phase(1, [(X3p[0:128, :], None, None, 128, 128, 0, None, None),
                      (X3p[128:2176, :], None, y_own, 2048, 128, 0, "keep", o_conv_p[1]),
                      (X3s, None, y_s, 32, 32, 0, cconv[1], o_conv_s[1])], False)
        P.barrier(tiny)
        dbgd("X2p", X2p[2048:4096, :], [2048, D], F32, ["Xdst"])
        dbgd("X2s", X2s, [32, D], F32, ["Xdst"])
        P.emit()
    return nc


_NC = None


def _get_nc():
    global _NC
    if _NC is None:
        _NC = build()
    return _NC


def kernel(x_prompt, x_sample, state_ssm_re, state_ssm_im, cache_k, cache_v, cache_kidx, cache_conv,
           norm_mix, norm_ffn, ssm_lambda_re, ssm_lambda_im, ssm_log_dt, ssm_b_re, ssm_b_im,
           ssm_c_re, ssm_c_im, ssm_d, ssm_w_glu, ssm_b_glu, attn_w_in, attn_q_norm, attn_k_norm,
           attn_w_o, ffn_w_up, ffn_conv_w, ffn_conv_b, ffn_w_down):
    f = lambda a: np.ascontiguousarray(np.asarray(a, dtype=np.float32))
    x_prompt = f(x_prompt)
    nc = _get_nc()
    shared = {
        "norm_mix": f(norm_mix), "norm_ffn": f(norm_ffn), "lam_re": f(ssm_lambda_re)[0], "lam_im": f(ssm_lambda_im)[0],
        "log_dt": f(ssm_log_dt), "b_re": f(ssm_b_re)[0], "b_im": f(ssm_b_im)[0], "c_re": f(ssm_c_re)[0], "c_im": f(ssm_c_im)[0],
        "ssm_d": f(ssm_d), "w_glu": f(ssm_w_glu)[0], "b_glu": f(ssm_b_glu), "w_in": f(attn_w_in)[0], "q_norm": f(attn_q_norm),
        "k_norm": f(attn_k_norm), "w_o": f(attn_w_o)[0], "w_up": f(ffn_w_up), "conv_w": f(ffn_conv_w), "conv_b": f(ffn_conv_b),
        "w_down": f(ffn_w_down),
    }
    in_maps = []
    for c in range(8):
        b, h = c // 2, c % 2
        xpc = np.zeros((4096, D), np.float32)
        if h == 1:
            xpc[:2048] = x_prompt[b, :2048]
        xpc[2048:] = x_prompt[b, 2048 * h:2048 * (h + 1)]
        fl = np.zeros((128, 4), np.float32)
        fl[:, 0] = float(h)
        fl[:, 1] = 0.0 if h == 1 else -1e30
        fl[:, 2] = 2048.0 * h
        m = dict(shared)
        m.update({
            "xp": xpc, "xs": f(x_sample)[c], "flag": fl,
            "st_re": f(state_ssm_re)[0, c], "st_im": f(state_ssm_im)[0, c],
            "ck": f(cache_k)[0, c].reshape(4096, 256), "cv": f(cache_v)[0, c].reshape(4096, 256),
            "cki": f(cache_kidx)[0, c], "cconv": f(cache_conv)[:, c],
        })
        in_maps.append(m)
    res = run_bass_kernel_spmd(nc, in_maps, core_ids=list(range(8))).results
    global _LAST
    _LAST = res
    y_p = np.zeros((4, 4096, D), np.float32)
    k_p = np.zeros((1, 4, 4096, 4, 64), np.float32)
    v_p = np.zeros((1, 4, 4096, 4, 64), np.float32)
    ki_p = np.zeros((1, 4, 4096, 64), np.float32)
    for c in range(8):
        b, h = c // 2, c % 2
        sl = slice(2048 * h, 2048 * (h + 1))
        y_p[b, sl] = res[c]["y_own"]
        k_p[0, b, sl] = res[c]["o_k"].reshape(2048, 4, 64)
        v_p[0, b, sl] = res[c]["o_v"].reshape(2048, 4, 64)
        ki_p[0, b, sl] = res[c]["o_ki"]
    y_s = np.stack([res[c]["y_s"] for c in range(8)])
    ssm_re_p = np.stack([res[2 * b + 1]["o_ssm_p"][0] for b in range(4)])[None]
    ssm_im_p = np.stack([res[2 * b + 1]["o_ssm_p"][1] for b in range(4)])[None]
    ssm_re_s = np.stack([res[c]["o_ssm_s"][0] for c in range(8)])[None]
    ssm_im_s = np.stack([res[c]["o_ssm_s"][1] for c in range(8)])[None]
    k_s = np.stack([res[c]["o_ks"].reshape(32, 4, 64) for c in range(8)])[None]
    v_s = np.stack([res[c]["o_vs"].reshape(32, 4, 64) for c in range(8)])[None]
    ki_s = np.stack([res[c]["o_kis"] for c in range(8)])[None]
    conv_p = np.stack([res[2 * b + 1]["o_conv_p"] for b in range(4)], axis=1)
    conv_s = np.stack([res[c]["o_conv_s"] for c in range(8)], axis=1)
    return (y_p, y_s, ssm_re_p, ssm_im_p, ssm_re_s, ssm_im_s, k_p, v_p, ki_p, k_s, v_s, ki_s, conv_p, conv_s)
```
